# Optimizing a Trainium2 kernel written in Bass

```python
import math
import jax
import jax.numpy as jnp
from jax import lax
import numpy as np

D_MODEL = 1024
BATCH = 8
SEQ = 2048
DEPTH = 1
DEC_BATCH = 32
DEC_SEQ = 64
PAST_LEN = 4096

CHUNK = 64
D_LRU = 1024
LRU_HEADS = 16
LRU_HEAD_DIM = D_LRU // LRU_HEADS
LRU_CONV = 4
RG_C = 8.0
D_CONV = 1024
CCM_KERNEL = 31
D_FF = ((8 * D_MODEL // 3 + 255) // 256) * 256
D_IN = 2 * D_LRU + 2 * D_CONV + 2 * D_MODEL
EPS = 1e-6

kernel_name = "hawk_conformer_parallel_stream_step"


def rms_norm(x, g):
    xf = x.astype(jnp.float32)
    y = xf * lax.rsqrt(jnp.mean(xf * xf, axis=-1, keepdims=True) + EPS)
    return (y * g.astype(jnp.float32)).astype(x.dtype)


def layer_norm(x, g, b):
    xf = x.astype(jnp.float32)
    mu = jnp.mean(xf, axis=-1, keepdims=True)
    var = jnp.mean(jnp.square(xf - mu), axis=-1, keepdims=True)
    y = (xf - mu) * lax.rsqrt(var + EPS)
    return (y * g.astype(jnp.float32) + b.astype(jnp.float32)).astype(x.dtype)


def causal_dwconv(x_pad, w, b):
    c = x_pad.shape[-1]
    out = lax.conv_general_dilated(
        x_pad, w[:, None, :].astype(x_pad.dtype), window_strides=(1,), padding="VALID",
        dimension_numbers=("NWC", "WIO", "NWC"), feature_group_count=c)
    return out + b


def rg_lru(xc, h0, w_r, b_r, w_i, b_i, lam):
    bsz, t, _ = xc.shape
    xh = xc.reshape(bsz, t, LRU_HEADS, LRU_HEAD_DIM)
    r = jax.nn.sigmoid((jnp.einsum("bthd,hde->bthe", xh, w_r).reshape(bsz, t, D_LRU) + b_r).astype(jnp.float32))
    i = jax.nn.sigmoid((jnp.einsum("bthd,hde->bthe", xh, w_i).reshape(bsz, t, D_LRU) + b_i).astype(jnp.float32))
    log_a = -RG_C * r * jax.nn.softplus(-lam.astype(jnp.float32))
    a = jnp.exp(log_a)
    u = jnp.sqrt(-jnp.expm1(2.0 * log_a)) * (i * xc.astype(jnp.float32))
    u = u.at[:, 0].add(a[:, 0] * h0.astype(jnp.float32))

    def combine(c1, c2):
        a1, b1 = c1
        a2, b2 = c2
        return a1 * a2, a2 * b1 + b2

    _, hs = lax.associative_scan(combine, (a, u), axis=1)
    return hs, hs[:, -1]


def encoder_layer(x, h0, lru_buf, ccm_buf, g_mix, w_in, b_in, w_lru_conv, b_lru_conv,
                  w_rg_r, b_rg_r, w_rg_i, b_rg_i, lru_lambda, w_lru_o, w_ccm_dw, b_ccm_dw,
                  g_ccm_ln, b_ccm_ln, w_ccm_o, w_out, g_ffn, w_ffn_gate, w_ffn_up, w_ffn_down):
    h = rms_norm(x, g_mix)
    proj = jnp.einsum("btd,de->bte", h, w_in) + b_in
    splits = [D_LRU, 2 * D_LRU, 2 * D_LRU + D_CONV, 2 * D_LRU + 2 * D_CONV,
              2 * D_LRU + 2 * D_CONV + D_MODEL]
    xl, gl, ca, cb, s_lru, s_ccm = jnp.split(proj, splits, axis=-1)

    xl_pad = jnp.concatenate([lru_buf.astype(xl.dtype), xl], axis=1)
    new_lru_buf = xl_pad[:, -(LRU_CONV - 1):]
    xc = causal_dwconv(xl_pad, w_lru_conv, b_lru_conv)
    hs, h_last = rg_lru(xc, h0, w_rg_r, b_rg_r, w_rg_i, b_rg_i, lru_lambda)
    lru_out = jnp.einsum("bte,ed->btd", jax.nn.gelu(gl) * hs.astype(x.dtype), w_lru_o)

    u = ca * jax.nn.sigmoid(cb)
    u_pad = jnp.concatenate([ccm_buf.astype(u.dtype), u], axis=1)
    new_ccm_buf = u_pad[:, -(CCM_KERNEL - 1):]
    d = causal_dwconv(u_pad, w_ccm_dw, b_ccm_dw)
    d = jax.nn.silu(layer_norm(d, g_ccm_ln, b_ccm_ln))
    ccm_out = jnp.einsum("btc,cd->btd", d, w_ccm_o)

    merged = jax.nn.sigmoid(s_lru) * lru_out + jax.nn.sigmoid(s_ccm) * ccm_out
    x = x + jnp.einsum("btd,de->bte", merged, w_out)

    h2 = rms_norm(x, g_ffn)
    ff = jax.nn.silu(jnp.einsum("btd,df->btf", h2, w_ffn_gate)) * jnp.einsum("btd,df->btf", h2, w_ffn_up)
    x = x + jnp.einsum("btf,fd->btd", ff, w_ffn_down)
    return x, h_last.astype(x.dtype), new_lru_buf, new_ccm_buf


def run_trunk(x, h0s, lru_bufs, ccm_bufs, layer_params, g_final):
    hs, lbs, cbs = [], [], []
    for l in range(DEPTH):
        p = [w[l] for w in layer_params]
        x, h_last, lb, cb = encoder_layer(x, h0s[l], lru_bufs[l], ccm_bufs[l], *p)
        hs.append(h_last)
        lbs.append(lb)
        cbs.append(cb)
    return rms_norm(x, g_final), jnp.stack(hs), jnp.stack(lbs), jnp.stack(cbs)


def setup_inputs(seed: int = 0) -> dict:
    key = jax.random.key(seed)
    ks = jax.random.split(key, 32)
    f32 = jnp.float32
    nrm = lambda k, shape, s: jax.random.normal(k, shape, f32) * s
    u = jax.random.uniform(ks[12], (DEPTH, D_LRU), f32, 0.9, 0.999)
    sa = u ** (1.0 / RG_C)
    lam = jnp.log(sa) - jnp.log1p(-sa)
    return {
        "x_prompt": nrm(ks[0], (BATCH, SEQ, D_MODEL), 1.0),
        "x_sample": nrm(ks[1], (DEC_BATCH, DEC_SEQ, D_MODEL), 1.0),
        "state_lru_h": nrm(ks[2], (DEPTH, DEC_BATCH, D_LRU), 0.5),
        "cache_lru_conv": nrm(ks[3], (DEPTH, DEC_BATCH, LRU_CONV - 1, D_LRU), 1.0),
        "cache_ccm_conv": nrm(ks[4], (DEPTH, DEC_BATCH, CCM_KERNEL - 1, D_CONV), 0.5),
        "g_mix": 1.0 + nrm(ks[5], (DEPTH, D_MODEL), 0.02),
        "w_in": nrm(ks[6], (DEPTH, D_MODEL, D_IN), D_MODEL ** -0.5),
        "b_in": nrm(ks[7], (DEPTH, D_IN), 0.02),
        "w_lru_conv": nrm(ks[8], (DEPTH, LRU_CONV, D_LRU), LRU_CONV ** -0.5),
        "b_lru_conv": nrm(ks[9], (DEPTH, D_LRU), 0.02),
        "w_rg_r": nrm(ks[10], (DEPTH, LRU_HEADS, LRU_HEAD_DIM, LRU_HEAD_DIM), LRU_HEAD_DIM ** -0.5),
        "b_rg_r": nrm(ks[11], (DEPTH, D_LRU), 0.02),
        "w_rg_i": nrm(ks[13], (DEPTH, LRU_HEADS, LRU_HEAD_DIM, LRU_HEAD_DIM), LRU_HEAD_DIM ** -0.5),
        "b_rg_i": nrm(ks[14], (DEPTH, D_LRU), 0.02),
        "lru_lambda": lam,
        "w_lru_o": nrm(ks[15], (DEPTH, D_LRU, D_MODEL), D_LRU ** -0.5),
        "w_ccm_dw": nrm(ks[16], (DEPTH, CCM_KERNEL, D_CONV), CCM_KERNEL ** -0.5),
        "b_ccm_dw": nrm(ks[17], (DEPTH, D_CONV), 0.02),
        "g_ccm_ln": 1.0 + nrm(ks[18], (DEPTH, D_CONV), 0.02),
        "b_ccm_ln": nrm(ks[19], (DEPTH, D_CONV), 0.02),
        "w_ccm_o": nrm(ks[20], (DEPTH, D_CONV, D_MODEL), D_CONV ** -0.5),
        "w_out": nrm(ks[21], (DEPTH, D_MODEL, D_MODEL), D_MODEL ** -0.5),
        "g_ffn": 1.0 + nrm(ks[22], (DEPTH, D_MODEL), 0.02),
        "w_ffn_gate": nrm(ks[23], (DEPTH, D_MODEL, D_FF), D_MODEL ** -0.5),
        "w_ffn_up": nrm(ks[24], (DEPTH, D_MODEL, D_FF), D_MODEL ** -0.5),
        "w_ffn_down": nrm(ks[25], (DEPTH, D_FF, D_MODEL), D_FF ** -0.5),
        "g_final": 1.0 + nrm(ks[26], (D_MODEL,), 0.02),
    }


def reference(x_prompt, x_sample, state_lru_h, cache_lru_conv, cache_ccm_conv,
              g_mix, w_in, b_in, w_lru_conv, b_lru_conv, w_rg_r, b_rg_r, w_rg_i, b_rg_i,
              lru_lambda, w_lru_o, w_ccm_dw, b_ccm_dw, g_ccm_ln, b_ccm_ln, w_ccm_o, w_out,
              g_ffn, w_ffn_gate, w_ffn_up, w_ffn_down, g_final):
    layer_params = (g_mix, w_in, b_in, w_lru_conv, b_lru_conv, w_rg_r, b_rg_r, w_rg_i, b_rg_i,
                    lru_lambda, w_lru_o, w_ccm_dw, b_ccm_dw, g_ccm_ln, b_ccm_ln, w_ccm_o, w_out,
                    g_ffn, w_ffn_gate, w_ffn_up, w_ffn_down)
    bp = x_prompt.shape[0]
    dt = x_prompt.dtype
    h0_p = jnp.zeros((DEPTH, bp, D_LRU), dt)
    lb_p = jnp.zeros((DEPTH, bp, LRU_CONV - 1, D_LRU), dt)
    cb_p = jnp.zeros((DEPTH, bp, CCM_KERNEL - 1, D_CONV), dt)
    y_prompt, p_h, p_lb, p_cb = run_trunk(x_prompt, h0_p, lb_p, cb_p, layer_params, g_final)
    y_sample, s_h, s_lb, s_cb = run_trunk(x_sample, state_lru_h, cache_lru_conv, cache_ccm_conv,
                                          layer_params, g_final)
    return (y_prompt, y_sample, p_h, p_lb, p_cb, s_h, s_lb, s_cb)
```

```python
import contextlib
from collections import defaultdict

import numpy as np
import concourse.bass as bass
import concourse.mybir as mybir
from concourse.bass_utils import run_bass_kernel_spmd

F32 = mybir.dt.float32
BF16 = mybir.dt.bfloat16
U8 = mybir.dt.uint8
AF = mybir.ActivationFunctionType
ALU = mybir.AluOpType

T = 2304
TP = 2048
NS = 4
TS = 64
NT = 18
D = 1024
KT = 8
DFF = 2816
FT = 22
EPS = 1e-6
CH = [(0, 512), (512, 512), (1024, 512), (1536, 512), (2048, 256)]
HALVES = [([0, 1], list(range(0, 8)), 0, 1024), ([2, 3, 4], list(range(8, 18)), 1024, 1280)]

PP_BIN = 0
PP_WLC = 48
PP_BLC = 80
PP_BR = 88
PP_BI = 96
PP_LAM = 104
PP_WDW = 112
PP_BDW = 360
PP_GLN = 368
PP_BLN = 376
NPP = 384

ENGS = ("pe", "act", "dve", "pool", "sp")


class Buf:
    __slots__ = ("writer", "readers", "dreaders", "excl")

    def __init__(self):
        self.writer = None
        self.readers = {}
        self.dreaders = []
        self.excl = False


class Op:
    __slots__ = ("eng", "fn", "idx", "is_dma", "waits", "signal", "sig_count", "dma_sem", "dma_val",
                 "pre_dma_wait")

    def __init__(self, eng, fn, is_dma):
        self.eng = eng
        self.fn = fn
        self.is_dma = is_dma
        self.waits = []
        self.signal = False
        self.sig_count = None
        self.dma_sem = None
        self.dma_val = None
        self.pre_dma_wait = None


class Sched:
    def __init__(self, nc, n_dma_sems=6):
        self.nc = nc
        self.ops = {e: [] for e in ENGS}
        self.n_dma_sems = n_dma_sems
        self.dma_rr = {e: 0 for e in ENGS}
        self.dma_state = {}
        self.same_eng_gap = 8
        self.bufs = {}
        self.name2reg = {}
        self.region_bufs = defaultdict(list)
        self.region_ghost = {}

    def b(self, *key):
        buf = self.bufs.get(key)
        if buf is not None:
            return buf
        buf = Buf()
        self.bufs[key] = buf
        for reg in (self.name2reg.get(key[:2]) or self.name2reg.get(key[0], ())):
            self.region_bufs[reg].append((key, buf))
            g = self.region_ghost.get(reg)
            if g:
                for e, op in g[0].items():
                    if e not in buf.readers or buf.readers[e].idx < op.idx:
                        buf.readers[e] = op
                buf.dreaders.extend(g[1])
        return buf

    def retire(self, reg):
        rd, dr = {}, []
        g = self.region_ghost.get(reg)
        if g:
            rd.update(g[0])
            dr.extend(g[1])
        for key, buf in self.region_bufs[reg]:
            ops = list(buf.readers.values()) + list(buf.dreaders)
            if buf.writer is not None:
                ops.append(buf.writer)
            for op in ops:
                if op.is_dma:
                    if op not in dr:
                        dr.append(op)
                elif op.eng not in rd or rd[op.eng].idx < op.idx:
                    rd[op.eng] = op
            self.bufs.pop(key, None)
        self.region_bufs[reg] = []
        self.region_ghost[reg] = (rd, dr[-12:])

    def _add(self, eng, fn, reads, writes, is_dma):
        op = Op(eng, fn, is_dma)
        lst = self.ops[eng]
        op.idx = len(lst)
        deps = []
        for b in reads:
            if b.writer is not None:
                deps.append(b.writer)
            if b.excl:
                for e2, r in b.readers.items():
                    if e2 != eng:
                        deps.append(r)
        for b in writes:
            if b.writer is not None:
                deps.append(b.writer)
            deps.extend(b.readers.values())
            deps.extend(b.dreaders)
        seen = set()
        for d in deps:
            if d is op or id(d) in seen:
                continue
            seen.add(id(d))
            if (not d.is_dma) and d.eng == eng:
                if eng == "pe":
                    continue
                if op.idx - d.idx >= self.same_eng_gap:
                    continue
            op.waits.append(d)
            d.signal = True
        for b in reads:
            if is_dma:
                b.dreaders.append(op)
            else:
                b.readers[eng] = op
        for b in writes:
            b.writer = op
            b.readers = {}
            b.dreaders = []
        if is_dma:
            slot = self.dma_rr[eng]
            self.dma_rr[eng] = (slot + 1) % self.n_dma_sems
            key = (eng, slot)
            val, prev = self.dma_state.get(key, (0, None))
            op.pre_dma_wait = prev
            val += 16
            op.dma_sem = key
            op.dma_val = val
            self.dma_state[key] = (val, op)
        lst.append(op)
        return op

    def op(self, eng, fn, reads=(), writes=()):
        return self._add(eng, fn, list(reads), list(writes), False)

    def dma(self, eng, fn, reads=(), writes=()):
        return self._add(eng, fn, list(reads), list(writes), True)

    def barrier(self):
        lasts = []
        for e in ENGS:
            for op in reversed(self.ops[e]):
                if (not op.is_dma) and op.fn is not None:
                    op.signal = True
                    lasts.append(op)
                    break
        dl = [op for (_, op) in self.dma_state.values()]
        for e in ENGS:
            op = Op(e, None, False)
            op.idx = len(self.ops[e])
            op.waits = list(lasts) + list(dl)
            self.ops[e].append(op)

    def emit(self, final_wait_ops=()):
        nc = self.nc
        for op in final_wait_ops:
            op.signal = True
        with contextlib.ExitStack() as st:
            esem = {e: st.enter_context(nc.semaphore("s_" + e)) for e in ENGS}
            dsem = {}
            for key in sorted(self.dma_state.keys()):
                dsem[key] = st.enter_context(nc.semaphore("d_%s%d" % key))
            for e in ENGS:
                c = 0
                for op in self.ops[e]:
                    if op.is_dma or op.fn is None:
                        continue
                    if op.signal:
                        c += 1
                        op.sig_count = c
            block = st.enter_context(nc.Block())
            engobj = {"pe": "tensor", "act": "scalar", "dve": "vector", "pool": "gpsimd", "sp": "sync"}

            def make(e):
                def body(eng):
                    seen = {}

                    def wait(semkey, sem, val):
                        if seen.get(semkey, 0) >= val:
                            return
                        seen[semkey] = val
                        eng.wait_ge(sem, val)

                    def wait_op(d):
                        if d.is_dma:
                            wait(d.dma_sem, dsem[d.dma_sem], d.dma_val)
                        else:
                            wait(d.eng, esem[d.eng], d.sig_count)

                    for op in self.ops[e]:
                        for d in op.waits:
                            wait_op(d)
                        if op.fn is None:
                            continue
                        if op.is_dma and op.pre_dma_wait is not None:
                            wait_op(op.pre_dma_wait)
                        ins = op.fn(eng)
                        if op.is_dma:
                            ins.then_inc(dsem[op.dma_sem], 16)
                        elif op.signal:
                            ins.then_inc(esem[e], 1)
                    if e == "sp":
                        for op in final_wait_ops:
                            wait_op(op)
                return body

            for e in ENGS:
                getattr(block, engobj[e])(make(e))


def build_nc():
    nc = bass.Bass("TRN2", target_bir_lowering=False)

    def din(name, shape):
        return nc.dram_tensor(name, list(shape), F32, kind="ExternalInput").ap()

    x_d = din("x", [T, D])
    h0_d = din("h0", [128, KT * NS])
    lc_d = din("lc", [128, KT * NS * 3])
    cc_d = din("cc", [128, KT * NS * 30])
    pp_d = din("pp", [128, NPP])
    gmix_d = din("gmixB", [128, D])
    gffn_d = din("gffnB", [128, D])
    gfin_d = din("gfinB", [128, D])
    w_in_d = din("w_in", [D, 6 * D])
    w_rgr_d = din("w_rg_r", [16, 64, 64])
    w_rgi_d = din("w_rg_i", [16, 64, 64])
    w_lruo_d = din("w_lru_o", [D, D])
    w_ccmo_d = din("w_ccm_o", [D, D])
    w_out_d = din("w_out", [D, D])
    w_gate_d = din("w_ffn_gate", [D, DFF])
    w_up_d = din("w_ffn_up", [D, DFF])
    w_down_d = din("w_ffn_down", [DFF, D])
    y_d = nc.dram_tensor("y", [T, D], F32, kind="ExternalOutput").ap()
    st_d = nc.dram_tensor("st", [128, KT * 5 * 34], F32, kind="ExternalOutput").ap()
    x2s_d = nc.dram_tensor("x2s", [T, D], F32).ap()

    w_in_v = w_in_d.rearrange("(kt p) (g c) -> p kt g c", p=128, c=128)
    w_lruo_v = w_lruo_d.rearrange("(kt p) (g c) -> p kt g c", p=128, c=128)
    w_ccmo_v = w_ccmo_d.rearrange("(kt p) (g c) -> p kt g c", p=128, c=128)
    w_out_v = w_out_d.rearrange("(kt p) c -> p kt c", p=128)
    w_gate_v = w_gate_d.rearrange("(kt p) (g c) -> p kt g c", p=128, c=128)
    w_up_v = w_up_d.rearrange("(kt p) (g c) -> p kt g c", p=128, c=128)
    w_down_v = w_down_d.rearrange("(ft p) c -> p ft c", p=128)

    with contextlib.ExitStack() as st:
        ARENA = 207616
        arena = st.enter_context(nc.sbuf_tensor("arena", [128, ARENA + 4608 + 512], U8))
        import os as _os
        PB = [st.enter_context(nc.psum_tensor("pb%d" % i, [128, 512], F32)) for i in range(8)]

        def V(off, dt, *shape):
            sz = 2 if dt == BF16 else 4
            n = 1
            for s in shape:
                n *= s
            v = arena[:, off:off + n * sz].bitcast(dt)
            if len(shape) == 2:
                return v.rearrange("p (a b) -> p a b", a=shape[0])
            if len(shape) == 3:
                return v.rearrange("p (a b c) -> p a b c", a=shape[0], b=shape[1])
            return v

        R1, DN, GL, Z = 0, 36864, 73728, 110592
        WS0, WS1, S = 157696, 165888, 174080
        P0 = 190464
        GBo = P0
        STo = GBo + 4096
        PPo = STo + 5440
        IDBo = PPo + 1536
        IDFo = IDBo + 256
        ONFo = IDFo + 512
        CSTo = ONFo + 512
        SSo = CSTo + 64
        CLo = SSo + 256
        H0o = CLo + 128
        LCo = H0o + 128
        CCo = LCo + 384
        assert CCo + 3840 <= ARENA

        hT = V(R1, BF16, KT, T)
        dn = V(DN, BF16, KT, T)
        glru = V(GL, BF16, KT, T)
        GB = V(GBo, F32, KT, 128)
        GBflat = V(GBo, F32, D)
        stage = V(STo, F32, KT, 5, 34)
        stage_flat = V(STo, F32, KT * 5 * 34)
        pp = V(PPo, F32, NPP)
        identb = V(IDBo, BF16, 128)
        identf = V(IDFo, F32, 128)
        onesf = V(ONFo, F32, 128)
        cst = V(CSTo, F32, 16)
        ss = V(SSo, F32, 3, 20)
        cl = V(CLo, F32, 4, 8)
        h0 = V(H0o, F32, KT, NS)
        lc = V(LCo, F32, KT, NS, 3)
        cc = V(CCo, F32, KT, NS, 30)
        WS = [V(WS0, BF16, KT, 4, 128), V(WS1, BF16, KT, 4, 128)]
        eps_ap = cst[:, 0:1]
        one_ap = cst[:, 1:2]

        def col(base, j):
            return pp[:, base + j:base + j + 1]

        S_ = Sched(nc)
        S_.name2reg = {
            "xaA": ["DN"], "jkA": ["DN"], "xsA": ["DN"], "hT": ["R1"], "h2T": ["R1"],
            "ws": ["WS"],
            "d31": ["GLa", "GLb"], "S1": ["GLb"], "S2": ["GLb"],
            "ub": ["S"], "ubpad": ["S"], "ubs": ["S"], "sig": ["S"], "dq": ["S"],
            "dz": ["DN"], "acc": ["Z"],
            "mu": ["Zs"], "rs": ["Zs"], "t1": ["Zs", "GLb"], "t2": ["Zs"], "dn": ["DN"],
            "d4": ["S"], "wg": ["S"], "xlb": ["S"], "xlbpad": ["S"], "xlbs": ["S"], "xcb": ["S"],
            ("X", 0): ["Z"], ("X", 1): ["Zx1"], ("mu", 1): ["Zx1"], ("rs", 1): ["Zx1"], "s1b": ["Zx1"], "R": ["Z"], "I": ["Z"], "A": ["Zs"], "hs": ["Z"], "glru": ["GLa", "GLb"],
            "sgl": ["Zs"], "sgc": ["Zs"], "m1": ["Zs"], "m2": ["Zs"], "mg": ["Z", "Zx1"],
            "wo": ["S"], "x2D": ["GLb"], "xaD": ["GLb"], "xsD": ["GLb"], "jkD": ["Zs"],
            "wd": ["DN", "GLa"], "ff": ["GLb", "Z", "Zx1"], "x2t": ["Z", "Zs"], "yt": ["Zs"],
            "sgf": ["S"], "jkF": ["S"], "sd": ["S"],
        }
        b = S_.b
        pbank = [0]
        for _i in range(8):
            b("pb", _i).excl = True

        def bank():
            i = pbank[0]
            pbank[0] = (i + 1) % 8
            return PB[i], b("pb", i)

        wslot = [0]

        def wsbuf():
            i = wslot[0]
            wslot[0] = (i + 1) % 2
            return WS[i], i

        finals = []

        S_.dma("sp", lambda e: e.dma_start(out=pp, in_=pp_d), writes=[b("pp")])
        NXA = 6
        XA = [V(DN + k * 4096, F32, D) for k in range(NXA)]

        def load_xA(i):
            xa = XA[i % NXA]
            S_.dma("sp", lambda e: e.dma_start(out=xa, in_=x_d[i * 128:(i + 1) * 128, :]), writes=[b("xaA", i % NXA)])

        for i in range(NXA):
            load_xA(i)
        S_.dma("sp", lambda e: e.dma_start(out=GBflat, in_=gmix_d), writes=[b("GB")])
        S_.dma("sp", lambda e: e.dma_start(out=V(H0o, F32, KT * NS), in_=h0_d), writes=[b("h0")])
        S_.dma("sp", lambda e: e.dma_start(out=V(LCo, F32, KT * NS * 3), in_=lc_d), writes=[b("lc")])
        S_.dma("sp", lambda e: e.dma_start(out=V(CCo, F32, KT * NS * 30), in_=cc_d), writes=[b("cc")])
        S_.op("pool", lambda e: e.memset(identf, 0.0), writes=[b("identf")])
        S_.op("pool", lambda e: e.affine_select(out=identf, in_=identf, pattern=[[-1, 128]],
                                                compare_op=ALU.not_equal, fill=1.0, base=0,
                                                channel_multiplier=1),
              reads=[b("identf")], writes=[b("identf")])
        S_.op("pool", lambda e: e.memset(onesf, 1.0 / D), writes=[b("onesf")])
        S_.op("pool", lambda e: e.memset(cst[:, 0:1], EPS), writes=[b("cst")])
        S_.op("pool", lambda e: e.memset(cst[:, 1:2], 1.0), writes=[b("cst")])
        S_.op("pool", lambda e: e.memset(V(SSo, F32, 64), 0.0), writes=[b("ss")])
        S_.op("dve", lambda e: e.tensor_copy(out=identb, in_=identf), reads=[b("identf")], writes=[b("identb")])
        S_.op("act", lambda e: e.activation(out=cl[:, 2, :], in_=pp[:, PP_LAM:PP_LAM + 8], func=AF.Exp, scale=-1.0),
              reads=[b("pp")], writes=[b("cl")])
        S_.op("act", lambda e: e.activation(out=cl[:, 3, :], in_=cl[:, 2, :], func=AF.Ln, bias=one_ap),
              reads=[b("cl"), b("cst")], writes=[b("cl")])
        S_.op("dve", lambda e: e.tensor_scalar(out=cl[:, 0, :], in0=cl[:, 3, :], scalar1=-8.0, scalar2=None, op0=ALU.mult),
              reads=[b("cl")], writes=[b("cl")])
        S_.op("dve", lambda e: e.tensor_scalar(out=cl[:, 1, :], in0=cl[:, 3, :], scalar1=-16.0, scalar2=None, op0=ALU.mult),
              reads=[b("cl")], writes=[b("cl")])

        def wload(ws, wsb, g, src):
            S_.dma("pool", lambda e: e.dma_start(out=ws[:, :, g, :], in_=src), writes=[b("ws", wsb, g)])

        def mm8(pb, pbb, n, ws, wsb, g, rhs3, rbufs, c0):
            for kt in range(KT):
                S_.op("pe", lambda e, kt=kt: e.matmul(pb[:, 0:n], lhsT=ws[:, kt, g, :], rhs=rhs3[:, kt, c0:c0 + n],
                                                      start=(kt == 0), stop=(kt == KT - 1)),
                      reads=[b("ws", wsb, g)] + rbufs, writes=[pbb])

        def hT_bufs(name, c0, n):
            return [b(name, i) for i in range(c0 // 128, (c0 + n) // 128)]

        ws, wsb = wsbuf()
        wload(ws, wsb, 0, w_in_v[:, :, 16, :])
        wload(ws, wsb, 1, w_in_v[:, :, 24, :])

        def norm_stage1(tiles, row, src_fn, jk, jkname):
            for i in tiles:
                src_ap, src_bufs = src_fn(i)
                S_.op("act", lambda e, src_ap=src_ap, i=i: e.activation(out=jk, in_=src_ap, func=AF.Square,
                                                                       accum_out=ss[:, row, i:i + 1]),
                      reads=list(src_bufs) + [b("ss")], writes=[b(jkname), b("ssc", row, i)])
            lo, hi = tiles[0], tiles[-1] + 1
            S_.op("act", lambda e: e.activation(out=ss[:, row, lo:hi], in_=ss[:, row, lo:hi], func=AF.Sqrt,
                                                bias=eps_ap, scale=1.0 / D),
                  reads=[b("ssc", row, i) for i in tiles] + [b("cst")], writes=[b("rstd", row, i) for i in tiles])
            S_.op("dve", lambda e: e.reciprocal(out=ss[:, row, lo:hi], in_=ss[:, row, lo:hi]),
                  reads=[b("rstd", row, i) for i in tiles], writes=[b("rstd", row, i) for i in tiles])

        def norm_scale(i, row, src_ap, src_bufs, xs, xs_buf, eng="act"):
            if eng == "act":
                S_.op("act", lambda e: e.activation(out=xs, in_=src_ap, func=AF.Identity, scale=ss[:, row, i:i + 1]),
                      reads=list(src_bufs) + [b("rstd", row, i)], writes=[xs_buf])
            else:
                S_.op("dve", lambda e: e.tensor_scalar(out=xs, in0=src_ap, scalar1=ss[:, row, i:i + 1], scalar2=None,
                                                       op0=ALU.mult),
                      reads=list(src_bufs) + [b("rstd", row, i)], writes=[xs_buf])

        def norm_transpose(i, xs, xs_buf):
            pb, pbb = bank()
            pv = pb[:].bitcast(BF16)
            for kt in range(KT):
                S_.op("pe", lambda e, kt=kt: e.transpose(pv[:, kt * 128:(kt + 1) * 128], xs[:, kt * 128:(kt + 1) * 128], identb),
                      reads=[xs_buf, b("identb")], writes=[pbb])
            return pv.rearrange("p (a n) -> p a n", a=KT), pbb

        def norm_evac(i, pv3, pbb, dstT, dst_name):
            S_.op("dve", lambda e: e.tensor_tensor(out=dstT[:, :, i * 128:(i + 1) * 128], in0=pv3, in1=GB, op=ALU.mult),
                  reads=[pbb, b("GB")], writes=[b(dst_name, i)])

        JK_A = V(DN + 24576, BF16, D)
        XSA = [V(DN + 26624 + k * 2048, BF16, D) for k in range(3)]

        def phase_A_chunk(ci):
            c0, n = CH[ci]
            tiles = list(range(c0 // 128, (c0 + n) // 128))
            for i in tiles:
                norm_stage1([i], 0, lambda i_: (XA[i_ % NXA], [b("xaA", i_ % NXA)]), JK_A, "jkA")
            pend = None
            for i in tiles:
                xs, xsb = XSA[i % 3], b("xsA", i % 3)
                norm_scale(i, 0, XA[i % NXA], [b("xaA", i % NXA)], xs, xsb, eng="dve")
                if i + NXA < NT:
                    load_xA(i + NXA)
                if pend is not None:
                    norm_evac(*pend)
                pv3, pbb = norm_transpose(i, xs, xsb)
                pend = (i, pv3, pbb, hT, "hT")
            norm_evac(*pend)

        D31 = [V(GL + k * 7936, BF16, 31, 128) for k in range(2)]

        _ND = int(_os.environ.get("K_ND", "4"))

        def npe_of(j):
            return 31 if j == KT - 1 else 31 - _ND

        def build_d31(j):
            d31_ = D31[j % 2]
            for k in range(npe_of(j)):
                S_.op("act", lambda e, k=k, d31_=d31_, j=j: e.activation(out=d31_[:, k, :], in_=identf, func=AF.Identity,
                                                                        scale=col(PP_WDW, j * 31 + k)),
                      reads=[b("identf"), b("pp")], writes=[b("d31", j % 2, k)])

        build_d31(0)
        phase_A_chunk(0)

        dz = V(DN, BF16, KT, T)
        S1 = V(GL + 15872, F32, T)
        S2 = V(GL + 25088, F32, T)
        ub = V(S, BF16, 2454)
        ub_s = ub[:, 2078:2454].rearrange("p (s n) -> p s n", s=NS)
        SIG = [V(S + 4912 + k * 2048, F32, 512) for k in range(2)]
        DQ = [V(S + 9008 + k * 2048, F32, 512) for k in range(2)]
        MU = [V(Z + 36864, F32, 512)]
        RS = [V(Z + 38912, F32, 512)]
        T1 = [V(Z + 40960, F32, 512), V(GL + 34304, F32, 512)]
        T2 = [V(Z + 43008 + k * 2048, F32, 512) for k in range(2)]

        MU = [MU[0], V(Z + 9216, F32, 512)]
        RS = [RS[0], V(Z + 11264, F32, 512)]


        def ln_stats(ci):
            c0, n = CH[ci]
            p1, p1b = bank()
            p2, p2b = bank()
            S_.op("pe", lambda e: e.matmul(p1[:, 0:n], lhsT=onesf, rhs=S1[:, c0:c0 + n], start=True, stop=True),
                  reads=[b("onesf"), b("S1", ci)], writes=[p1b])
            S_.op("pe", lambda e: e.matmul(p2[:, 0:n], lhsT=onesf, rhs=S2[:, c0:c0 + n], start=True, stop=True),
                  reads=[b("onesf"), b("S2", ci)], writes=[p2b])
            mu, mub = MU[ci % 2], b("mu", ci % 2)
            rs, rsb = RS[ci % 2], b("rs", ci % 2)
            S_.op("dve", lambda e: e.tensor_copy(out=mu[:, 0:n], in_=p1[:, 0:n]),
                  reads=[p1b], writes=[mub])
            S_.op("dve", lambda e: e.tensor_tensor(out=rs[:, 0:n], in0=mu[:, 0:n], in1=mu[:, 0:n], op=ALU.mult),
                  reads=[mub], writes=[rsb])
            S_.op("dve", lambda e: e.tensor_tensor(out=rs[:, 0:n], in0=p2[:, 0:n], in1=rs[:, 0:n], op=ALU.subtract),
                  reads=[p2b, rsb], writes=[rsb])
            S_.op("dve", lambda e: e.tensor_scalar(out=rs[:, 0:n], in0=rs[:, 0:n], scalar1=0.0, scalar2=EPS, op0=ALU.max, op1=ALU.add),
                  reads=[rsb], writes=[rsb])
            S_.op("act", lambda e: e.activation(out=rs[:, 0:n], in_=rs[:, 0:n], func=AF.Ln),
                  reads=[rsb], writes=[rsb])
            S_.op("act", lambda e: e.activation(out=rs[:, 0:n], in_=rs[:, 0:n], func=AF.Exp, scale=-0.5),
                  reads=[rsb], writes=[rsb])

        NEGO = V(ARENA + 4608, BF16, 128)
        S_.op("pool", lambda e: e.memset(NEGO, -1.0 / D), writes=[b("nego")])
        S1B = [V(Z + 13312 + k * 1024, BF16, 512) for k in range(2)]

        def ln_norm(ci):
            c0, n = CH[ci]
            rs, rsb = RS[ci % 2], b("rs", ci % 2)
            s1b, s1bb = S1B[ci % 2], b("s1b", ci % 2)
            S_.op("dve", lambda e: e.tensor_copy(out=s1b[:, 0:n], in_=S1[:, c0:c0 + n]),
                  reads=[b("S1", ci)], writes=[s1bb])
            for j2 in range(KT):
                t2, t2b = T2[j2 % 2], b("t2", j2 % 2)
                pc_, pcb_ = bank()
                S_.op("pe", lambda e, pc_=pc_, j2=j2: e.matmul(pc_[:, 0:n], lhsT=identb, rhs=dz[:, j2, c0:c0 + n],
                                                              start=True, stop=False),
                      reads=[b("identb"), b("dz", j2, ci)], writes=[pcb_])
                S_.op("pe", lambda e, pc_=pc_: e.matmul(pc_[:, 0:n], lhsT=NEGO, rhs=s1b[:, 0:n],
                                                       start=False, stop=True),
                      reads=[b("nego"), s1bb], writes=[pcb_])
                S_.op("dve", lambda e, t2=t2, pc_=pc_: e.tensor_tensor(
                    out=t2[:, 0:n], in0=pc_[:, 0:n], in1=rs[:, 0:n], op=ALU.mult),
                    reads=[pcb_, rsb], writes=[t2b])
                S_.op("act", lambda e, t2=t2, j2=j2: e.activation(
                    out=dn[:, j2, c0:c0 + n], in_=t2[:, 0:n], func=AF.Silu, bias=col(PP_BLN, j2), scale=col(PP_GLN, j2)),
                    reads=[t2b, b("pp")], writes=[b("dn", j2, ci)])

        S_.op("pool", lambda e: e.memset(ub[:, 0:30], 0.0), writes=[b("ubpad")])
        ACC = [V(Z + k * 2048, F32, 512) for k in range(2)]
        for j in range(KT):
            if j + 1 < KT:
                ws_n, wsb_n = wsbuf()
                wload(ws_n, wsb_n, 0, w_in_v[:, :, 16 + j + 1, :])
                wload(ws_n, wsb_n, 1, w_in_v[:, :, 24 + j + 1, :])
            d31 = D31[j % 2]
            S_.op("dve", lambda e, j=j: e.tensor_copy(out=ub_s[:, :, 0:30], in_=cc[:, j, :, :]),
                  reads=[b("cc")], writes=[b("ubs")])
            for ci, (c0, n) in enumerate(CH):
                if j == 0 and ci + 1 < len(CH):
                    phase_A_chunk(ci + 1)
                pa, pab = bank()
                pc, pcb = bank()
                mm8(pa, pab, n, ws, wsb, 0, hT, hT_bufs("hT", c0, n), c0)
                mm8(pc, pcb, n, ws, wsb, 1, hT, hT_bufs("hT", c0, n), c0)
                sg, sgb = SIG[ci % 2], b("sig", ci % 2)
                S_.op("act", lambda e, pc=pc, n=n, sg=sg, j=j: e.activation(out=sg[:, 0:n], in_=pc[:, 0:n], func=AF.Sigmoid,
                                                                           bias=col(PP_BIN, 24 + j)),
                      reads=[pcb, b("pp")], writes=[sgb])
                bca = col(PP_BIN, 16 + j)
                if ci < 4:
                    S_.op("dve", lambda e, pa=pa, sg=sg, c0=c0, bca=bca: e.scalar_tensor_tensor(
                        out=ub[:, 30 + c0:30 + c0 + 512], in0=pa[:, 0:512], scalar=bca, in1=sg[:, 0:512],
                        op0=ALU.add, op1=ALU.mult),
                        reads=[pab, sgb, b("pp")], writes=[b("ub", ci)])
                    if ci == 3:
                        S_.op("dve", lambda e, pa=pa, sg=sg, bca=bca, j=j: e.scalar_tensor_tensor(
                            out=stage[:, j, 0, 4:34], in0=pa[:, 482:512], scalar=bca, in1=sg[:, 482:512],
                            op0=ALU.add, op1=ALU.mult),
                            reads=[pab, sgb, b("pp")], writes=[b("stage")])
                else:
                    pa3 = pa[:, 0:256].rearrange("p (s n) -> p s n", s=NS)
                    sg3 = sg[:, 0:256].rearrange("p (s n) -> p s n", s=NS)
                    S_.op("dve", lambda e, pa3=pa3, sg3=sg3, bca=bca: e.scalar_tensor_tensor(
                        out=ub_s[:, :, 30:94], in0=pa3, scalar=bca, in1=sg3, op0=ALU.add, op1=ALU.mult),
                        reads=[pab, sgb, b("pp")], writes=[b("ub", ci)])
                    S_.op("dve", lambda e, pa3=pa3, sg3=sg3, bca=bca, j=j: e.scalar_tensor_tensor(
                        out=stage[:, j, 1:5, 4:34], in0=pa3[:, :, 34:64], scalar=bca, in1=sg3[:, :, 34:64],
                        op0=ALU.add, op1=ALU.mult),
                        reads=[pab, sgb, b("pp")], writes=[b("stage")])
            if j == 0:
                S_.retire("DN")
            if j + 1 < KT:
                build_d31(j + 1)
            NPE = npe_of(j)
            for ci, (c0, n) in enumerate(CH):
                pd, pdb = bank()
                acc, accb = ACC[ci % 2], b("acc", ci % 2)
                if ci < 4:
                    rb = [b("ub", ci), b("ubpad")] + ([b("ub", ci - 1)] if ci > 0 else [])
                else:
                    rb = [b("ub", 4), b("ubs")]
                for k in range(NPE):
                    if ci < 4:
                        rhs = ub[:, c0 + k:c0 + k + 512]
                    else:
                        rhs = ub_s[:, :, k:k + 64]
                    pdo = pd[:, 0:n] if ci < 4 else pd[:, 0:256].rearrange("p (s n) -> p s n", s=NS)
                    S_.op("pe", lambda e, pdo=pdo, d31=d31, k=k, rhs=rhs, last=(k == NPE - 1): e.matmul(
                        pdo, lhsT=d31[:, k, :], rhs=rhs, start=(k == 0), stop=last),
                        reads=[b("d31", j % 2, k)] + rb, writes=[pdb])
                acco = acc[:, 0:n] if ci < 4 else acc[:, 0:256].rearrange("p (s n) -> p s n", s=NS)
                for k in range(NPE, 31):
                    src = ub[:, c0 + k:c0 + k + 512] if ci < 4 else ub_s[:, :, k:k + 64]
                    wk = col(PP_WDW, j * 31 + k)
                    if k == NPE:
                        S_.op("dve", lambda e, acco=acco, src=src, wk=wk: e.tensor_scalar(
                            out=acco, in0=src, scalar1=wk, scalar2=None, op0=ALU.mult),
                            reads=rb + [b("pp")], writes=[accb])
                    else:
                        S_.op("dve", lambda e, acco=acco, src=src, wk=wk: e.scalar_tensor_tensor(
                            out=acco, in0=src, scalar=wk, in1=acco, op0=ALU.mult, op1=ALU.add),
                            reads=rb + [b("pp"), accb], writes=[accb])
                bdw = col(PP_BDW, j)
                if NPE < 31:
                    S_.op("dve", lambda e, pd=pd, n=n, acc=acc, bdw=bdw: e.scalar_tensor_tensor(
                        out=acc[:, 0:n], in0=pd[:, 0:n], scalar=bdw, in1=acc[:, 0:n], op0=ALU.add, op1=ALU.add),
                        reads=[pdb, b("pp"), accb], writes=[accb])
                else:
                    S_.op("act", lambda e, pd=pd, n=n, acc=acc, bdw=bdw: e.activation(
                        out=acc[:, 0:n], in_=pd[:, 0:n], func=AF.Identity, bias=bdw),
                        reads=[pdb, b("pp")], writes=[accb])
                S_.op("act", lambda e, acc=acc, n=n, c0=c0, j=j: e.activation(
                    out=dz[:, j, c0:c0 + n], in_=acc[:, 0:n], func=AF.Identity),
                    reads=[accb], writes=[b("dz", j, ci)])
                dq, dqb = DQ[ci % 2], b("dq", ci % 2)
                S_.op("act", lambda e, acc=acc, n=n, dq=dq: e.activation(
                    out=dq[:, 0:n], in_=acc[:, 0:n], func=AF.Square),
                    reads=[accb], writes=[dqb])
                if j == 0:
                    S_.op("dve", lambda e, acc=acc, n=n, c0=c0: e.tensor_copy(out=S1[:, c0:c0 + n], in_=acc[:, 0:n]),
                          reads=[accb], writes=[b("S1", ci)])
                    S_.op("dve", lambda e, n=n, c0=c0, dq=dq: e.tensor_copy(out=S2[:, c0:c0 + n], in_=dq[:, 0:n]),
                          reads=[dqb], writes=[b("S2", ci)])
                else:
                    S_.op("dve", lambda e, acc=acc, n=n, c0=c0: e.tensor_tensor(
                        out=S1[:, c0:c0 + n], in0=S1[:, c0:c0 + n], in1=acc[:, 0:n], op=ALU.add),
                        reads=[accb, b("S1", ci)], writes=[b("S1", ci)])
                    S_.op("dve", lambda e, n=n, c0=c0, dq=dq: e.tensor_tensor(
                        out=S2[:, c0:c0 + n], in0=S2[:, c0:c0 + n], in1=dq[:, 0:n], op=ALU.add),
                        reads=[dqb, b("S2", ci)], writes=[b("S2", ci)])
                if j == KT - 1:
                    ln_stats(ci)
                    if ci >= 1:
                        ln_norm(ci - 1)
            if j == KT - 1:
                ln_norm(len(CH) - 1)
            if j + 1 < KT:
                ws, wsb = ws_n, wsb_n
        ws, wsb = wsbuf()
        wload(ws, wsb, 0, w_in_v[:, :, 0, :])
        wload(ws, wsb, 1, w_in_v[:, :, 8, :])
        S_.retire("S")
        S_.retire("Z")
        S_.retire("Zx1")
        S_.retire("Zs")
        S_.retire("GLa")
        S_.retire("GLb")

        Gb = V(ARENA, BF16, T)
        XX = [V(Z, F32, T), V(Z + 9216, F32, T)]
        R = V(Z + 18432, F32, T)
        I = V(Z + 27648, F32, T)
        A = V(Z + 36864, F32, T)
        D4 = [V(S + k * 1024, BF16, 4, 128) for k in range(2)]
        WG = V(S + 2048, BF16, KT, 2, 128)
        xlb = V(S + 6144, BF16, 2320)
        xlb_s = xlb[:, 2051:2319].rearrange("p (s n) -> p s n", s=NS)
        xcb = V(S + 10784, BF16, T)
        TAIL_ENG = "dve" if _os.environ.get("K_NOPOOL") else "pool"

        S_.op("pool", lambda e: e.memset(V(S + 2048, BF16, KT * 2 * 128), 0.0),
              writes=[b("wg", g_, q_) for g_ in range(2) for q_ in range(2)])
        S_.op("pool", lambda e: e.memset(xlb[:, 0:3], 0.0), writes=[b("xlbpad")])
        for g, wsrc in enumerate((w_rgr_d, w_rgi_d)):
            wv = wsrc.rearrange("(t q) d e -> q d t e", q=2)
            for q in range(2):
                S_.dma("pool", lambda e, g=g, q=q, wv=wv: e.dma_start(
                    out=WG[64 * q:64 * q + 64, :, g, 64 * q:64 * q + 64], in_=wv[q]),
                    reads=[], writes=[b("wg", g, q)])

        def build_d4(j):
            d4 = D4[j % 2]
            for k in range(4):
                S_.op("act", lambda e, k=k, d4=d4, j=j: e.activation(out=d4[:, k, :], in_=identf, func=AF.Identity,
                                                                    scale=col(PP_WLC, j * 4 + k)),
                      reads=[b("identf"), b("pp")], writes=[b("d4", j % 2, k)])

        def emit_exp_min(jq, ci):
            c0, n = CH[ci]
            S_.op("act", lambda e: e.activation(
                out=A[:, c0:c0 + n], in_=R[:, c0:c0 + n], func=AF.Exp, scale=cl[:, 0, jq:jq + 1]),
                reads=[b("R", ci), b("cl")], writes=[b("A", ci)])
            S_.op("act", lambda e: e.activation(
                out=R[:, c0:c0 + n], in_=R[:, c0:c0 + n], func=AF.Exp, scale=cl[:, 1, jq:jq + 1]),
                reads=[b("R", ci), b("cl")], writes=[b("R", ci)])
            S_.op("dve", lambda e: e.tensor_scalar(
                out=R[:, c0:c0 + n], in0=R[:, c0:c0 + n], scalar1=1.0, scalar2=-1.0, op0=ALU.min, op1=ALU.mult),
                reads=[b("R", ci)], writes=[b("R", ci)])

        def emit_sqrt(jq):
            for ci, (c0, n) in enumerate(CH):
                S_.op("act", lambda e, n=n, c0=c0: e.activation(
                    out=R[:, c0:c0 + n], in_=R[:, c0:c0 + n], func=AF.Sqrt, bias=one_ap),
                    reads=[b("R", ci), b("cst")], writes=[b("R", ci)])

        def g_gl(jq):
            return 1 if (jq // 2) % 2 == 0 else 3

        def emit_gl(jq):
            ws_q, wsb_q = lru_ws[jq]
            for ci, (c0, n) in enumerate(CH):
                pg, pgb = bank()
                mm8(pg, pgb, n, ws_q, wsb_q, g_gl(jq), hT, hT_bufs("hT", c0, n), c0)
                S_.op("act", lambda e, pg=pg, n=n, c0=c0: e.activation(
                    out=Gb[:, c0:c0 + n], in_=pg[:, 0:n], func=AF.Gelu_apprx_tanh, bias=col(PP_BIN, 8 + jq)),
                    reads=[pgb, b("pp")], writes=[b("G", ci)])

        build_d4(0)
        lru_ws = {0: (ws, wsb)}
        for j in range(KT + 1):
            jp = j - 1
            if j < KT:
                ws, wsb = lru_ws[j]
                if j + 1 < KT:
                    ws_n, wsb_n = wsbuf()
                    wload(ws_n, wsb_n, 0, w_in_v[:, :, j + 1, :])
                    wload(ws_n, wsb_n, g_gl(j + 1), w_in_v[:, :, 8 + j + 1, :])
                    lru_ws[j + 1] = (ws_n, wsb_n)
                    build_d4(j + 1)
                X = XX[j % 2]
                d4 = D4[j % 2]
                S_.op("dve", lambda e, j=j: e.tensor_copy(out=xlb_s[:, :, 0:3], in_=lc[:, j, :, :]),
                      reads=[b("lc")], writes=[b("xlbs")])
                for ci, (c0, n) in enumerate(CH):
                    if jp >= 0:
                        emit_exp_min(jp, ci)
                    px, pxb = bank()
                    mm8(px, pxb, n, ws, wsb, 0, hT, hT_bufs("hT", c0, n), c0)
                    bxl = col(PP_BIN, j)
                    if ci < 4:
                        S_.op("dve", lambda e, px=px, c0=c0, bxl=bxl: e.tensor_scalar(
                            out=xlb[:, 3 + c0:3 + c0 + 512], in0=px[:, 0:512], scalar1=bxl, scalar2=None, op0=ALU.add),
                            reads=[pxb, b("pp")], writes=[b("xlb", ci)])
                        if ci == 3:
                            S_.op("dve", lambda e, px=px, bxl=bxl, j=j: e.tensor_scalar(
                                out=stage[:, j, 0, 1:4], in0=px[:, 509:512], scalar1=bxl, scalar2=None, op0=ALU.add),
                                reads=[pxb, b("pp")], writes=[b("stage")])
                    else:
                        px3 = px[:, 0:256].rearrange("p (s n) -> p s n", s=NS)
                        S_.op("dve", lambda e, px3=px3, bxl=bxl: e.tensor_scalar(
                            out=xlb_s[:, :, 3:67], in0=px3, scalar1=bxl, scalar2=None, op0=ALU.add),
                            reads=[pxb, b("pp")], writes=[b("xlb", ci)])
                        S_.op("dve", lambda e, px3=px3, bxl=bxl, j=j: e.tensor_scalar(
                            out=stage[:, j, 1:5, 1:4], in0=px3[:, :, 61:64], scalar1=bxl, scalar2=None, op0=ALU.add),
                            reads=[pxb, b("pp")], writes=[b("stage")])
            if jp >= 0:
                if j == KT:
                    for ci in range(len(CH)):
                        emit_exp_min(jp, ci)
                emit_sqrt(jp)
                emit_gl(jp)
                Xp = XX[jp % 2]
                for ci, (c0, n) in enumerate(CH):
                    S_.op("dve", lambda e, n=n, c0=c0: e.tensor_tensor(
                        out=I[:, c0:c0 + n], in0=I[:, c0:c0 + n], in1=R[:, c0:c0 + n], op=ALU.mult),
                        reads=[b("I", ci), b("R", ci)], writes=[b("I", ci)])
                    S_.op(TAIL_ENG, lambda e, n=n, c0=c0, Xp=Xp: e.tensor_tensor(
                        out=Xp[:, c0:c0 + n], in0=Xp[:, c0:c0 + n], in1=I[:, c0:c0 + n], op=ALU.mult),
                        reads=[b("I", ci), b("X", jp % 2, ci)], writes=[b("X", jp % 2, ci)])
            if j < KT:
                for ci, (c0, n) in enumerate(CH):
                    pc, pcb = bank()
                    if ci < 4:
                        rb = [b("xlb", ci), b("xlbpad")] + ([b("xlb", ci - 1)] if ci > 0 else [])
                    else:
                        rb = [b("xlb", 4), b("xlbs")]
                    for k in range(4):
                        rhs = xlb[:, c0 + k:c0 + k + 512] if ci < 4 else xlb_s[:, :, k:k + 64]
                        pco_ = pc[:, 0:n] if ci < 4 else pc[:, 0:256].rearrange("p (s n) -> p s n", s=NS)
                        S_.op("pe", lambda e, pco_=pco_, d4=d4, k=k, rhs=rhs: e.matmul(
                            pco_, lhsT=d4[:, k, :], rhs=rhs, start=(k == 0), stop=(k == 3)),
                            reads=[b("d4", j % 2, k)] + rb, writes=[pcb])
                    S_.op("dve", lambda e, pc=pc, n=n, c0=c0, j=j: e.tensor_scalar(
                        out=xcb[:, c0:c0 + n], in0=pc[:, 0:n], scalar1=col(PP_BLC, j), scalar2=None, op0=ALU.add),
                        reads=[pcb, b("pp")], writes=[b("xcb", ci)])
                    S_.op("dve", lambda e, pc=pc, n=n, c0=c0, j=j, X=X: e.tensor_scalar(
                        out=X[:, c0:c0 + n], in0=pc[:, 0:n], scalar1=col(PP_BLC, j), scalar2=None, op0=ALU.add),
                        reads=[pcb, b("pp")], writes=[b("X", j % 2, ci)])
            if jp >= 0:
                Xp = XX[jp % 2]
                for ci in range(4):
                    c0 = ci * 512
                    init = 0.0 if ci == 0 else Xp[:, c0 - 1:c0]
                    S_.op("dve", lambda e, c0=c0, init=init, Xp=Xp: e.tensor_tensor_scan(
                        out=Xp[:, c0:c0 + 512], data0=A[:, c0:c0 + 512], data1=Xp[:, c0:c0 + 512], initial=init,
                        op0=ALU.mult, op1=ALU.add),
                        reads=[b("A", ci), b("X", jp % 2, ci)] + ([b("X", jp % 2, ci - 1)] if ci > 0 else []),
                        writes=[b("X", jp % 2, ci)])
                for s in range(NS):
                    c0 = TP + s * TS
                    S_.op("dve", lambda e, c0=c0, s=s, jp=jp, Xp=Xp: e.tensor_tensor_scan(
                        out=Xp[:, c0:c0 + TS], data0=A[:, c0:c0 + TS], data1=Xp[:, c0:c0 + TS], initial=h0[:, jp, s:s + 1],
                        op0=ALU.mult, op1=ALU.add),
                        reads=[b("A", 4), b("X", jp % 2, 4), b("h0")], writes=[b("X", jp % 2, 4)])
                S_.op("dve", lambda e, jp=jp, Xp=Xp: e.tensor_copy(out=stage[:, jp, 0, 0:1], in_=Xp[:, TP - 1:TP]),
                      reads=[b("X", jp % 2, 3)], writes=[b("stage")])
                X3 = Xp[:, TP:T].rearrange("p (s n) -> p s n", s=NS)
                S_.op("dve", lambda e, jp=jp, X3=X3: e.tensor_copy(out=stage[:, jp, 1:5, 0:1], in_=X3[:, :, TS - 1:TS]),
                      reads=[b("X", jp % 2, 4)], writes=[b("stage")])
                for ci, (c0, n) in enumerate(CH):
                    S_.op(TAIL_ENG, lambda e, n=n, c0=c0, jp=jp, Xp=Xp: e.tensor_tensor(
                        out=glru[:, jp, c0:c0 + n], in0=Gb[:, c0:c0 + n], in1=Xp[:, c0:c0 + n], op=ALU.mult),
                        reads=[b("G", ci), b("X", jp % 2, ci)], writes=[b("glru", jp, ci)])
            if j < KT:
                for ci, (c0, n) in enumerate(CH):
                    pr, prb = bank()
                    pi, pib = bank()
                    S_.op("pe", lambda e, pr=pr, n=n, c0=c0, j=j: e.matmul(pr[:, 0:n], lhsT=WG[:, j, 0, :], rhs=xcb[:, c0:c0 + n],
                                                                            start=True, stop=True),
                          reads=[b("wg", 0, 0), b("wg", 0, 1), b("xcb", ci)], writes=[prb])
                    S_.op("pe", lambda e, pi=pi, n=n, c0=c0, j=j: e.matmul(pi[:, 0:n], lhsT=WG[:, j, 1, :], rhs=xcb[:, c0:c0 + n],
                                                                            start=True, stop=True),
                          reads=[b("wg", 1, 0), b("wg", 1, 1), b("xcb", ci)], writes=[pib])
                    S_.op("act", lambda e, pr=pr, n=n, c0=c0, j=j: e.activation(
                        out=R[:, c0:c0 + n], in_=pr[:, 0:n], func=AF.Sigmoid, bias=col(PP_BR, j)),
                        reads=[prb, b("pp")], writes=[b("R", ci)])
                    S_.op("act", lambda e, pi=pi, n=n, c0=c0, j=j: e.activation(
                        out=I[:, c0:c0 + n], in_=pi[:, 0:n], func=AF.Sigmoid, bias=col(PP_BI, j)),
                        reads=[pib, b("pp")], writes=[b("I", ci)])
        finals.append(S_.dma("sp", lambda e: e.dma_start(out=st_d, in_=stage_flat), reads=[b("stage")]))

        def load_merge_w(e_):
            ws_, wsb_ = wsbuf()
            wload(ws_, wsb_, 0, w_in_v[:, :, 32 + e_, :])
            wload(ws_, wsb_, 1, w_in_v[:, :, 40 + e_, :])
            wload(ws_, wsb_, 2, w_lruo_v[:, :, e_, :])
            wload(ws_, wsb_, 3, w_ccmo_v[:, :, e_, :])
            return ws_, wsb_

        ws, wsb = load_merge_w(0)
        S_.retire("S")
        S_.retire("Z")
        S_.retire("Zx1")
        S_.retire("Zs")

        merged = V(Z, BF16, KT, T)
        SGL = [V(Z + 36864, F32, 512)] * 2
        SGC = [V(Z + 38912, F32, 512)] * 2
        M1 = [V(Z + 40960, F32, 512)] * 2
        M2 = [V(Z + 43008, F32, 512)] * 2
        WO = V(S, BF16, KT, D)
        S_.dma("pool", lambda e: e.dma_start(out=WO, in_=w_out_v), writes=[b("wo")])
        it = 0
        for e_ in range(KT):
            if e_ + 1 < KT:
                ws_n, wsb_n = load_merge_w(e_ + 1)
            for ci, (c0, n) in enumerate(CH):
                psl, pslb = bank()
                psc, pscb = bank()
                plo, plob = bank()
                pco, pcob = bank()
                mm8(psl, pslb, n, ws, wsb, 0, hT, hT_bufs("hT", c0, n), c0)
                mm8(psc, pscb, n, ws, wsb, 1, hT, hT_bufs("hT", c0, n), c0)
                mm8(plo, plob, n, ws, wsb, 2, glru, [b("glru", kt, ci) for kt in range(KT)], c0)
                mm8(pco, pcob, n, ws, wsb, 3, dn, [b("dn", kt, ci) for kt in range(KT)], c0)
                k2 = it % 2
                it += 1
                sgl, sglb = SGL[k2], b("sgl", 0)
                sgc, sgcb = SGC[k2], b("sgc", 0)
                m1, m1b = M1[k2], b("m1", 0)
                m2, m2b = M2[k2], b("m2", 0)
                S_.op("act", lambda e, psl=psl, n=n, sgl=sgl, e_=e_: e.activation(
                    out=sgl[:, 0:n], in_=psl[:, 0:n], func=AF.Sigmoid, bias=col(PP_BIN, 32 + e_)),
                    reads=[pslb, b("pp")], writes=[sglb])
                S_.op("act", lambda e, psc=psc, n=n, sgc=sgc, e_=e_: e.activation(
                    out=sgc[:, 0:n], in_=psc[:, 0:n], func=AF.Sigmoid, bias=col(PP_BIN, 40 + e_)),
                    reads=[pscb, b("pp")], writes=[sgcb])
                S_.op("dve", lambda e, plo=plo, n=n, sgl=sgl, m1=m1: e.tensor_tensor(
                    out=m1[:, 0:n], in0=plo[:, 0:n], in1=sgl[:, 0:n], op=ALU.mult),
                    reads=[plob, sglb], writes=[m1b])
                S_.op("dve", lambda e, pco=pco, n=n, sgc=sgc, m2=m2: e.tensor_tensor(
                    out=m2[:, 0:n], in0=pco[:, 0:n], in1=sgc[:, 0:n], op=ALU.mult),
                    reads=[pcob, sgcb], writes=[m2b])
                S_.op("dve", lambda e, n=n, m1=m1, m2=m2, e_=e_, c0=c0: e.tensor_tensor(
                    out=merged[:, e_, c0:c0 + n], in0=m1[:, 0:n], in1=m2[:, 0:n], op=ALU.add),
                    reads=[m1b, m2b], writes=[b("mg", e_, ci)])
            if e_ + 1 < KT:
                ws, wsb = ws_n, wsb_n
        S_.retire("DN")
        S_.retire("GLa")
        S_.retire("GLb")
        S_.retire("R1")

        X2D = [V(GL + 8192 + k * 4096, F32, D) for k in range(3)]
        XAD = [V(GL + 20480 + k * 4096, F32, D) for k in range(3)]
        XSD = [V(GL + 32768 + k * 2048, BF16, D) for k in range(2)]
        JK_D = V(Z + 36864, BF16, D)
        WD = V(DN, BF16, FT, D)
        h2T = V(R1, BF16, KT, T)
        S_.dma("sp", lambda e: e.dma_start(out=GBflat, in_=gffn_d), writes=[b("GB")])

        def load_ffn_w(f):
            ws_, wsb_ = wsbuf()
            wload(ws_, wsb_, 0, w_gate_v[:, :, f, :])
            wload(ws_, wsb_, 1, w_up_v[:, :, f, :])
            return ws_, wsb_

        ffn_w0 = load_ffn_w(0)
        WDQ = list(range(FT))

        def load_xD(i):
            xa = XAD[i % 3]
            S_.dma("sp", lambda e: e.dma_start(out=xa, in_=x_d[i * 128:(i + 1) * 128, :]), writes=[b("xaD", i % 3)])

        for i in range(3):
            load_xD(i)
        trD = {}
        for i in range(NT + 1):
            if i < NT:
                xa, xab = XAD[i % 3], b("xaD", i % 3)
                x2, x2b = X2D[i % 3], b("x2D", i % 3)
                ci = min(i // 4, 4)
                pos = []
                for hh in range(2):
                    po, pob = bank()
                    pos.append((po, pob))
                    for kt in range(KT):
                        S_.op("pe", lambda e, po=po, kt=kt, i=i, hh=hh: e.matmul(
                            po[:, 0:512], lhsT=merged[:, kt, i * 128:(i + 1) * 128], rhs=WO[:, kt, hh * 512:(hh + 1) * 512],
                            start=(kt == 0), stop=(kt == KT - 1)),
                            reads=[b("wo")] + [b("mg", kt, ci) for kt in range(KT)], writes=[pob])
            if i >= 1:
                i2 = i - 1
                x2p, x2pb = X2D[i2 % 3], b("x2D", i2 % 3)
                xs, xsb = XSD[i2 % 2], b("xsD", i2 % 2)
                norm_scale(i2, 1, x2p, [x2pb], xs, xsb)
                S_.dma("pool", lambda e, i2=i2, x2p=x2p: e.dma_start(out=x2s_d[i2 * 128:(i2 + 1) * 128, :], in_=x2p),
                       reads=[x2pb], writes=[b("x2s", i2)])
                trD[i2] = norm_transpose(i2, xs, xsb)
            if i < NT:
                for hh in range(2):
                    po, pob = pos[hh]
                    S_.op("dve", lambda e, po=po, hh=hh, xa=xa, x2=x2: e.tensor_tensor(
                        out=x2[:, hh * 512:(hh + 1) * 512], in0=po[:, 0:512], in1=xa[:, hh * 512:(hh + 1) * 512], op=ALU.add),
                        reads=[pob, xab], writes=[x2b])
                if i + 3 < NT:
                    load_xD(i + 3)
                norm_stage1([i], 1, lambda i_: (X2D[i_ % 3], [b("x2D", i_ % 3)]), JK_D, "jkD")
            if i >= 1:
                pv3, pbb = trD.pop(i - 1)
                norm_evac(i - 1, pv3, pbb, h2T, "h2T")
        S_.retire("Z")
        S_.retire("Zx1")
        S_.retire("Zs")
        S_.retire("S")
        S_.retire("GLb")

        FF = V(DN + 45056, BF16, FT, 1280)
        X2T = [V(DN + 101376 + k * 4096, F32, D) for k in range(2)]
        YT = [V(DN + 109568 + k * 4096, F32, D) for k in range(2)]
        assert DN + 109568 + 8192 <= WS0
        SGF = [V(S + k * 2048, F32, 512) for k in range(2)]
        JK_F = V(S + 4096, BF16, D)
        SD = V(S + 8192, F32, 32)
        it = 0
        for hi_, (chs, tiles, t0, tn) in enumerate(HALVES):
            ws, wsb = ffn_w0
            for f in range(FT):
                if f + 1 < FT:
                    ws_n, wsb_n = load_ffn_w(f + 1)
                elif hi_ == 0:
                    ffn_w0 = load_ffn_w(0)
                if hi_ == 0:
                    S_.dma("pool", lambda e, f=f: e.dma_start(out=WD[:, f:f + 1, :], in_=w_down_v[:, f:f + 1, :]),
                           writes=[b("wd", f)])
                for ci in chs:
                    c0, n = CH[ci]
                    pg, pgb = bank()
                    pu, pub = bank()
                    mm8(pg, pgb, n, ws, wsb, 0, h2T, hT_bufs("h2T", c0, n), c0)
                    mm8(pu, pub, n, ws, wsb, 1, h2T, hT_bufs("h2T", c0, n), c0)
                    sgf, sgfb = SGF[it % 2], b("sgf", it % 2)
                    it += 1
                    S_.op("act", lambda e, pg=pg, n=n, sgf=sgf: e.activation(out=sgf[:, 0:n], in_=pg[:, 0:n], func=AF.Silu),
                          reads=[pgb], writes=[sgfb])
                    S_.op("dve", lambda e, pu=pu, n=n, sgf=sgf, f=f, c0=c0, t0=t0: e.tensor_tensor(
                        out=FF[:, f, c0 - t0:c0 - t0 + n], in0=pu[:, 0:n], in1=sgf[:, 0:n], op=ALU.mult),
                        reads=[pub, sgfb], writes=[b("ff", f, ci - chs[0])])
                if f + 1 < FT:
                    ws, wsb = ws_n, wsb_n
            if hi_ == 0:
                S_.dma("sp", lambda e: e.dma_start(out=GBflat, in_=gfin_d), writes=[b("GB")])
            for i in tiles:
                ci = min(i // 4, 4)
                x2t, x2tb = X2T[i % 2], b("x2t", i % 2)
                yt, ytb = YT[i % 2], b("yt", i % 2)
                S_.dma("sp", lambda e, x2t=x2t, i=i: e.dma_start(out=x2t, in_=x2s_d[i * 128:(i + 1) * 128, :]),
                       reads=[b("x2s", i)], writes=[x2tb])
                lo = i * 128 - t0
                for hh in range(2):
                    po, pob = bank()
                    for f in range(FT):
                        S_.op("pe", lambda e, po=po, f=f, lo=lo, hh=hh: e.matmul(
                            po[:, 0:512], lhsT=FF[:, f, lo:lo + 128], rhs=WD[:, f, hh * 512:(hh + 1) * 512],
                            start=(f == 0), stop=(f == FT - 1)),
                            reads=[b("wd", WDQ[f]), b("ff", f, ci - chs[0])], writes=[pob])
                    S_.op("dve", lambda e, po=po, hh=hh, x2t=x2t: e.tensor_tensor(
                        out=x2t[:, hh * 512:(hh + 1) * 512], in0=po[:, 0:512], in1=x2t[:, hh * 512:(hh + 1) * 512], op=ALU.add),
                        reads=[pob, x2tb], writes=[x2tb])
                S_.op("act", lambda e, x2t=x2t, i=i: e.activation(out=JK_F, in_=x2t, func=AF.Square, accum_out=ss[:, 2, i:i + 1]),
                      reads=[x2tb, b("ss")], writes=[b("jkF"), b("ss3", i)])
                S_.op("act", lambda e, i=i: e.activation(out=SD[:, i:i + 1], in_=ss[:, 2, i:i + 1], func=AF.Sqrt,
                                                        bias=eps_ap, scale=1.0 / D),
                      reads=[b("ss3", i), b("cst")], writes=[b("sd", i)])
                S_.op("dve", lambda e, i=i: e.reciprocal(out=SD[:, i:i + 1], in_=SD[:, i:i + 1]),
                      reads=[b("sd", i)], writes=[b("sd", i)])
                S_.op("dve", lambda e, x2t=x2t, yt=yt, i=i: e.scalar_tensor_tensor(
                    out=yt, in0=x2t, scalar=SD[:, i:i + 1], in1=GBflat, op0=ALU.mult, op1=ALU.mult),
                    reads=[x2tb, b("sd", i), b("GB")], writes=[ytb])
                finals.append(S_.dma("pool", lambda e, yt=yt, i=i: e.dma_start(out=y_d[i * 128:(i + 1) * 128, :], in_=yt),
                                     reads=[ytb]))
        S_.emit(final_wait_ops=finals)
    return nc


_NC_CACHE = {}


def _colT(v):
    return np.ascontiguousarray(np.asarray(v, np.float32).reshape(-1, 128).T)


def kernel(x_prompt, x_sample, state_lru_h, cache_lru_conv, cache_ccm_conv,
           g_mix, w_in, b_in, w_lru_conv, b_lru_conv, w_rg_r, b_rg_r, w_rg_i, b_rg_i,
           lru_lambda, w_lru_o, w_ccm_dw, b_ccm_dw, g_ccm_ln, b_ccm_ln, w_ccm_o, w_out,
           g_ffn, w_ffn_gate, w_ffn_up, w_ffn_down, g_final):
    f = lambda a: np.ascontiguousarray(np.asarray(a, np.float32))
    n_cores = 8
    pp = np.zeros((128, NPP), np.float32)
    pp[:, PP_BIN:PP_BIN + 48] = _colT(b_in[0])
    pp[:, PP_WLC:PP_WLC + 32] = f(w_lru_conv[0]).reshape(4, 8, 128).transpose(2, 1, 0).reshape(128, 32)
    pp[:, PP_BLC:PP_BLC + 8] = _colT(b_lru_conv[0])
    pp[:, PP_BR:PP_BR + 8] = _colT(b_rg_r[0])
    pp[:, PP_BI:PP_BI + 8] = _colT(b_rg_i[0])
    pp[:, PP_LAM:PP_LAM + 8] = _colT(lru_lambda[0])
    pp[:, PP_WDW:PP_WDW + 248] = f(w_ccm_dw[0]).reshape(31, 8, 128).transpose(2, 1, 0).reshape(128, 248)
    pp[:, PP_BDW:PP_BDW + 8] = _colT(b_ccm_dw[0])
    pp[:, PP_GLN:PP_GLN + 8] = _colT(g_ccm_ln[0])
    pp[:, PP_BLN:PP_BLN + 8] = _colT(b_ccm_ln[0])

    def gB(g):
        return np.ascontiguousarray(np.repeat(_colT(g)[:, :, None], 128, axis=2).reshape(128, D))

    gmixB = gB(g_mix[0])
    gffnB = gB(g_ffn[0])
    gfinB = np.ascontiguousarray(np.broadcast_to(f(g_final)[None, :], (128, D)))
    shared = {
        "pp": pp, "gmixB": gmixB, "gffnB": gffnB, "gfinB": gfinB,
        "w_in": f(w_in[0]), "w_rg_r": f(w_rg_r[0]), "w_rg_i": f(w_rg_i[0]),
        "w_lru_o": f(w_lru_o[0]), "w_ccm_o": f(w_ccm_o[0]), "w_out": f(w_out[0]),
        "w_ffn_gate": f(w_ffn_gate[0]), "w_ffn_up": f(w_ffn_up[0]), "w_ffn_down": f(w_ffn_down[0]),
    }
    xp = f(x_prompt)
    xs = f(x_sample)
    in_maps = []
    for c in range(n_cores):
        sl = slice(NS * c, NS * c + NS)
        m = dict(shared)
        m["x"] = np.ascontiguousarray(np.concatenate([xp[c], xs[sl].reshape(NS * TS, D)], axis=0))
        m["h0"] = np.ascontiguousarray(f(state_lru_h[0, sl]).reshape(NS, 8, 128).transpose(2, 1, 0).reshape(128, KT * NS))
        m["lc"] = np.ascontiguousarray(f(cache_lru_conv[0, sl]).reshape(NS, 3, 8, 128).transpose(3, 2, 0, 1).reshape(128, KT * NS * 3))
        m["cc"] = np.ascontiguousarray(f(cache_ccm_conv[0, sl]).reshape(NS, 30, 8, 128).transpose(3, 2, 0, 1).reshape(128, KT * NS * 30))
        in_maps.append(m)
    if "nc" not in _NC_CACHE:
        _NC_CACHE["nc"] = build_nc()
    nc = _NC_CACHE["nc"]
    import os as _os2
    if _os2.environ.get("K_TRACE"):
        res = run_bass_kernel_spmd(nc, in_maps, core_ids=list(range(n_cores)), trace=True)
        print("EXEC_TIME_NS", res.exec_time_ns)
    else:
        res = run_bass_kernel_spmd(nc, in_maps, core_ids=list(range(n_cores)))
    y_prompt = np.zeros((8, TP, D), np.float32)
    y_sample = np.zeros((32, TS, D), np.float32)
    p_h = np.zeros((1, 8, D), np.float32)
    p_lb = np.zeros((1, 8, 3, D), np.float32)
    p_cb = np.zeros((1, 8, 30, D), np.float32)
    s_h = np.zeros((1, 32, D), np.float32)
    s_lb = np.zeros((1, 32, 3, D), np.float32)
    s_cb = np.zeros((1, 32, 30, D), np.float32)
    for c in range(n_cores):
        r = res.results[c]
        y = np.asarray(r["y"], np.float32)
        y_prompt[c] = y[:TP]
        y_sample[NS * c:NS * c + NS] = y[TP:].reshape(NS, TS, D)
        st5 = np.asarray(r["st"], np.float32).reshape(128, KT, 5, 34).transpose(2, 3, 1, 0).reshape(5, 34, D)
        p_h[0, c] = st5[0, 0]
        p_lb[0, c] = st5[0, 1:4]
        p_cb[0, c] = st5[0, 4:34]
        for s in range(NS):
            s_h[0, NS * c + s] = st5[1 + s, 0]
            s_lb[0, NS * c + s] = st5[1 + s, 1:4]
            s_cb[0, NS * c + s] = st5[1 + s, 4:34]
    return (y_prompt, y_sample, p_h, p_lb, p_cb, s_h, s_lb, s_cb)
```

```python
import contextlib
from collections import defaultdict

import numpy as np
import concourse.bass as bass
import concourse.mybir as mybir
from concourse.bass_utils import run_bass_kernel_spmd

F32 = mybir.dt.float32
BF16 = mybir.dt.bfloat16
U8 = mybir.dt.uint8
AF = mybir.ActivationFunctionType
ALU = mybir.AluOpType

T = 2304
TP = 2048
NS = 4
TS = 64
NT = 18
D = 1024
KT = 8
DFF = 2816
FT = 22
EPS = 1e-6
CH = [(0, 512), (512, 512), (1024, 512), (1536, 512), (2048, 256)]
HALVES = [([0, 1], list(range(0, 8)), 0, 1024), ([2, 3, 4], list(range(8, 18)), 1024, 1280)]

PP_BIN = 0
PP_WLC = 48
PP_BLC = 80
PP_BR = 88
PP_BI = 96
PP_LAM = 104
PP_WDW = 112
PP_BDW = 360
PP_GLN = 368
PP_BLN = 376
NPP = 384

ENGS = ("pe", "act", "dve", "pool", "sp")


class Buf:
    __slots__ = ("writer", "readers", "dreaders", "excl")

    def __init__(self):
        self.writer = None
        self.readers = {}
        self.dreaders = []
        self.excl = False


class Op:
    __slots__ = ("eng", "fn", "idx", "is_dma", "waits", "signal", "sig_count", "dma_sem", "dma_val",
                 "pre_dma_wait")

    def __init__(self, eng, fn, is_dma):
        self.eng = eng
        self.fn = fn
        self.is_dma = is_dma
        self.waits = []
        self.signal = False
        self.sig_count = None
        self.dma_sem = None
        self.dma_val = None
        self.pre_dma_wait = None


class Sched:
    def __init__(self, nc, n_dma_sems=10):
        self.nc = nc
        self.ops = {e: [] for e in ENGS}
        self.n_dma_sems = n_dma_sems
        self.dma_rr = {e: 0 for e in ENGS}
        self.dma_state = {}
        self.same_eng_gap = 8
        self.bufs = {}
        self.name2reg = {}
        self.region_bufs = defaultdict(list)
        self.region_ghost = {}

    def b(self, *key):
        buf = self.bufs.get(key)
        if buf is not None:
            return buf
        buf = Buf()
        self.bufs[key] = buf
        for reg in (self.name2reg.get(key[:2]) or self.name2reg.get(key[0], ())):
            self.region_bufs[reg].append((key, buf))
            g = self.region_ghost.get(reg)
            if g:
                for e, op in g[0].items():
                    if e not in buf.readers or buf.readers[e].idx < op.idx:
                        buf.readers[e] = op
                buf.dreaders.extend(g[1])
        return buf

    def retire(self, reg):
        rd, dr = {}, []
        g = self.region_ghost.get(reg)
        if g:
            rd.update(g[0])
            dr.extend(g[1])
        for key, buf in self.region_bufs[reg]:
            ops = list(buf.readers.values()) + list(buf.dreaders)
            if buf.writer is not None:
                ops.append(buf.writer)
            for op in ops:
                if op.is_dma:
                    if op not in dr:
                        dr.append(op)
                elif op.eng not in rd or rd[op.eng].idx < op.idx:
                    rd[op.eng] = op
            self.bufs.pop(key, None)
        self.region_bufs[reg] = []
        self.region_ghost[reg] = (rd, dr[-12:])

    def _add(self, eng, fn, reads, writes, is_dma):
        op = Op(eng, fn, is_dma)
        lst = self.ops[eng]
        op.idx = len(lst)
        deps = []
        for b in reads:
            if b.writer is not None:
                deps.append(b.writer)
            if b.excl:
                for e2, r in b.readers.items():
                    if e2 != eng:
                        deps.append(r)
        for b in writes:
            if b.writer is not None:
                deps.append(b.writer)
            deps.extend(b.readers.values())
            deps.extend(b.dreaders)
        seen = set()
        for d in deps:
            if d is op or id(d) in seen:
                continue
            seen.add(id(d))
            if (not d.is_dma) and d.eng == eng:
                if eng == "pe":
                    continue
                if op.idx - d.idx >= self.same_eng_gap:
                    continue
            op.waits.append(d)
            d.signal = True
        for b in reads:
            if is_dma:
                b.dreaders.append(op)
            else:
                b.readers[eng] = op
        for b in writes:
            b.writer = op
            b.readers = {}
            b.dreaders = []
        if is_dma:
            slot = self.dma_rr[eng]
            self.dma_rr[eng] = (slot + 1) % self.n_dma_sems
            key = (eng, slot)
            val, prev = self.dma_state.get(key, (0, None))
            op.pre_dma_wait = prev
            val += 16
            op.dma_sem = key
            op.dma_val = val
            self.dma_state[key] = (val, op)
        lst.append(op)
        return op

    def op(self, eng, fn, reads=(), writes=()):
        return self._add(eng, fn, list(reads), list(writes), False)

    def dma(self, eng, fn, reads=(), writes=()):
        return self._add(eng, fn, list(reads), list(writes), True)

    def barrier(self):
        lasts = []
        for e in ENGS:
            for op in reversed(self.ops[e]):
                if (not op.is_dma) and op.fn is not None:
                    op.signal = True
                    lasts.append(op)
                    break
        dl = [op for (_, op) in self.dma_state.values()]
        for e in ENGS:
            op = Op(e, None, False)
            op.idx = len(self.ops[e])
            op.waits = list(lasts) + list(dl)
            self.ops[e].append(op)

    def emit(self, final_wait_ops=()):
        nc = self.nc
        for op in final_wait_ops:
            op.signal = True
        with contextlib.ExitStack() as st:
            esem = {e: st.enter_context(nc.semaphore("s_" + e)) for e in ENGS}
            dsem = {}
            for key in sorted(self.dma_state.keys()):
                dsem[key] = st.enter_context(nc.semaphore("d_%s%d" % key))
            for e in ENGS:
                c = 0
                for op in self.ops[e]:
                    if op.is_dma or op.fn is None:
                        continue
                    if op.signal:
                        c += 1
                        op.sig_count = c
            block = st.enter_context(nc.Block())
            engobj = {"pe": "tensor", "act": "scalar", "dve": "vector", "pool": "gpsimd", "sp": "sync"}

            def make(e):
                def body(eng):
                    seen = {}

                    def wait(semkey, sem, val):
                        if seen.get(semkey, 0) >= val:
                            return
                        seen[semkey] = val
                        eng.wait_ge(sem, val)

                    def wait_op(d):
                        if d.is_dma:
                            wait(d.dma_sem, dsem[d.dma_sem], d.dma_val)
                        else:
                            wait(d.eng, esem[d.eng], d.sig_count)

                    for op in self.ops[e]:
                        for d in op.waits:
                            wait_op(d)
                        if op.fn is None:
                            continue
                        if op.is_dma and op.pre_dma_wait is not None:
                            wait_op(op.pre_dma_wait)
                        ins = op.fn(eng)
                        if op.is_dma:
                            ins.then_inc(dsem[op.dma_sem], 16)
                        elif op.signal:
                            ins.then_inc(esem[e], 1)
                    if e == "sp":
                        for op in final_wait_ops:
                            wait_op(op)
                return body

            for e in ENGS:
                getattr(block, engobj[e])(make(e))


def build_nc():
    nc = bass.Bass("TRN2", target_bir_lowering=False)

    def din(name, shape):
        return nc.dram_tensor(name, list(shape), F32, kind="ExternalInput").ap()

    x_d = din("x", [T, D])
    h0_d = din("h0", [128, KT * NS])
    lc_d = din("lc", [128, KT * NS * 3])
    cc_d = din("cc", [128, KT * NS * 30])
    pp_d = din("pp", [128, NPP])
    gmix_d = din("gmixB", [128, D])
    gffn_d = din("gffnB", [128, D])
    gfin_d = din("gfinB", [128, D])
    w_in_d = din("w_in", [D, 6 * D])
    w_rgr_d = din("w_rg_r", [16, 64, 64])
    w_rgi_d = din("w_rg_i", [16, 64, 64])
    w_lruo_d = din("w_lru_o", [D, D])
    w_ccmo_d = din("w_ccm_o", [D, D])
    w_out_d = din("w_out", [D, D])
    w_gate_d = din("w_ffn_gate", [D, DFF])
    w_up_d = din("w_ffn_up", [D, DFF])
    w_down_d = din("w_ffn_down", [DFF, D])
    y_d = nc.dram_tensor("y", [T, D], F32, kind="ExternalOutput").ap()
    st_d = nc.dram_tensor("st", [128, KT * 5 * 34], F32, kind="ExternalOutput").ap()
    x2s_d = nc.dram_tensor("x2s", [T, D], F32).ap()

    w_in_v = w_in_d.rearrange("(kt p) (g c) -> p kt g c", p=128, c=128)
    w_lruo_v = w_lruo_d.rearrange("(kt p) (g c) -> p kt g c", p=128, c=128)
    w_ccmo_v = w_ccmo_d.rearrange("(kt p) (g c) -> p kt g c", p=128, c=128)
    w_out_v = w_out_d.rearrange("(kt p) c -> p kt c", p=128)
    w_gate_v = w_gate_d.rearrange("(kt p) (g c) -> p kt g c", p=128, c=128)
    w_up_v = w_up_d.rearrange("(kt p) (g c) -> p kt g c", p=128, c=128)
    w_down_v = w_down_d.rearrange("(ft p) c -> p ft c", p=128)

    with contextlib.ExitStack() as st:
        ARENA = 207616
        arena = st.enter_context(nc.sbuf_tensor("arena", [128, ARENA + 4608 + 512], U8))
        import os as _os
        PB = [st.enter_context(nc.psum_tensor("pb%d" % i, [128, 512], F32)) for i in range(8)]

        def V(off, dt, *shape):
            sz = 2 if dt == BF16 else 4
            n = 1
            for s in shape:
                n *= s
            v = arena[:, off:off + n * sz].bitcast(dt)
            if len(shape) == 2:
                return v.rearrange("p (a b) -> p a b", a=shape[0])
            if len(shape) == 3:
                return v.rearrange("p (a b c) -> p a b c", a=shape[0], b=shape[1])
            return v

        R1, DN, GL, Z = 0, 36864, 73728, 110592
        WS0, WS1, S = 157696, 165888, 174080
        P0 = 190464
        GBo = P0
        STo = GBo + 4096
        PPo = STo + 5440
        IDBo = PPo + 1536
        IDFo = IDBo + 256
        ONFo = IDFo + 512
        CSTo = ONFo + 512
        SSo = CSTo + 64
        CLo = SSo + 256
        H0o = CLo + 128
        LCo = H0o + 128
        CCo = LCo + 384
        assert CCo + 3840 <= ARENA

        hT = V(R1, BF16, KT, T)
        dn = V(DN, BF16, KT, T)
        glru = V(GL, BF16, KT, T)
        GB = V(GBo, F32, KT, 128)
        GBflat = V(GBo, F32, D)
        stage = V(STo, F32, KT, 5, 34)
        stage_flat = V(STo, F32, KT * 5 * 34)
        pp = V(PPo, F32, NPP)
        identb = V(IDBo, BF16, 128)
        identf = V(IDFo, F32, 128)
        onesf = V(ONFo, F32, 128)
        cst = V(CSTo, F32, 16)
        ss = V(SSo, F32, 3, 20)
        cl = V(CLo, F32, 4, 8)
        h0 = V(H0o, F32, KT, NS)
        lc = V(LCo, F32, KT, NS, 3)
        cc = V(CCo, F32, KT, NS, 30)
        WS = [V(WS0, BF16, KT, 4, 128), V(WS1, BF16, KT, 4, 128)]
        eps_ap = cst[:, 0:1]
        one_ap = cst[:, 1:2]

        def col(base, j):
            return pp[:, base + j:base + j + 1]

        S_ = Sched(nc)
        S_.name2reg = {
            "xaA": ["DN"], "jkA": ["DN"], "xsA": ["DN"], "hT": ["R1"], "h2T": ["R1"],
            "ws": ["WS"],
            "d31": ["GLa", "GLb"], "S1": ["GLb"], "S2": ["GLb"],
            "ub": ["S"], "ubpad": ["S"], "ubs": ["S"], "sig": ["S"], "dq": ["S"],
            "dz": ["DN"], "acc": ["Z"],
            "mu": ["Zs"], "rs": ["Zs"], "t1": ["Zs", "GLb"], "t2": ["Zs"], "dn": ["DN"],
            "d4": ["S"], "wg": ["S"], "xlb": ["S"], "xlbpad": ["S"], "xlbs": ["S"], "xcb": ["S"],
            ("X", 0): ["Z"], ("X", 1): ["Zx1"], ("mu", 1): ["Zx1"], ("rs", 1): ["Zx1"], "s1b": ["Zx1"], "R": ["Z"], "I": ["Z"], "A": ["Zs"], "hs": ["Z"], "glru": ["GLa", "GLb"],
            "sgl": ["Zs"], "sgc": ["Zs"], "m1": ["Zs"], "m2": ["Zs"], "mg": ["Z", "Zx1"],
            "wo": ["S"], "x2D": ["GLb"], "xaD": ["GLb"], "xsD": ["GLb"], "jkD": ["Zs"],
            "wd": ["DN", "GLa"], "ff": ["GLb", "Z", "Zx1"], "x2t": ["Z", "Zs"], "yt": ["Zs"],
            "sgf": ["S"], "jkF": ["S"], "sd": ["S"],
        }
        b = S_.b
        pbank = [0]
        for _i in range(8):
            b("pb", _i).excl = True

        def bank():
            i = pbank[0]
            pbank[0] = (i + 1) % 8
            return PB[i], b("pb", i)

        wslot = [0]

        def wsbuf():
            i = wslot[0]
            wslot[0] = (i + 1) % 2
            return WS[i], i

        finals = []

        S_.dma("sp", lambda e: e.dma_start(out=pp, in_=pp_d), writes=[b("pp")])
        NXA = 6
        XA = [V(DN + k * 4096, F32, D) for k in range(NXA)]

        def load_xA(i):
            xa = XA[i % NXA]
            S_.dma("sp", lambda e: e.dma_start(out=xa, in_=x_d[i * 128:(i + 1) * 128, :]), writes=[b("xaA", i % NXA)])

        for i in range(NXA):
            load_xA(i)
        S_.dma("sp", lambda e: e.dma_start(out=GBflat, in_=gmix_d), writes=[b("GB")])
        S_.dma("sp", lambda e: e.dma_start(out=V(H0o, F32, KT * NS), in_=h0_d), writes=[b("h0")])
        S_.dma("sp", lambda e: e.dma_start(out=V(LCo, F32, KT * NS * 3), in_=lc_d), writes=[b("lc")])
        S_.dma("sp", lambda e: e.dma_start(out=V(CCo, F32, KT * NS * 30), in_=cc_d), writes=[b("cc")])
        S_.op("pool", lambda e: e.memset(identf, 0.0), writes=[b("identf")])
        S_.op("pool", lambda e: e.affine_select(out=identf, in_=identf, pattern=[[-1, 128]],
                                                compare_op=ALU.not_equal, fill=1.0, base=0,
                                                channel_multiplier=1),
              reads=[b("identf")], writes=[b("identf")])
        S_.op("pool", lambda e: e.memset(onesf, 1.0 / D), writes=[b("onesf")])
        S_.op("pool", lambda e: e.memset(cst[:, 0:1], EPS), writes=[b("cst")])
        S_.op("pool", lambda e: e.memset(cst[:, 1:2], 1.0), writes=[b("cst")])
        S_.op("pool", lambda e: e.memset(V(SSo, F32, 64), 0.0), writes=[b("ss")])
        S_.op("dve", lambda e: e.tensor_copy(out=identb, in_=identf), reads=[b("identf")], writes=[b("identb")])
        S_.op("act", lambda e: e.activation(out=cl[:, 2, :], in_=pp[:, PP_LAM:PP_LAM + 8], func=AF.Exp, scale=-1.0),
              reads=[b("pp")], writes=[b("cl")])
        S_.op("act", lambda e: e.activation(out=cl[:, 3, :], in_=cl[:, 2, :], func=AF.Ln, bias=one_ap),
              reads=[b("cl"), b("cst")], writes=[b("cl")])
        S_.op("dve", lambda e: e.tensor_scalar(out=cl[:, 0, :], in0=cl[:, 3, :], scalar1=-8.0, scalar2=None, op0=ALU.mult),
              reads=[b("cl")], writes=[b("cl")])
        S_.op("dve", lambda e: e.tensor_scalar(out=cl[:, 1, :], in0=cl[:, 3, :], scalar1=-16.0, scalar2=None, op0=ALU.mult),
              reads=[b("cl")], writes=[b("cl")])

        def wload(ws, wsb, g, src):
            S_.dma("pool", lambda e: e.dma_start(out=ws[:, :, g, :], in_=src), writes=[b("ws", wsb, g)])

        def mm8(pb, pbb, n, ws, wsb, g, rhs3, rbufs, c0):
            for kt in range(KT):
                S_.op("pe", lambda e, kt=kt: e.matmul(pb[:, 0:n], lhsT=ws[:, kt, g, :], rhs=rhs3[:, kt, c0:c0 + n],
                                                      start=(kt == 0), stop=(kt == KT - 1)),
                      reads=[b("ws", wsb, g)] + rbufs, writes=[pbb])

        def hT_bufs(name, c0, n):
            return [b(name, i) for i in range(c0 // 128, (c0 + n) // 128)]

        ws, wsb = wsbuf()
        wload(ws, wsb, 0, w_in_v[:, :, 16, :])
        wload(ws, wsb, 1, w_in_v[:, :, 24, :])

        def norm_stage1(tiles, row, src_fn, jk, jkname):
            for i in tiles:
                src_ap, src_bufs = src_fn(i)
                S_.op("act", lambda e, src_ap=src_ap, i=i: e.activation(out=jk, in_=src_ap, func=AF.Square,
                                                                       accum_out=ss[:, row, i:i + 1]),
                      reads=list(src_bufs) + [b("ss")], writes=[b(jkname), b("ssc", row, i)])
            lo, hi = tiles[0], tiles[-1] + 1
            S_.op("act", lambda e: e.activation(out=ss[:, row, lo:hi], in_=ss[:, row, lo:hi], func=AF.Sqrt,
                                                bias=eps_ap, scale=1.0 / D),
                  reads=[b("ssc", row, i) for i in tiles] + [b("cst")], writes=[b("rstd", row, i) for i in tiles])
            S_.op("dve", lambda e: e.reciprocal(out=ss[:, row, lo:hi], in_=ss[:, row, lo:hi]),
                  reads=[b("rstd", row, i) for i in tiles], writes=[b("rstd", row, i) for i in tiles])

        def norm_scale(i, row, src_ap, src_bufs, xs, xs_buf, eng="act"):
            if eng == "act":
                S_.op("act", lambda e: e.activation(out=xs, in_=src_ap, func=AF.Identity, scale=ss[:, row, i:i + 1]),
                      reads=list(src_bufs) + [b("rstd", row, i)], writes=[xs_buf])
            else:
                S_.op("dve", lambda e: e.tensor_scalar(out=xs, in0=src_ap, scalar1=ss[:, row, i:i + 1], scalar2=None,
                                                       op0=ALU.mult),
                      reads=list(src_bufs) + [b("rstd", row, i)], writes=[xs_buf])

        def norm_transpose(i, xs, xs_buf):
            pb, pbb = bank()
            pv = pb[:].bitcast(BF16)
            for kt in range(KT):
                S_.op("pe", lambda e, kt=kt: e.transpose(pv[:, kt * 128:(kt + 1) * 128], xs[:, kt * 128:(kt + 1) * 128], identb),
                      reads=[xs_buf, b("identb")], writes=[pbb])
            return pv.rearrange("p (a n) -> p a n", a=KT), pbb

        def norm_evac(i, pv3, pbb, dstT, dst_name):
            S_.op("dve", lambda e: e.tensor_tensor(out=dstT[:, :, i * 128:(i + 1) * 128], in0=pv3, in1=GB, op=ALU.mult),
                  reads=[pbb, b("GB")], writes=[b(dst_name, i)])

        JK_A = V(DN + 24576, BF16, D)
        XSA = [V(DN + 26624 + k * 2048, BF16, D) for k in range(3)]

        def phase_A_chunk(ci):
            c0, n = CH[ci]
            tiles = list(range(c0 // 128, (c0 + n) // 128))
            for i in tiles:
                norm_stage1([i], 0, lambda i_: (XA[i_ % NXA], [b("xaA", i_ % NXA)]), JK_A, "jkA")
            pend = None
            for i in tiles:
                xs, xsb = XSA[i % 3], b("xsA", i % 3)
                norm_scale(i, 0, XA[i % NXA], [b("xaA", i % NXA)], xs, xsb, eng="dve")
                if i + NXA < NT:
                    load_xA(i + NXA)
                if pend is not None:
                    norm_evac(*pend)
                pv3, pbb = norm_transpose(i, xs, xsb)
                pend = (i, pv3, pbb, hT, "hT")
            norm_evac(*pend)

        D31 = [V(GL + k * 7936, BF16, 31, 128) for k in range(2)]

        _ND = int(_os.environ.get("K_ND", "5"))

        def npe_of(j):
            return 31 if j == KT - 1 else 31 - _ND

        def build_d31(j):
            d31_ = D31[j % 2]
            for k in range(npe_of(j)):
                S_.op("act", lambda e, k=k, d31_=d31_, j=j: e.activation(out=d31_[:, k, :], in_=identf, func=AF.Identity,
                                                                        scale=col(PP_WDW, j * 31 + k)),
                      reads=[b("identf"), b("pp")], writes=[b("d31", j % 2, k)])

        build_d31(0)
        phase_A_chunk(0)

        dz = V(DN, BF16, KT, T)
        S1 = V(GL + 15872, F32, T)
        S2 = V(GL + 25088, F32, T)
        ub = V(S, BF16, 2454)
        ub_s = ub[:, 2078:2454].rearrange("p (s n) -> p s n", s=NS)
        SIG = [V(S + 4912 + k * 2048, F32, 512) for k in range(2)]
        DQ = [V(S + 9008 + k * 2048, F32, 512) for k in range(2)]
        MU = [V(Z + 36864, F32, 512)]
        RS = [V(Z + 38912, F32, 512)]
        T1 = [V(Z + 40960, F32, 512), V(GL + 34304, F32, 512)]
        T2 = [V(Z + 43008 + k * 2048, F32, 512) for k in range(2)]

        MU = [MU[0], V(Z + 9216, F32, 512)]
        RS = [RS[0], V(Z + 11264, F32, 512)]


        def ln_stats(ci):
            c0, n = CH[ci]
            p1, p1b = bank()
            p2, p2b = bank()
            S_.op("pe", lambda e: e.matmul(p1[:, 0:n], lhsT=onesf, rhs=S1[:, c0:c0 + n], start=True, stop=True),
                  reads=[b("onesf"), b("S1", ci)], writes=[p1b])
            S_.op("pe", lambda e: e.matmul(p2[:, 0:n], lhsT=onesf, rhs=S2[:, c0:c0 + n], start=True, stop=True),
                  reads=[b("onesf"), b("S2", ci)], writes=[p2b])
            mu, mub = MU[ci % 2], b("mu", ci % 2)
            rs, rsb = RS[ci % 2], b("rs", ci % 2)
            S_.op("dve", lambda e: e.tensor_copy(out=mu[:, 0:n], in_=p1[:, 0:n]),
                  reads=[p1b], writes=[mub])
            S_.op("dve", lambda e: e.tensor_tensor(out=rs[:, 0:n], in0=mu[:, 0:n], in1=mu[:, 0:n], op=ALU.mult),
                  reads=[mub], writes=[rsb])
            S_.op("dve", lambda e: e.tensor_tensor(out=rs[:, 0:n], in0=p2[:, 0:n], in1=rs[:, 0:n], op=ALU.subtract),
                  reads=[p2b, rsb], writes=[rsb])
            S_.op("dve", lambda e: e.tensor_scalar(out=rs[:, 0:n], in0=rs[:, 0:n], scalar1=0.0, scalar2=EPS, op0=ALU.max, op1=ALU.add),
                  reads=[rsb], writes=[rsb])
            S_.op("act", lambda e: e.activation(out=rs[:, 0:n], in_=rs[:, 0:n], func=AF.Ln),
                  reads=[rsb], writes=[rsb])
            S_.op("act", lambda e: e.activation(out=rs[:, 0:n], in_=rs[:, 0:n], func=AF.Exp, scale=-0.5),
                  reads=[rsb], writes=[rsb])

        NEGO = V(ARENA + 4608, BF16, 128)
        S_.op("pool", lambda e: e.memset(NEGO, -1.0 / D), writes=[b("nego")])
        S1B = [V(Z + 13312 + k * 1024, BF16, 512) for k in range(2)]

        def ln_norm(ci):
            c0, n = CH[ci]
            rs, rsb = RS[ci % 2], b("rs", ci % 2)
            s1b, s1bb = S1B[ci % 2], b("s1b", ci % 2)
            S_.op("dve", lambda e: e.tensor_copy(out=s1b[:, 0:n], in_=S1[:, c0:c0 + n]),
                  reads=[b("S1", ci)], writes=[s1bb])
            for j2 in range(KT):
                t2, t2b = T2[j2 % 2], b("t2", j2 % 2)
                pc_, pcb_ = bank()
                S_.op("pe", lambda e, pc_=pc_, j2=j2: e.matmul(pc_[:, 0:n], lhsT=identb, rhs=dz[:, j2, c0:c0 + n],
                                                              start=True, stop=False),
                      reads=[b("identb"), b("dz", j2, ci)], writes=[pcb_])
                S_.op("pe", lambda e, pc_=pc_: e.matmul(pc_[:, 0:n], lhsT=NEGO, rhs=s1b[:, 0:n],
                                                       start=False, stop=True),
                      reads=[b("nego"), s1bb], writes=[pcb_])
                S_.op("dve", lambda e, t2=t2, pc_=pc_: e.tensor_tensor(
                    out=t2[:, 0:n], in0=pc_[:, 0:n], in1=rs[:, 0:n], op=ALU.mult),
                    reads=[pcb_, rsb], writes=[t2b])
                S_.op("act", lambda e, t2=t2, j2=j2: e.activation(
                    out=dn[:, j2, c0:c0 + n], in_=t2[:, 0:n], func=AF.Silu, bias=col(PP_BLN, j2), scale=col(PP_GLN, j2)),
                    reads=[t2b, b("pp")], writes=[b("dn", j2, ci)])

        S_.op("pool", lambda e: e.memset(ub[:, 0:30], 0.0), writes=[b("ubpad")])
        ACC = [V(Z + k * 2048, F32, 512) for k in range(2)]
        for j in range(KT):
            if j + 1 < KT:
                ws_n, wsb_n = wsbuf()
                wload(ws_n, wsb_n, 0, w_in_v[:, :, 16 + j + 1, :])
                wload(ws_n, wsb_n, 1, w_in_v[:, :, 24 + j + 1, :])
            d31 = D31[j % 2]
            S_.op("dve", lambda e, j=j: e.tensor_copy(out=ub_s[:, :, 0:30], in_=cc[:, j, :, :]),
                  reads=[b("cc")], writes=[b("ubs")])
            for ci, (c0, n) in enumerate(CH):
                if j == 0 and ci + 1 < len(CH):
                    phase_A_chunk(ci + 1)
                pa, pab = bank()
                pc, pcb = bank()
                mm8(pa, pab, n, ws, wsb, 0, hT, hT_bufs("hT", c0, n), c0)
                mm8(pc, pcb, n, ws, wsb, 1, hT, hT_bufs("hT", c0, n), c0)
                sg, sgb = SIG[ci % 2], b("sig", ci % 2)
                S_.op("act", lambda e, pc=pc, n=n, sg=sg, j=j: e.activation(out=sg[:, 0:n], in_=pc[:, 0:n], func=AF.Sigmoid,
                                                                           bias=col(PP_BIN, 24 + j)),
                      reads=[pcb, b("pp")], writes=[sgb])
                bca = col(PP_BIN, 16 + j)
                if ci < 4:
                    S_.op("dve", lambda e, pa=pa, sg=sg, c0=c0, bca=bca: e.scalar_tensor_tensor(
                        out=ub[:, 30 + c0:30 + c0 + 512], in0=pa[:, 0:512], scalar=bca, in1=sg[:, 0:512],
                        op0=ALU.add, op1=ALU.mult),
                        reads=[pab, sgb, b("pp")], writes=[b("ub", ci)])
                    if ci == 3:
                        S_.op("dve", lambda e, pa=pa, sg=sg, bca=bca, j=j: e.scalar_tensor_tensor(
                            out=stage[:, j, 0, 4:34], in0=pa[:, 482:512], scalar=bca, in1=sg[:, 482:512],
                            op0=ALU.add, op1=ALU.mult),
                            reads=[pab, sgb, b("pp")], writes=[b("stage")])
                else:
                    pa3 = pa[:, 0:256].rearrange("p (s n) -> p s n", s=NS)
                    sg3 = sg[:, 0:256].rearrange("p (s n) -> p s n", s=NS)
                    S_.op("dve", lambda e, pa3=pa3, sg3=sg3, bca=bca: e.scalar_tensor_tensor(
                        out=ub_s[:, :, 30:94], in0=pa3, scalar=bca, in1=sg3, op0=ALU.add, op1=ALU.mult),
                        reads=[pab, sgb, b("pp")], writes=[b("ub", ci)])
                    S_.op("dve", lambda e, pa3=pa3, sg3=sg3, bca=bca, j=j: e.scalar_tensor_tensor(
                        out=stage[:, j, 1:5, 4:34], in0=pa3[:, :, 34:64], scalar=bca, in1=sg3[:, :, 34:64],
                        op0=ALU.add, op1=ALU.mult),
                        reads=[pab, sgb, b("pp")], writes=[b("stage")])
            if j == 0:
                S_.retire("DN")
            if j + 1 < KT:
                build_d31(j + 1)
            NPE = npe_of(j)
            for ci, (c0, n) in enumerate(CH):
                pd, pdb = bank()
                acc, accb = ACC[ci % 2], b("acc", ci % 2)
                if ci < 4:
                    rb = [b("ub", ci), b("ubpad")] + ([b("ub", ci - 1)] if ci > 0 else [])
                else:
                    rb = [b("ub", 4), b("ubs")]
                for k in range(NPE):
                    if ci < 4:
                        rhs = ub[:, c0 + k:c0 + k + 512]
                    else:
                        rhs = ub_s[:, :, k:k + 64]
                    pdo = pd[:, 0:n] if ci < 4 else pd[:, 0:256].rearrange("p (s n) -> p s n", s=NS)
                    S_.op("pe", lambda e, pdo=pdo, d31=d31, k=k, rhs=rhs, last=(k == NPE - 1): e.matmul(
                        pdo, lhsT=d31[:, k, :], rhs=rhs, start=(k == 0), stop=last),
                        reads=[b("d31", j % 2, k)] + rb, writes=[pdb])
                acco = acc[:, 0:n] if ci < 4 else acc[:, 0:256].rearrange("p (s n) -> p s n", s=NS)
                for k in range(NPE, 31):
                    src = ub[:, c0 + k:c0 + k + 512] if ci < 4 else ub_s[:, :, k:k + 64]
                    wk = col(PP_WDW, j * 31 + k)
                    if k == NPE:
                        S_.op("dve", lambda e, acco=acco, src=src, wk=wk: e.tensor_scalar(
                            out=acco, in0=src, scalar1=wk, scalar2=None, op0=ALU.mult),
                            reads=rb + [b("pp")], writes=[accb])
                    else:
                        S_.op("dve", lambda e, acco=acco, src=src, wk=wk: e.scalar_tensor_tensor(
                            out=acco, in0=src, scalar=wk, in1=acco, op0=ALU.mult, op1=ALU.add),
                            reads=rb + [b("pp"), accb], writes=[accb])
                bdw = col(PP_BDW, j)
                if NPE < 31:
                    S_.op("dve", lambda e, pd=pd, n=n, acc=acc, bdw=bdw: e.scalar_tensor_tensor(
                        out=acc[:, 0:n], in0=pd[:, 0:n], scalar=bdw, in1=acc[:, 0:n], op0=ALU.add, op1=ALU.add),
                        reads=[pdb, b("pp"), accb], writes=[accb])
                else:
                    S_.op("act", lambda e, pd=pd, n=n, acc=acc, bdw=bdw: e.activation(
                        out=acc[:, 0:n], in_=pd[:, 0:n], func=AF.Identity, bias=bdw),
                        reads=[pdb, b("pp")], writes=[accb])
                S_.op("act", lambda e, acc=acc, n=n, c0=c0, j=j: e.activation(
                    out=dz[:, j, c0:c0 + n], in_=acc[:, 0:n], func=AF.Identity),
                    reads=[accb], writes=[b("dz", j, ci)])
                dq, dqb = DQ[ci % 2], b("dq", ci % 2)
                S_.op("act", lambda e, acc=acc, n=n, dq=dq: e.activation(
                    out=dq[:, 0:n], in_=acc[:, 0:n], func=AF.Square),
                    reads=[accb], writes=[dqb])
                if j == 0:
                    S_.op("dve", lambda e, acc=acc, n=n, c0=c0: e.tensor_copy(out=S1[:, c0:c0 + n], in_=acc[:, 0:n]),
                          reads=[accb], writes=[b("S1", ci)])
                    S_.op("dve", lambda e, n=n, c0=c0, dq=dq: e.tensor_copy(out=S2[:, c0:c0 + n], in_=dq[:, 0:n]),
                          reads=[dqb], writes=[b("S2", ci)])
                else:
                    S_.op("dve", lambda e, acc=acc, n=n, c0=c0: e.tensor_tensor(
                        out=S1[:, c0:c0 + n], in0=S1[:, c0:c0 + n], in1=acc[:, 0:n], op=ALU.add),
                        reads=[accb, b("S1", ci)], writes=[b("S1", ci)])
                    S_.op("dve", lambda e, n=n, c0=c0, dq=dq: e.tensor_tensor(
                        out=S2[:, c0:c0 + n], in0=S2[:, c0:c0 + n], in1=dq[:, 0:n], op=ALU.add),
                        reads=[dqb, b("S2", ci)], writes=[b("S2", ci)])
                if j == KT - 1:
                    ln_stats(ci)
                    if ci >= 1:
                        ln_norm(ci - 1)
            if j == KT - 1:
                ln_norm(len(CH) - 1)
            if j + 1 < KT:
                ws, wsb = ws_n, wsb_n
        ws, wsb = wsbuf()
        wload(ws, wsb, 0, w_in_v[:, :, 0, :])
        wload(ws, wsb, 1, w_in_v[:, :, 8, :])
        S_.retire("S")
        S_.retire("Z")
        S_.retire("Zx1")
        S_.retire("Zs")
        S_.retire("GLa")
        S_.retire("GLb")

        Gb = V(ARENA, BF16, T)
        XX = [V(Z, F32, T), V(Z + 9216, F32, T)]
        R = V(Z + 18432, F32, T)
        I = V(Z + 27648, F32, T)
        A = V(Z + 36864, F32, T)
        D4 = [V(S + k * 1024, BF16, 4, 128) for k in range(2)]
        WG = V(S + 2048, BF16, KT, 2, 128)
        xlb = V(S + 6144, BF16, 2320)
        xlb_s = xlb[:, 2051:2319].rearrange("p (s n) -> p s n", s=NS)
        xcb = V(S + 10784, BF16, T)
        TAIL_ENG = "dve" if _os.environ.get("K_NOPOOL") else "pool"

        S_.op("pool", lambda e: e.memset(V(S + 2048, BF16, KT * 2 * 128), 0.0),
              writes=[b("wg", g_, q_) for g_ in range(2) for q_ in range(2)])
        S_.op("pool", lambda e: e.memset(xlb[:, 0:3], 0.0), writes=[b("xlbpad")])
        for g, wsrc in enumerate((w_rgr_d, w_rgi_d)):
            wv = wsrc.rearrange("(t q) d e -> q d t e", q=2)
            for q in range(2):
                S_.dma("pool", lambda e, g=g, q=q, wv=wv: e.dma_start(
                    out=WG[64 * q:64 * q + 64, :, g, 64 * q:64 * q + 64], in_=wv[q]),
                    reads=[], writes=[b("wg", g, q)])

        def build_d4(j):
            d4 = D4[j % 2]
            for k in range(4):
                S_.op("act", lambda e, k=k, d4=d4, j=j: e.activation(out=d4[:, k, :], in_=identf, func=AF.Identity,
                                                                    scale=col(PP_WLC, j * 4 + k)),
                      reads=[b("identf"), b("pp")], writes=[b("d4", j % 2, k)])

        def emit_exp_min(jq, ci):
            c0, n = CH[ci]
            S_.op("act", lambda e: e.activation(
                out=A[:, c0:c0 + n], in_=R[:, c0:c0 + n], func=AF.Exp, scale=cl[:, 0, jq:jq + 1]),
                reads=[b("R", ci), b("cl")], writes=[b("A", ci)])
            S_.op("act", lambda e: e.activation(
                out=R[:, c0:c0 + n], in_=R[:, c0:c0 + n], func=AF.Exp, scale=cl[:, 1, jq:jq + 1]),
                reads=[b("R", ci), b("cl")], writes=[b("R", ci)])
            S_.op("dve", lambda e: e.tensor_scalar(
                out=R[:, c0:c0 + n], in0=R[:, c0:c0 + n], scalar1=1.0, scalar2=-1.0, op0=ALU.min, op1=ALU.mult),
                reads=[b("R", ci)], writes=[b("R", ci)])

        def emit_sqrt(jq):
            for ci, (c0, n) in enumerate(CH):
                S_.op("act", lambda e, n=n, c0=c0: e.activation(
                    out=R[:, c0:c0 + n], in_=R[:, c0:c0 + n], func=AF.Sqrt, bias=one_ap),
                    reads=[b("R", ci), b("cst")], writes=[b("R", ci)])

        def g_gl(jq):
            return 1 if (jq // 2) % 2 == 0 else 3

        def emit_gl(jq):
            ws_q, wsb_q = lru_ws[jq]
            for ci, (c0, n) in enumerate(CH):
                pg, pgb = bank()
                mm8(pg, pgb, n, ws_q, wsb_q, g_gl(jq), hT, hT_bufs("hT", c0, n), c0)
                S_.op("act", lambda e, pg=pg, n=n, c0=c0: e.activation(
                    out=Gb[:, c0:c0 + n], in_=pg[:, 0:n], func=AF.Gelu_apprx_tanh, bias=col(PP_BIN, 8 + jq)),
                    reads=[pgb, b("pp")], writes=[b("G", ci)])

        build_d4(0)
        lru_ws = {0: (ws, wsb)}
        for j in range(KT + 1):
            jp = j - 1
            if j < KT:
                ws, wsb = lru_ws[j]
                if j + 1 < KT:
                    ws_n, wsb_n = wsbuf()
                    wload(ws_n, wsb_n, 0, w_in_v[:, :, j + 1, :])
                    wload(ws_n, wsb_n, g_gl(j + 1), w_in_v[:, :, 8 + j + 1, :])
                    lru_ws[j + 1] = (ws_n, wsb_n)
                    build_d4(j + 1)
                X = XX[j % 2]
                d4 = D4[j % 2]
                S_.op("dve", lambda e, j=j: e.tensor_copy(out=xlb_s[:, :, 0:3], in_=lc[:, j, :, :]),
                      reads=[b("lc")], writes=[b("xlbs")])
                for ci, (c0, n) in enumerate(CH):
                    if jp >= 0:
                        emit_exp_min(jp, ci)
                    px, pxb = bank()
                    mm8(px, pxb, n, ws, wsb, 0, hT, hT_bufs("hT", c0, n), c0)
                    bxl = col(PP_BIN, j)
                    if ci < 4:
                        S_.op("dve", lambda e, px=px, c0=c0, bxl=bxl: e.tensor_scalar(
                            out=xlb[:, 3 + c0:3 + c0 + 512], in0=px[:, 0:512], scalar1=bxl, scalar2=None, op0=ALU.add),
                            reads=[pxb, b("pp")], writes=[b("xlb", ci)])
                        if ci == 3:
                            S_.op("dve", lambda e, px=px, bxl=bxl, j=j: e.tensor_scalar(
                                out=stage[:, j, 0, 1:4], in0=px[:, 509:512], scalar1=bxl, scalar2=None, op0=ALU.add),
                                reads=[pxb, b("pp")], writes=[b("stage")])
                    else:
                        px3 = px[:, 0:256].rearrange("p (s n) -> p s n", s=NS)
                        S_.op("dve", lambda e, px3=px3, bxl=bxl: e.tensor_scalar(
                            out=xlb_s[:, :, 3:67], in0=px3, scalar1=bxl, scalar2=None, op0=ALU.add),
                            reads=[pxb, b("pp")], writes=[b("xlb", ci)])
                        S_.op("dve", lambda e, px3=px3, bxl=bxl, j=j: e.tensor_scalar(
                            out=stage[:, j, 1:5, 1:4], in0=px3[:, :, 61:64], scalar1=bxl, scalar2=None, op0=ALU.add),
                            reads=[pxb, b("pp")], writes=[b("stage")])
            if jp >= 0:
                if j == KT:
                    for ci in range(len(CH)):
                        emit_exp_min(jp, ci)
                emit_sqrt(jp)
                emit_gl(jp)
                Xp = XX[jp % 2]
                for ci, (c0, n) in enumerate(CH):
                    S_.op("dve", lambda e, n=n, c0=c0: e.tensor_tensor(
                        out=I[:, c0:c0 + n], in0=I[:, c0:c0 + n], in1=R[:, c0:c0 + n], op=ALU.mult),
                        reads=[b("I", ci), b("R", ci)], writes=[b("I", ci)])
                    S_.op(TAIL_ENG, lambda e, n=n, c0=c0, Xp=Xp: e.tensor_tensor(
                        out=Xp[:, c0:c0 + n], in0=Xp[:, c0:c0 + n], in1=I[:, c0:c0 + n], op=ALU.mult),
                        reads=[b("I", ci), b("X", jp % 2, ci)], writes=[b("X", jp % 2, ci)])
            if j < KT:
                for ci, (c0, n) in enumerate(CH):
                    pc, pcb = bank()
                    if ci < 4:
                        rb = [b("xlb", ci), b("xlbpad")] + ([b("xlb", ci - 1)] if ci > 0 else [])
                    else:
                        rb = [b("xlb", 4), b("xlbs")]
                    for k in range(4):
                        rhs = xlb[:, c0 + k:c0 + k + 512] if ci < 4 else xlb_s[:, :, k:k + 64]
                        pco_ = pc[:, 0:n] if ci < 4 else pc[:, 0:256].rearrange("p (s n) -> p s n", s=NS)
                        S_.op("pe", lambda e, pco_=pco_, d4=d4, k=k, rhs=rhs: e.matmul(
                            pco_, lhsT=d4[:, k, :], rhs=rhs, start=(k == 0), stop=(k == 3)),
                            reads=[b("d4", j % 2, k)] + rb, writes=[pcb])
                    S_.op("dve", lambda e, pc=pc, n=n, c0=c0, j=j: e.tensor_scalar(
                        out=xcb[:, c0:c0 + n], in0=pc[:, 0:n], scalar1=col(PP_BLC, j), scalar2=None, op0=ALU.add),
                        reads=[pcb, b("pp")], writes=[b("xcb", ci)])
                    S_.op("dve", lambda e, pc=pc, n=n, c0=c0, j=j, X=X: e.tensor_scalar(
                        out=X[:, c0:c0 + n], in0=pc[:, 0:n], scalar1=col(PP_BLC, j), scalar2=None, op0=ALU.add),
                        reads=[pcb, b("pp")], writes=[b("X", j % 2, ci)])
            if jp >= 0:
                Xp = XX[jp % 2]
                for ci in range(4):
                    c0 = ci * 512
                    init = 0.0 if ci == 0 else Xp[:, c0 - 1:c0]
                    S_.op("dve", lambda e, c0=c0, init=init, Xp=Xp: e.tensor_tensor_scan(
                        out=Xp[:, c0:c0 + 512], data0=A[:, c0:c0 + 512], data1=Xp[:, c0:c0 + 512], initial=init,
                        op0=ALU.mult, op1=ALU.add),
                        reads=[b("A", ci), b("X", jp % 2, ci)] + ([b("X", jp % 2, ci - 1)] if ci > 0 else []),
                        writes=[b("X", jp % 2, ci)])
                for s in range(NS):
                    c0 = TP + s * TS
                    S_.op("dve", lambda e, c0=c0, s=s, jp=jp, Xp=Xp: e.tensor_tensor_scan(
                        out=Xp[:, c0:c0 + TS], data0=A[:, c0:c0 + TS], data1=Xp[:, c0:c0 + TS], initial=h0[:, jp, s:s + 1],
                        op0=ALU.mult, op1=ALU.add),
                        reads=[b("A", 4), b("X", jp % 2, 4), b("h0")], writes=[b("X", jp % 2, 4)])
                S_.op("dve", lambda e, jp=jp, Xp=Xp: e.tensor_copy(out=stage[:, jp, 0, 0:1], in_=Xp[:, TP - 1:TP]),
                      reads=[b("X", jp % 2, 3)], writes=[b("stage")])
                X3 = Xp[:, TP:T].rearrange("p (s n) -> p s n", s=NS)
                S_.op("dve", lambda e, jp=jp, X3=X3: e.tensor_copy(out=stage[:, jp, 1:5, 0:1], in_=X3[:, :, TS - 1:TS]),
                      reads=[b("X", jp % 2, 4)], writes=[b("stage")])
                for ci, (c0, n) in enumerate(CH):
                    S_.op(TAIL_ENG, lambda e, n=n, c0=c0, jp=jp, Xp=Xp: e.tensor_tensor(
                        out=glru[:, jp, c0:c0 + n], in0=Gb[:, c0:c0 + n], in1=Xp[:, c0:c0 + n], op=ALU.mult),
                        reads=[b("G", ci), b("X", jp % 2, ci)], writes=[b("glru", jp, ci)])
            if j < KT:
                for ci, (c0, n) in enumerate(CH):
                    pr, prb = bank()
                    pi, pib = bank()
                    S_.op("pe", lambda e, pr=pr, n=n, c0=c0, j=j: e.matmul(pr[:, 0:n], lhsT=WG[:, j, 0, :], rhs=xcb[:, c0:c0 + n],
                                                                            start=True, stop=True),
                          reads=[b("wg", 0, 0), b("wg", 0, 1), b("xcb", ci)], writes=[prb])
                    S_.op("pe", lambda e, pi=pi, n=n, c0=c0, j=j: e.matmul(pi[:, 0:n], lhsT=WG[:, j, 1, :], rhs=xcb[:, c0:c0 + n],
                                                                            start=True, stop=True),
                          reads=[b("wg", 1, 0), b("wg", 1, 1), b("xcb", ci)], writes=[pib])
                    S_.op("act", lambda e, pr=pr, n=n, c0=c0, j=j: e.activation(
                        out=R[:, c0:c0 + n], in_=pr[:, 0:n], func=AF.Sigmoid, bias=col(PP_BR, j)),
                        reads=[prb, b("pp")], writes=[b("R", ci)])
                    S_.op("act", lambda e, pi=pi, n=n, c0=c0, j=j: e.activation(
                        out=I[:, c0:c0 + n], in_=pi[:, 0:n], func=AF.Sigmoid, bias=col(PP_BI, j)),
                        reads=[pib, b("pp")], writes=[b("I", ci)])
        finals.append(S_.dma("sp", lambda e: e.dma_start(out=st_d, in_=stage_flat), reads=[b("stage")]))

        def load_merge_w(e_):
            ws_, wsb_ = wsbuf()
            wload(ws_, wsb_, 0, w_in_v[:, :, 32 + e_, :])
            wload(ws_, wsb_, 1, w_in_v[:, :, 40 + e_, :])
            wload(ws_, wsb_, 2, w_lruo_v[:, :, e_, :])
            wload(ws_, wsb_, 3, w_ccmo_v[:, :, e_, :])
            return ws_, wsb_

        ws, wsb = load_merge_w(0)
        S_.retire("S")
        S_.retire("Z")
        S_.retire("Zx1")
        S_.retire("Zs")

        merged = V(Z, BF16, KT, T)
        SGL = [V(Z + 36864, F32, 512)] * 2
        SGC = [V(Z + 38912, F32, 512)] * 2
        M1 = [V(Z + 40960, F32, 512)] * 2
        M2 = [V(Z + 43008, F32, 512)] * 2
        WO = V(S, BF16, KT, D)
        S_.dma("pool", lambda e: e.dma_start(out=WO, in_=w_out_v), writes=[b("wo")])
        it = 0
        for e_ in range(KT):
            if e_ + 1 < KT:
                ws_n, wsb_n = load_merge_w(e_ + 1)
            for ci, (c0, n) in enumerate(CH):
                psl, pslb = bank()
                psc, pscb = bank()
                plo, plob = bank()
                pco, pcob = bank()
                mm8(psl, pslb, n, ws, wsb, 0, hT, hT_bufs("hT", c0, n), c0)
                mm8(psc, pscb, n, ws, wsb, 1, hT, hT_bufs("hT", c0, n), c0)
                mm8(plo, plob, n, ws, wsb, 2, glru, [b("glru", kt, ci) for kt in range(KT)], c0)
                mm8(pco, pcob, n, ws, wsb, 3, dn, [b("dn", kt, ci) for kt in range(KT)], c0)
                k2 = it % 2
                it += 1
                sgl, sglb = SGL[k2], b("sgl", 0)
                sgc, sgcb = SGC[k2], b("sgc", 0)
                m1, m1b = M1[k2], b("m1", 0)
                m2, m2b = M2[k2], b("m2", 0)
                S_.op("act", lambda e, psl=psl, n=n, sgl=sgl, e_=e_: e.activation(
                    out=sgl[:, 0:n], in_=psl[:, 0:n], func=AF.Sigmoid, bias=col(PP_BIN, 32 + e_)),
                    reads=[pslb, b("pp")], writes=[sglb])
                S_.op("act", lambda e, psc=psc, n=n, sgc=sgc, e_=e_: e.activation(
                    out=sgc[:, 0:n], in_=psc[:, 0:n], func=AF.Sigmoid, bias=col(PP_BIN, 40 + e_)),
                    reads=[pscb, b("pp")], writes=[sgcb])
                S_.op("dve", lambda e, plo=plo, n=n, sgl=sgl, m1=m1: e.tensor_tensor(
                    out=m1[:, 0:n], in0=plo[:, 0:n], in1=sgl[:, 0:n], op=ALU.mult),
                    reads=[plob, sglb], writes=[m1b])
                S_.op("dve", lambda e, pco=pco, n=n, sgc=sgc, m2=m2: e.tensor_tensor(
                    out=m2[:, 0:n], in0=pco[:, 0:n], in1=sgc[:, 0:n], op=ALU.mult),
                    reads=[pcob, sgcb], writes=[m2b])
                S_.op("dve", lambda e, n=n, m1=m1, m2=m2, e_=e_, c0=c0: e.tensor_tensor(
                    out=merged[:, e_, c0:c0 + n], in0=m1[:, 0:n], in1=m2[:, 0:n], op=ALU.add),
                    reads=[m1b, m2b], writes=[b("mg", e_, ci)])
            if e_ + 1 < KT:
                ws, wsb = ws_n, wsb_n
        S_.retire("DN")
        S_.retire("GLa")
        S_.retire("GLb")
        S_.retire("R1")

        X2D = [V(GL + 8192 + k * 4096, F32, D) for k in range(3)]
        XAD = [V(GL + 20480 + k * 4096, F32, D) for k in range(3)]
        XSD = [V(GL + 32768 + k * 2048, BF16, D) for k in range(2)]
        JK_D = V(Z + 36864, BF16, D)
        WD = V(DN, BF16, FT, D)
        h2T = V(R1, BF16, KT, T)
        S_.dma("sp", lambda e: e.dma_start(out=GBflat, in_=gffn_d), writes=[b("GB")])

        def load_ffn_w(f):
            ws_, wsb_ = wsbuf()
            wload(ws_, wsb_, 0, w_gate_v[:, :, f, :])
            wload(ws_, wsb_, 1, w_up_v[:, :, f, :])
            return ws_, wsb_

        ffn_w0 = load_ffn_w(0)
        WDQ = list(range(FT))

        def load_xD(i):
            xa = XAD[i % 3]
            S_.dma("sp", lambda e: e.dma_start(out=xa, in_=x_d[i * 128:(i + 1) * 128, :]), writes=[b("xaD", i % 3)])

        for i in range(3):
            load_xD(i)
        trD = {}
        for i in range(NT + 1):
            if i < NT:
                xa, xab = XAD[i % 3], b("xaD", i % 3)
                x2, x2b = X2D[i % 3], b("x2D", i % 3)
                ci = min(i // 4, 4)
                pos = []
                for hh in range(2):
                    po, pob = bank()
                    pos.append((po, pob))
                    for kt in range(KT):
                        S_.op("pe", lambda e, po=po, kt=kt, i=i, hh=hh: e.matmul(
                            po[:, 0:512], lhsT=merged[:, kt, i * 128:(i + 1) * 128], rhs=WO[:, kt, hh * 512:(hh + 1) * 512],
                            start=(kt == 0), stop=(kt == KT - 1)),
                            reads=[b("wo")] + [b("mg", kt, ci) for kt in range(KT)], writes=[pob])
            if i >= 1:
                i2 = i - 1
                x2p, x2pb = X2D[i2 % 3], b("x2D", i2 % 3)
                xs, xsb = XSD[i2 % 2], b("xsD", i2 % 2)
                norm_scale(i2, 1, x2p, [x2pb], xs, xsb)
                S_.dma("pool", lambda e, i2=i2, x2p=x2p: e.dma_start(out=x2s_d[i2 * 128:(i2 + 1) * 128, :], in_=x2p),
                       reads=[x2pb], writes=[b("x2s", i2)])
                trD[i2] = norm_transpose(i2, xs, xsb)
            if i < NT:
                for hh in range(2):
                    po, pob = pos[hh]
                    S_.op("dve", lambda e, po=po, hh=hh, xa=xa, x2=x2: e.tensor_tensor(
                        out=x2[:, hh * 512:(hh + 1) * 512], in0=po[:, 0:512], in1=xa[:, hh * 512:(hh + 1) * 512], op=ALU.add),
                        reads=[pob, xab], writes=[x2b])
                if i + 3 < NT:
                    load_xD(i + 3)
                norm_stage1([i], 1, lambda i_: (X2D[i_ % 3], [b("x2D", i_ % 3)]), JK_D, "jkD")
            if i >= 1:
                pv3, pbb = trD.pop(i - 1)
                norm_evac(i - 1, pv3, pbb, h2T, "h2T")
        S_.retire("Z")
        S_.retire("Zx1")
        S_.retire("Zs")
        S_.retire("S")
        S_.retire("GLb")

        FF = V(DN + 45056, BF16, FT, 1280)
        X2T = [V(DN + 101376 + k * 4096, F32, D) for k in range(2)]
        YT = [V(DN + 109568 + k * 4096, F32, D) for k in range(2)]
        assert DN + 109568 + 8192 <= WS0
        SGF = [V(S + k * 2048, F32, 512) for k in range(2)]
        JK_F = V(S + 4096, BF16, D)
        SD = V(S + 8192, F32, 32)
        it = 0
        for hi_, (chs, tiles, t0, tn) in enumerate(HALVES):
            ws, wsb = ffn_w0
            for f in range(FT):
                if f + 1 < FT:
                    ws_n, wsb_n = load_ffn_w(f + 1)
                elif hi_ == 0:
                    ffn_w0 = load_ffn_w(0)
                if hi_ == 0:
                    S_.dma("pool", lambda e, f=f: e.dma_start(out=WD[:, f:f + 1, :], in_=w_down_v[:, f:f + 1, :]),
                           writes=[b("wd", f)])
                for ci in chs:
                    c0, n = CH[ci]
                    pg, pgb = bank()
                    pu, pub = bank()
                    mm8(pg, pgb, n, ws, wsb, 0, h2T, hT_bufs("h2T", c0, n), c0)
                    mm8(pu, pub, n, ws, wsb, 1, h2T, hT_bufs("h2T", c0, n), c0)
                    sgf, sgfb = SGF[it % 2], b("sgf", it % 2)
                    it += 1
                    S_.op("act", lambda e, pg=pg, n=n, sgf=sgf: e.activation(out=sgf[:, 0:n], in_=pg[:, 0:n], func=AF.Silu),
                          reads=[pgb], writes=[sgfb])
                    S_.op("dve", lambda e, pu=pu, n=n, sgf=sgf, f=f, c0=c0, t0=t0: e.tensor_tensor(
                        out=FF[:, f, c0 - t0:c0 - t0 + n], in0=pu[:, 0:n], in1=sgf[:, 0:n], op=ALU.mult),
                        reads=[pub, sgfb], writes=[b("ff", f, ci - chs[0])])
                if f + 1 < FT:
                    ws, wsb = ws_n, wsb_n
            if hi_ == 0:
                S_.dma("sp", lambda e: e.dma_start(out=GBflat, in_=gfin_d), writes=[b("GB")])
            for i in tiles:
                ci = min(i // 4, 4)
                x2t, x2tb = X2T[i % 2], b("x2t", i % 2)
                yt, ytb = YT[i % 2], b("yt", i % 2)
                S_.dma("sp", lambda e, x2t=x2t, i=i: e.dma_start(out=x2t, in_=x2s_d[i * 128:(i + 1) * 128, :]),
                       reads=[b("x2s", i)], writes=[x2tb])
                lo = i * 128 - t0
                for hh in range(2):
                    po, pob = bank()
                    for f in range(FT):
                        S_.op("pe", lambda e, po=po, f=f, lo=lo, hh=hh: e.matmul(
                            po[:, 0:512], lhsT=FF[:, f, lo:lo + 128], rhs=WD[:, f, hh * 512:(hh + 1) * 512],
                            start=(f == 0), stop=(f == FT - 1)),
                            reads=[b("wd", WDQ[f]), b("ff", f, ci - chs[0])], writes=[pob])
                    S_.op("dve", lambda e, po=po, hh=hh, x2t=x2t: e.tensor_tensor(
                        out=x2t[:, hh * 512:(hh + 1) * 512], in0=po[:, 0:512], in1=x2t[:, hh * 512:(hh + 1) * 512], op=ALU.add),
                        reads=[pob, x2tb], writes=[x2tb])
                S_.op("act", lambda e, x2t=x2t, i=i: e.activation(out=JK_F, in_=x2t, func=AF.Square, accum_out=ss[:, 2, i:i + 1]),
                      reads=[x2tb, b("ss")], writes=[b("jkF"), b("ss3", i)])
                S_.op("act", lambda e, i=i: e.activation(out=SD[:, i:i + 1], in_=ss[:, 2, i:i + 1], func=AF.Sqrt,
                                                        bias=eps_ap, scale=1.0 / D),
                      reads=[b("ss3", i), b("cst")], writes=[b("sd", i)])
                S_.op("dve", lambda e, i=i: e.reciprocal(out=SD[:, i:i + 1], in_=SD[:, i:i + 1]),
                      reads=[b("sd", i)], writes=[b("sd", i)])
                S_.op("dve", lambda e, x2t=x2t, yt=yt, i=i: e.scalar_tensor_tensor(
                    out=yt, in0=x2t, scalar=SD[:, i:i + 1], in1=GBflat, op0=ALU.mult, op1=ALU.mult),
                    reads=[x2tb, b("sd", i), b("GB")], writes=[ytb])
                finals.append(S_.dma("pool", lambda e, yt=yt, i=i: e.dma_start(out=y_d[i * 128:(i + 1) * 128, :], in_=yt),
                                     reads=[ytb]))
        S_.emit(final_wait_ops=finals)
    return nc


_NC_CACHE = {}


def _colT(v):
    return np.ascontiguousarray(np.asarray(v, np.float32).reshape(-1, 128).T)


def kernel(x_prompt, x_sample, state_lru_h, cache_lru_conv, cache_ccm_conv,
           g_mix, w_in, b_in, w_lru_conv, b_lru_conv, w_rg_r, b_rg_r, w_rg_i, b_rg_i,
           lru_lambda, w_lru_o, w_ccm_dw, b_ccm_dw, g_ccm_ln, b_ccm_ln, w_ccm_o, w_out,
           g_ffn, w_ffn_gate, w_ffn_up, w_ffn_down, g_final):
    f = lambda a: np.ascontiguousarray(np.asarray(a, np.float32))
    n_cores = 8
    pp = np.zeros((128, NPP), np.float32)
    pp[:, PP_BIN:PP_BIN + 48] = _colT(b_in[0])
    pp[:, PP_WLC:PP_WLC + 32] = f(w_lru_conv[0]).reshape(4, 8, 128).transpose(2, 1, 0).reshape(128, 32)
    pp[:, PP_BLC:PP_BLC + 8] = _colT(b_lru_conv[0])
    pp[:, PP_BR:PP_BR + 8] = _colT(b_rg_r[0])
    pp[:, PP_BI:PP_BI + 8] = _colT(b_rg_i[0])
    pp[:, PP_LAM:PP_LAM + 8] = _colT(lru_lambda[0])
    pp[:, PP_WDW:PP_WDW + 248] = f(w_ccm_dw[0]).reshape(31, 8, 128).transpose(2, 1, 0).reshape(128, 248)
    pp[:, PP_BDW:PP_BDW + 8] = _colT(b_ccm_dw[0])
    pp[:, PP_GLN:PP_GLN + 8] = _colT(g_ccm_ln[0])
    pp[:, PP_BLN:PP_BLN + 8] = _colT(b_ccm_ln[0])

    def gB(g):
        return np.ascontiguousarray(np.repeat(_colT(g)[:, :, None], 128, axis=2).reshape(128, D))

    gmixB = gB(g_mix[0])
    gffnB = gB(g_ffn[0])
    gfinB = np.ascontiguousarray(np.broadcast_to(f(g_final)[None, :], (128, D)))
    shared = {
        "pp": pp, "gmixB": gmixB, "gffnB": gffnB, "gfinB": gfinB,
        "w_in": f(w_in[0]), "w_rg_r": f(w_rg_r[0]), "w_rg_i": f(w_rg_i[0]),
        "w_lru_o": f(w_lru_o[0]), "w_ccm_o": f(w_ccm_o[0]), "w_out": f(w_out[0]),
        "w_ffn_gate": f(w_ffn_gate[0]), "w_ffn_up": f(w_ffn_up[0]), "w_ffn_down": f(w_ffn_down[0]),
    }
    xp = f(x_prompt)
    xs = f(x_sample)
    in_maps = []
    for c in range(n_cores):
        sl = slice(NS * c, NS * c + NS)
        m = dict(shared)
        m["x"] = np.ascontiguousarray(np.concatenate([xp[c], xs[sl].reshape(NS * TS, D)], axis=0))
        m["h0"] = np.ascontiguousarray(f(state_lru_h[0, sl]).reshape(NS, 8, 128).transpose(2, 1, 0).reshape(128, KT * NS))
        m["lc"] = np.ascontiguousarray(f(cache_lru_conv[0, sl]).reshape(NS, 3, 8, 128).transpose(3, 2, 0, 1).reshape(128, KT * NS * 3))
        m["cc"] = np.ascontiguousarray(f(cache_ccm_conv[0, sl]).reshape(NS, 30, 8, 128).transpose(3, 2, 0, 1).reshape(128, KT * NS * 30))
        in_maps.append(m)
    if "nc" not in _NC_CACHE:
        _NC_CACHE["nc"] = build_nc()
    nc = _NC_CACHE["nc"]
    import os as _os2
    if _os2.environ.get("K_TRACE"):
        res = run_bass_kernel_spmd(nc, in_maps, core_ids=list(range(n_cores)), trace=True)
        print("EXEC_TIME_NS", res.exec_time_ns)
    else:
        res = run_bass_kernel_spmd(nc, in_maps, core_ids=list(range(n_cores)))
    y_prompt = np.zeros((8, TP, D), np.float32)
    y_sample = np.zeros((32, TS, D), np.float32)
    p_h = np.zeros((1, 8, D), np.float32)
    p_lb = np.zeros((1, 8, 3, D), np.float32)
    p_cb = np.zeros((1, 8, 30, D), np.float32)
    s_h = np.zeros((1, 32, D), np.float32)
    s_lb = np.zeros((1, 32, 3, D), np.float32)
    s_cb = np.zeros((1, 32, 30, D), np.float32)
    for c in range(n_cores):
        r = res.results[c]
        y = np.asarray(r["y"], np.float32)
        y_prompt[c] = y[:TP]
        y_sample[NS * c:NS * c + NS] = y[TP:].reshape(NS, TS, D)
        st5 = np.asarray(r["st"], np.float32).reshape(128, KT, 5, 34).transpose(2, 3, 1, 0).reshape(5, 34, D)
        p_h[0, c] = st5[0, 0]
        p_lb[0, c] = st5[0, 1:4]
        p_cb[0, c] = st5[0, 4:34]
        for s in range(NS):
            s_h[0, NS * c + s] = st5[1 + s, 0]
            s_lb[0, NS * c + s] = st5[1 + s, 1:4]
            s_cb[0, NS * c + s] = st5[1 + s, 4:34]
    return (y_prompt, y_sample, p_h, p_lb, p_cb, s_h, s_lb, s_cb)
```

```python
import contextlib
from collections import defaultdict

import numpy as np
import concourse.bass as bass
import concourse.mybir as mybir
from concourse.bass_utils import run_bass_kernel_spmd

F32 = mybir.dt.float32
BF16 = mybir.dt.bfloat16
U8 = mybir.dt.uint8
AF = mybir.ActivationFunctionType
ALU = mybir.AluOpType

T = 2304
TP = 2048
NS = 4
TS = 64
NT = 18
D = 1024
KT = 8
DFF = 2816
FT = 22
EPS = 1e-6
CH = [(0, 512), (512, 512), (1024, 512), (1536, 512), (2048, 256)]
HALVES = [([0, 1], list(range(0, 8)), 0, 1024), ([2, 3, 4], list(range(8, 18)), 1024, 1280)]

PP_BIN = 0
PP_WLC = 48
PP_BLC = 80
PP_BR = 88
PP_BI = 96
PP_LAM = 104
PP_WDW = 112
PP_BDW = 360
PP_GLN = 368
PP_BLN = 376
NPP = 384

ENGS = ("pe", "act", "dve", "pool", "sp")


class Buf:
    __slots__ = ("writer", "readers", "dreaders", "excl")

    def __init__(self):
        self.writer = None
        self.readers = {}
        self.dreaders = []
        self.excl = False


class Op:
    __slots__ = ("eng", "fn", "idx", "is_dma", "waits", "signal", "sig_count", "dma_sem", "dma_val",
                 "pre_dma_wait")

    def __init__(self, eng, fn, is_dma):
        self.eng = eng
        self.fn = fn
        self.is_dma = is_dma
        self.waits = []
        self.signal = False
        self.sig_count = None
        self.dma_sem = None
        self.dma_val = None
        self.pre_dma_wait = None


class Sched:
    def __init__(self, nc, n_dma_sems=10):
        self.nc = nc
        self.ops = {e: [] for e in ENGS}
        self.n_dma_sems = n_dma_sems
        self.dma_rr = {e: 0 for e in ENGS}
        self.dma_state = {}
        self.same_eng_gap = 8
        self.bufs = {}
        self.name2reg = {}
        self.region_bufs = defaultdict(list)
        self.region_ghost = {}

    def b(self, *key):
        buf = self.bufs.get(key)
        if buf is not None:
            return buf
        buf = Buf()
        self.bufs[key] = buf
        for reg in (self.name2reg.get(key[:2]) or self.name2reg.get(key[0], ())):
            self.region_bufs[reg].append((key, buf))
            g = self.region_ghost.get(reg)
            if g:
                for e, op in g[0].items():
                    if e not in buf.readers or buf.readers[e].idx < op.idx:
                        buf.readers[e] = op
                buf.dreaders.extend(g[1])
        return buf

    def retire(self, reg):
        rd, dr = {}, []
        g = self.region_ghost.get(reg)
        if g:
            rd.update(g[0])
            dr.extend(g[1])
        for key, buf in self.region_bufs[reg]:
            ops = list(buf.readers.values()) + list(buf.dreaders)
            if buf.writer is not None:
                ops.append(buf.writer)
            for op in ops:
                if op.is_dma:
                    if op not in dr:
                        dr.append(op)
                elif op.eng not in rd or rd[op.eng].idx < op.idx:
                    rd[op.eng] = op
            self.bufs.pop(key, None)
        self.region_bufs[reg] = []
        self.region_ghost[reg] = (rd, dr[-12:])

    def _add(self, eng, fn, reads, writes, is_dma):
        op = Op(eng, fn, is_dma)
        lst = self.ops[eng]
        op.idx = len(lst)
        deps = []
        for b in reads:
            if b.writer is not None:
                deps.append(b.writer)
            if b.excl:
                for e2, r in b.readers.items():
                    if e2 != eng:
                        deps.append(r)
        for b in writes:
            if b.writer is not None:
                deps.append(b.writer)
            deps.extend(b.readers.values())
            deps.extend(b.dreaders)
        seen = set()
        for d in deps:
            if d is op or id(d) in seen:
                continue
            seen.add(id(d))
            if (not d.is_dma) and d.eng == eng:
                if eng == "pe":
                    continue
                if op.idx - d.idx >= self.same_eng_gap:
                    continue
            op.waits.append(d)
            d.signal = True
        for b in reads:
            if is_dma:
                b.dreaders.append(op)
            else:
                b.readers[eng] = op
        for b in writes:
            b.writer = op
            b.readers = {}
            b.dreaders = []
        if is_dma:
            slot = self.dma_rr[eng]
            self.dma_rr[eng] = (slot + 1) % self.n_dma_sems
            key = (eng, slot)
            val, prev = self.dma_state.get(key, (0, None))
            op.pre_dma_wait = prev
            val += 16
            op.dma_sem = key
            op.dma_val = val
            self.dma_state[key] = (val, op)
        lst.append(op)
        return op

    def op(self, eng, fn, reads=(), writes=()):
        return self._add(eng, fn, list(reads), list(writes), False)

    def dma(self, eng, fn, reads=(), writes=()):
        return self._add(eng, fn, list(reads), list(writes), True)

    def barrier(self):
        lasts = []
        for e in ENGS:
            for op in reversed(self.ops[e]):
                if (not op.is_dma) and op.fn is not None:
                    op.signal = True
                    lasts.append(op)
                    break
        dl = [op for (_, op) in self.dma_state.values()]
        for e in ENGS:
            op = Op(e, None, False)
            op.idx = len(self.ops[e])
            op.waits = list(lasts) + list(dl)
            self.ops[e].append(op)

    def emit(self, final_wait_ops=()):
        nc = self.nc
        for op in final_wait_ops:
            op.signal = True
        with contextlib.ExitStack() as st:
            esem = {e: st.enter_context(nc.semaphore("s_" + e)) for e in ENGS}
            dsem = {}
            for key in sorted(self.dma_state.keys()):
                dsem[key] = st.enter_context(nc.semaphore("d_%s%d" % key))
            for e in ENGS:
                c = 0
                for op in self.ops[e]:
                    if op.is_dma or op.fn is None:
                        continue
                    if op.signal:
                        c += 1
                        op.sig_count = c
            block = st.enter_context(nc.Block())
            engobj = {"pe": "tensor", "act": "scalar", "dve": "vector", "pool": "gpsimd", "sp": "sync"}

            def make(e):
                def body(eng):
                    seen = {}

                    def wait(semkey, sem, val):
                        if seen.get(semkey, 0) >= val:
                            return
                        seen[semkey] = val
                        eng.wait_ge(sem, val)

                    def wait_op(d):
                        if d.is_dma:
                            wait(d.dma_sem, dsem[d.dma_sem], d.dma_val)
                        else:
                            wait(d.eng, esem[d.eng], d.sig_count)

                    for op in self.ops[e]:
                        for d in op.waits:
                            wait_op(d)
                        if op.fn is None:
                            continue
                        if op.is_dma and op.pre_dma_wait is not None:
                            wait_op(op.pre_dma_wait)
                        ins = op.fn(eng)
                        if op.is_dma:
                            ins.then_inc(dsem[op.dma_sem], 16)
                        elif op.signal:
                            ins.then_inc(esem[e], 1)
                    if e == "sp":
                        for op in final_wait_ops:
                            wait_op(op)
                return body

            for e in ENGS:
                getattr(block, engobj[e])(make(e))


def build_nc():
    nc = bass.Bass("TRN2", target_bir_lowering=False)

    def din(name, shape):
        return nc.dram_tensor(name, list(shape), F32, kind="ExternalInput").ap()

    x_d = din("x", [T, D])
    h0_d = din("h0", [128, KT * NS])
    lc_d = din("lc", [128, KT * NS * 3])
    cc_d = din("cc", [128, KT * NS * 30])
    pp_d = din("pp", [128, NPP])
    gmix_d = din("gmixB", [128, D])
    gffn_d = din("gffnB", [128, D])
    gfin_d = din("gfinB", [128, D])
    w_in_d = din("w_in", [D, 6 * D])
    w_rgr_d = din("w_rg_r", [16, 64, 64])
    w_rgi_d = din("w_rg_i", [16, 64, 64])
    w_lruo_d = din("w_lru_o", [D, D])
    w_ccmo_d = din("w_ccm_o", [D, D])
    w_out_d = din("w_out", [D, D])
    w_gate_d = din("w_ffn_gate", [D, DFF])
    w_up_d = din("w_ffn_up", [D, DFF])
    w_down_d = din("w_ffn_down", [DFF, D])
    y_d = nc.dram_tensor("y", [T, D], F32, kind="ExternalOutput").ap()
    st_d = nc.dram_tensor("st", [128, KT * 5 * 34], F32, kind="ExternalOutput").ap()
    x2s_d = nc.dram_tensor("x2s", [T, D], F32).ap()

    w_in_v = w_in_d.rearrange("(kt p) (g c) -> p kt g c", p=128, c=128)
    w_lruo_v = w_lruo_d.rearrange("(kt p) (g c) -> p kt g c", p=128, c=128)
    w_ccmo_v = w_ccmo_d.rearrange("(kt p) (g c) -> p kt g c", p=128, c=128)
    w_out_v = w_out_d.rearrange("(kt p) c -> p kt c", p=128)
    w_gate_v = w_gate_d.rearrange("(kt p) (g c) -> p kt g c", p=128, c=128)
    w_up_v = w_up_d.rearrange("(kt p) (g c) -> p kt g c", p=128, c=128)
    w_down_v = w_down_d.rearrange("(ft p) c -> p ft c", p=128)

    with contextlib.ExitStack() as st:
        ARENA = 207616
        arena = st.enter_context(nc.sbuf_tensor("arena", [128, ARENA + 4608 + 512], U8))
        import os as _os
        PB = [st.enter_context(nc.psum_tensor("pb%d" % i, [128, 512], F32)) for i in range(8)]

        def V(off, dt, *shape):
            sz = 2 if dt == BF16 else 4
            n = 1
            for s in shape:
                n *= s
            v = arena[:, off:off + n * sz].bitcast(dt)
            if len(shape) == 2:
                return v.rearrange("p (a b) -> p a b", a=shape[0])
            if len(shape) == 3:
                return v.rearrange("p (a b c) -> p a b c", a=shape[0], b=shape[1])
            return v

        R1, DN, GL, Z = 0, 36864, 73728, 110592
        WS0, WS1, S = 157696, 165888, 174080
        P0 = 190464
        GBo = P0
        STo = GBo + 4096
        PPo = STo + 5440
        IDBo = PPo + 1536
        IDFo = IDBo + 256
        ONFo = IDFo + 512
        CSTo = ONFo + 512
        SSo = CSTo + 64
        CLo = SSo + 256
        H0o = CLo + 128
        LCo = H0o + 128
        CCo = LCo + 384
        assert CCo + 3840 <= ARENA

        hT = V(R1, BF16, KT, T)
        dn = V(DN, BF16, KT, T)
        glru = V(GL, BF16, KT, T)
        GB = V(GBo, F32, KT, 128)
        GBflat = V(GBo, F32, D)
        stage = V(STo, F32, KT, 5, 34)
        stage_flat = V(STo, F32, KT * 5 * 34)
        pp = V(PPo, F32, NPP)
        identb = V(IDBo, BF16, 128)
        identf = V(IDFo, F32, 128)
        onesf = V(ONFo, F32, 128)
        cst = V(CSTo, F32, 16)
        ss = V(SSo, F32, 3, 20)
        cl = V(CLo, F32, 4, 8)
        h0 = V(H0o, F32, KT, NS)
        lc = V(LCo, F32, KT, NS, 3)
        cc = V(CCo, F32, KT, NS, 30)
        WS = [V(WS0, BF16, KT, 4, 128), V(WS1, BF16, KT, 4, 128)]
        eps_ap = cst[:, 0:1]
        one_ap = cst[:, 1:2]

        def col(base, j):
            return pp[:, base + j:base + j + 1]

        S_ = Sched(nc)
        S_.name2reg = {
            "xaA": ["DN"], "jkA": ["DN"], "xsA": ["DN"], "hT": ["R1"], "h2T": ["R1"],
            "ws": ["WS"],
            "d31": ["GLa", "GLb"], "S1": ["GLb"], "S2": ["GLb"],
            "ub": ["S"], "ubpad": ["S"], "ubs": ["S"], "sig": ["S"], "dq": ["S"],
            "dz": ["DN"], "acc": ["Z"],
            "mu": ["Zs"], "rs": ["Zs"], "t1": ["Zs", "GLb"], "t2": ["Zs"], "dn": ["DN"],
            "d4": ["S"], "wg": ["S"], "xlb": ["S"], "xlbpad": ["S"], "xlbs": ["S"], "xcb": ["S"],
            ("X", 0): ["Z"], ("X", 1): ["Zx1"], ("mu", 1): ["Zx1"], ("rs", 1): ["Zx1"], "s1b": ["Zx1"], "R": ["Z"], "I": ["Z"], "A": ["Zs"], "hs": ["Z"], "glru": ["GLa", "GLb"],
            "sgl": ["Zs"], "sgc": ["Zs"], "m1": ["Zs"], "m2": ["Zs"], "mg": ["Z", "Zx1"],
            "wo": ["S"], "x2D": ["GLb"], "xaD": ["GLb"], "xsD": ["GLb"], "jkD": ["Zs"],
            "wd": ["DN", "GLa"], "ff": ["GLb", "Z", "Zx1"], "x2t": ["Z", "Zs"], "yt": ["Zs"],
            "sgf": ["S"], "jkF": ["S"], "sd": ["S"],
        }
        b = S_.b
        pbank = [0]
        for _i in range(8):
            b("pb", _i).excl = True

        def bank():
            i = pbank[0]
            pbank[0] = (i + 1) % 8
            return PB[i], b("pb", i)

        wslot = [0]

        def wsbuf():
            i = wslot[0]
            wslot[0] = (i + 1) % 2
            return WS[i], i

        finals = []

        S_.dma("sp", lambda e: e.dma_start(out=pp, in_=pp_d), writes=[b("pp")])
        NXA = 7
        XA = [V(DN + k * 4096, F32, D) for k in range(NXA)]

        def load_xA(i):
            xa = XA[i % NXA]
            S_.dma("sp", lambda e: e.dma_start(out=xa, in_=x_d[i * 128:(i + 1) * 128, :]), writes=[b("xaA", i % NXA)])

        for i in range(NXA):
            load_xA(i)
        S_.dma("sp", lambda e: e.dma_start(out=GBflat, in_=gmix_d), writes=[b("GB")])
        S_.dma("sp", lambda e: e.dma_start(out=V(H0o, F32, KT * NS), in_=h0_d), writes=[b("h0")])
        S_.dma("sp", lambda e: e.dma_start(out=V(LCo, F32, KT * NS * 3), in_=lc_d), writes=[b("lc")])
        S_.dma("sp", lambda e: e.dma_start(out=V(CCo, F32, KT * NS * 30), in_=cc_d), writes=[b("cc")])
        S_.op("pool", lambda e: e.memset(identf, 0.0), writes=[b("identf")])
        S_.op("pool", lambda e: e.affine_select(out=identf, in_=identf, pattern=[[-1, 128]],
                                                compare_op=ALU.not_equal, fill=1.0, base=0,
                                                channel_multiplier=1),
              reads=[b("identf")], writes=[b("identf")])
        S_.op("pool", lambda e: e.memset(onesf, 1.0 / D), writes=[b("onesf")])
        S_.op("pool", lambda e: e.memset(cst[:, 0:1], EPS), writes=[b("cst")])
        S_.op("pool", lambda e: e.memset(cst[:, 1:2], 1.0), writes=[b("cst")])
        S_.op("pool", lambda e: e.memset(V(SSo, F32, 64), 0.0), writes=[b("ss")])
        S_.op("dve", lambda e: e.tensor_copy(out=identb, in_=identf), reads=[b("identf")], writes=[b("identb")])
        S_.op("act", lambda e: e.activation(out=cl[:, 2, :], in_=pp[:, PP_LAM:PP_LAM + 8], func=AF.Exp, scale=-1.0),
              reads=[b("pp")], writes=[b("cl")])
        S_.op("act", lambda e: e.activation(out=cl[:, 3, :], in_=cl[:, 2, :], func=AF.Ln, bias=one_ap),
              reads=[b("cl"), b("cst")], writes=[b("cl")])
        S_.op("dve", lambda e: e.tensor_scalar(out=cl[:, 0, :], in0=cl[:, 3, :], scalar1=-8.0, scalar2=None, op0=ALU.mult),
              reads=[b("cl")], writes=[b("cl")])
        S_.op("dve", lambda e: e.tensor_scalar(out=cl[:, 1, :], in0=cl[:, 3, :], scalar1=-16.0, scalar2=None, op0=ALU.mult),
              reads=[b("cl")], writes=[b("cl")])

        def wload(ws, wsb, g, src):
            S_.dma("pool", lambda e: e.dma_start(out=ws[:, :, g, :], in_=src), writes=[b("ws", wsb, g)])

        def mm8(pb, pbb, n, ws, wsb, g, rhs3, rbufs, c0):
            for kt in range(KT):
                S_.op("pe", lambda e, kt=kt: e.matmul(pb[:, 0:n], lhsT=ws[:, kt, g, :], rhs=rhs3[:, kt, c0:c0 + n],
                                                      start=(kt == 0), stop=(kt == KT - 1)),
                      reads=[b("ws", wsb, g)] + rbufs, writes=[pbb])

        def hT_bufs(name, c0, n):
            return [b(name, i) for i in range(c0 // 128, (c0 + n) // 128)]

        ws, wsb = wsbuf()
        wload(ws, wsb, 0, w_in_v[:, :, 16, :])
        wload(ws, wsb, 1, w_in_v[:, :, 24, :])

        def norm_stage1(tiles, row, src_fn, jk, jkname):
            for i in tiles:
                src_ap, src_bufs = src_fn(i)
                S_.op("act", lambda e, src_ap=src_ap, i=i: e.activation(out=jk, in_=src_ap, func=AF.Square,
                                                                       accum_out=ss[:, row, i:i + 1]),
                      reads=list(src_bufs) + [b("ss")], writes=[b(jkname), b("ssc", row, i)])
            lo, hi = tiles[0], tiles[-1] + 1
            S_.op("act", lambda e: e.activation(out=ss[:, row, lo:hi], in_=ss[:, row, lo:hi], func=AF.Sqrt,
                                                bias=eps_ap, scale=1.0 / D),
                  reads=[b("ssc", row, i) for i in tiles] + [b("cst")], writes=[b("rstd", row, i) for i in tiles])
            S_.op("dve", lambda e: e.reciprocal(out=ss[:, row, lo:hi], in_=ss[:, row, lo:hi]),
                  reads=[b("rstd", row, i) for i in tiles], writes=[b("rstd", row, i) for i in tiles])

        def norm_scale(i, row, src_ap, src_bufs, xs, xs_buf, eng="act"):
            if eng == "act":
                S_.op("act", lambda e: e.activation(out=xs, in_=src_ap, func=AF.Identity, scale=ss[:, row, i:i + 1]),
                      reads=list(src_bufs) + [b("rstd", row, i)], writes=[xs_buf])
            else:
                S_.op("dve", lambda e: e.tensor_scalar(out=xs, in0=src_ap, scalar1=ss[:, row, i:i + 1], scalar2=None,
                                                       op0=ALU.mult),
                      reads=list(src_bufs) + [b("rstd", row, i)], writes=[xs_buf])

        def norm_transpose(i, xs, xs_buf):
            pb, pbb = bank()
            pv = pb[:].bitcast(BF16)
            for kt in range(KT):
                S_.op("pe", lambda e, kt=kt: e.transpose(pv[:, kt * 128:(kt + 1) * 128], xs[:, kt * 128:(kt + 1) * 128], identb),
                      reads=[xs_buf, b("identb")], writes=[pbb])
            return pv.rearrange("p (a n) -> p a n", a=KT), pbb

        def norm_evac(i, pv3, pbb, dstT, dst_name):
            S_.op("dve", lambda e: e.tensor_tensor(out=dstT[:, :, i * 128:(i + 1) * 128], in0=pv3, in1=GB, op=ALU.mult),
                  reads=[pbb, b("GB")], writes=[b(dst_name, i)])

        JK_A = V(DN + 28672, BF16, D)
        XSA = [V(DN + 30720 + k * 2048, BF16, D) for k in range(3)]

        def phase_A_chunk(ci):
            c0, n = CH[ci]
            tiles = list(range(c0 // 128, (c0 + n) // 128))
            for i in tiles:
                norm_stage1([i], 0, lambda i_: (XA[i_ % NXA], [b("xaA", i_ % NXA)]), JK_A, "jkA")
            pend = None
            for i in tiles:
                xs, xsb = XSA[i % 3], b("xsA", i % 3)
                norm_scale(i, 0, XA[i % NXA], [b("xaA", i % NXA)], xs, xsb, eng="dve")
                if i + NXA < NT:
                    load_xA(i + NXA)
                if pend is not None:
                    norm_evac(*pend)
                pv3, pbb = norm_transpose(i, xs, xsb)
                pend = (i, pv3, pbb, hT, "hT")
            norm_evac(*pend)

        D31 = [V(GL + k * 7936, BF16, 31, 128) for k in range(2)]

        _ND = int(_os.environ.get("K_ND", "5"))

        def npe_of(j):
            return 31 if j == KT - 1 else 31 - _ND

        def build_d31(j):
            d31_ = D31[j % 2]
            for k in range(npe_of(j)):
                S_.op("act", lambda e, k=k, d31_=d31_, j=j: e.activation(out=d31_[:, k, :], in_=identf, func=AF.Identity,
                                                                        scale=col(PP_WDW, j * 31 + k)),
                      reads=[b("identf"), b("pp")], writes=[b("d31", j % 2, k)])

        build_d31(0)
        phase_A_chunk(0)

        dz = V(DN, BF16, KT, T)
        S1 = V(GL + 15872, F32, T)
        S2 = V(GL + 25088, F32, T)
        ub = V(S, BF16, 2454)
        ub_s = ub[:, 2078:2454].rearrange("p (s n) -> p s n", s=NS)
        SIG = [V(S + 4912 + k * 2048, F32, 512) for k in range(2)]
        DQ = [V(S + 9008 + k * 2048, F32, 512) for k in range(2)]
        MU = [V(Z + 36864, F32, 512)]
        RS = [V(Z + 38912, F32, 512)]
        T1 = [V(Z + 40960, F32, 512), V(GL + 34304, F32, 512)]
        T2 = [V(Z + 43008 + k * 2048, F32, 512) for k in range(2)]

        MU = [MU[0], V(Z + 9216, F32, 512)]
        RS = [RS[0], V(Z + 11264, F32, 512)]


        def ln_stats(ci):
            c0, n = CH[ci]
            p1, p1b = bank()
            p2, p2b = bank()
            S_.op("pe", lambda e: e.matmul(p1[:, 0:n], lhsT=onesf, rhs=S1[:, c0:c0 + n], start=True, stop=True),
                  reads=[b("onesf"), b("S1", ci)], writes=[p1b])
            S_.op("pe", lambda e: e.matmul(p2[:, 0:n], lhsT=onesf, rhs=S2[:, c0:c0 + n], start=True, stop=True),
                  reads=[b("onesf"), b("S2", ci)], writes=[p2b])
            mu, mub = MU[ci % 2], b("mu", ci % 2)
            rs, rsb = RS[ci % 2], b("rs", ci % 2)
            S_.op("dve", lambda e: e.tensor_copy(out=mu[:, 0:n], in_=p1[:, 0:n]),
                  reads=[p1b], writes=[mub])
            S_.op("dve", lambda e: e.tensor_tensor(out=rs[:, 0:n], in0=mu[:, 0:n], in1=mu[:, 0:n], op=ALU.mult),
                  reads=[mub], writes=[rsb])
            S_.op("dve", lambda e: e.tensor_tensor(out=rs[:, 0:n], in0=p2[:, 0:n], in1=rs[:, 0:n], op=ALU.subtract),
                  reads=[p2b, rsb], writes=[rsb])
            S_.op("dve", lambda e: e.tensor_scalar(out=rs[:, 0:n], in0=rs[:, 0:n], scalar1=0.0, scalar2=EPS, op0=ALU.max, op1=ALU.add),
                  reads=[rsb], writes=[rsb])
            S_.op("act", lambda e: e.activation(out=rs[:, 0:n], in_=rs[:, 0:n], func=AF.Ln),
                  reads=[rsb], writes=[rsb])
            S_.op("act", lambda e: e.activation(out=rs[:, 0:n], in_=rs[:, 0:n], func=AF.Exp, scale=-0.5),
                  reads=[rsb], writes=[rsb])

        NEGO = V(ARENA + 4608, BF16, 128)
        S_.op("pool", lambda e: e.memset(NEGO, -1.0 / D), writes=[b("nego")])
        S1B = [V(Z + 13312 + k * 1024, BF16, 512) for k in range(2)]

        def ln_norm(ci):
            c0, n = CH[ci]
            rs, rsb = RS[ci % 2], b("rs", ci % 2)
            s1b, s1bb = S1B[ci % 2], b("s1b", ci % 2)
            S_.op("dve", lambda e: e.tensor_copy(out=s1b[:, 0:n], in_=S1[:, c0:c0 + n]),
                  reads=[b("S1", ci)], writes=[s1bb])
            for j2 in range(KT):
                t2, t2b = T2[j2 % 2], b("t2", j2 % 2)
                pc_, pcb_ = bank()
                S_.op("pe", lambda e, pc_=pc_, j2=j2: e.matmul(pc_[:, 0:n], lhsT=identb, rhs=dz[:, j2, c0:c0 + n],
                                                              start=True, stop=False),
                      reads=[b("identb"), b("dz", j2, ci)], writes=[pcb_])
                S_.op("pe", lambda e, pc_=pc_: e.matmul(pc_[:, 0:n], lhsT=NEGO, rhs=s1b[:, 0:n],
                                                       start=False, stop=True),
                      reads=[b("nego"), s1bb], writes=[pcb_])
                S_.op("dve", lambda e, t2=t2, pc_=pc_: e.tensor_tensor(
                    out=t2[:, 0:n], in0=pc_[:, 0:n], in1=rs[:, 0:n], op=ALU.mult),
                    reads=[pcb_, rsb], writes=[t2b])
                S_.op("act", lambda e, t2=t2, j2=j2: e.activation(
                    out=dn[:, j2, c0:c0 + n], in_=t2[:, 0:n], func=AF.Silu, bias=col(PP_BLN, j2), scale=col(PP_GLN, j2)),
                    reads=[t2b, b("pp")], writes=[b("dn", j2, ci)])

        S_.op("pool", lambda e: e.memset(ub[:, 0:30], 0.0), writes=[b("ubpad")])
        ACC = [V(Z + k * 2048, F32, 512) for k in range(2)]
        for j in range(KT):
            if j + 1 < KT:
                ws_n, wsb_n = wsbuf()
                wload(ws_n, wsb_n, 0, w_in_v[:, :, 16 + j + 1, :])
                wload(ws_n, wsb_n, 1, w_in_v[:, :, 24 + j + 1, :])
            d31 = D31[j % 2]
            S_.op("dve", lambda e, j=j: e.tensor_copy(out=ub_s[:, :, 0:30], in_=cc[:, j, :, :]),
                  reads=[b("cc")], writes=[b("ubs")])
            for ci, (c0, n) in enumerate(CH):
                if j == 0 and ci + 1 < len(CH):
                    phase_A_chunk(ci + 1)
                pa, pab = bank()
                pc, pcb = bank()
                mm8(pa, pab, n, ws, wsb, 0, hT, hT_bufs("hT", c0, n), c0)
                mm8(pc, pcb, n, ws, wsb, 1, hT, hT_bufs("hT", c0, n), c0)
                sg, sgb = SIG[ci % 2], b("sig", ci % 2)
                S_.op("act", lambda e, pc=pc, n=n, sg=sg, j=j: e.activation(out=sg[:, 0:n], in_=pc[:, 0:n], func=AF.Sigmoid,
                                                                           bias=col(PP_BIN, 24 + j)),
                      reads=[pcb, b("pp")], writes=[sgb])
                bca = col(PP_BIN, 16 + j)
                if ci < 4:
                    S_.op("dve", lambda e, pa=pa, sg=sg, c0=c0, bca=bca: e.scalar_tensor_tensor(
                        out=ub[:, 30 + c0:30 + c0 + 512], in0=pa[:, 0:512], scalar=bca, in1=sg[:, 0:512],
                        op0=ALU.add, op1=ALU.mult),
                        reads=[pab, sgb, b("pp")], writes=[b("ub", ci)])
                    if ci == 3:
                        S_.op("dve", lambda e, pa=pa, sg=sg, bca=bca, j=j: e.scalar_tensor_tensor(
                            out=stage[:, j, 0, 4:34], in0=pa[:, 482:512], scalar=bca, in1=sg[:, 482:512],
                            op0=ALU.add, op1=ALU.mult),
                            reads=[pab, sgb, b("pp")], writes=[b("stage")])
                else:
                    pa3 = pa[:, 0:256].rearrange("p (s n) -> p s n", s=NS)
                    sg3 = sg[:, 0:256].rearrange("p (s n) -> p s n", s=NS)
                    S_.op("dve", lambda e, pa3=pa3, sg3=sg3, bca=bca: e.scalar_tensor_tensor(
                        out=ub_s[:, :, 30:94], in0=pa3, scalar=bca, in1=sg3, op0=ALU.add, op1=ALU.mult),
                        reads=[pab, sgb, b("pp")], writes=[b("ub", ci)])
                    S_.op("dve", lambda e, pa3=pa3, sg3=sg3, bca=bca, j=j: e.scalar_tensor_tensor(
                        out=stage[:, j, 1:5, 4:34], in0=pa3[:, :, 34:64], scalar=bca, in1=sg3[:, :, 34:64],
                        op0=ALU.add, op1=ALU.mult),
                        reads=[pab, sgb, b("pp")], writes=[b("stage")])
            if j == 0:
                S_.retire("DN")
            if j + 1 < KT:
                build_d31(j + 1)
            NPE = npe_of(j)
            for ci, (c0, n) in enumerate(CH):
                pd, pdb = bank()
                acc, accb = ACC[ci % 2], b("acc", ci % 2)
                if ci < 4:
                    rb = [b("ub", ci), b("ubpad")] + ([b("ub", ci - 1)] if ci > 0 else [])
                else:
                    rb = [b("ub", 4), b("ubs")]
                for k in range(NPE):
                    if ci < 4:
                        rhs = ub[:, c0 + k:c0 + k + 512]
                    else:
                        rhs = ub_s[:, :, k:k + 64]
                    pdo = pd[:, 0:n] if ci < 4 else pd[:, 0:256].rearrange("p (s n) -> p s n", s=NS)
                    S_.op("pe", lambda e, pdo=pdo, d31=d31, k=k, rhs=rhs, last=(k == NPE - 1): e.matmul(
                        pdo, lhsT=d31[:, k, :], rhs=rhs, start=(k == 0), stop=last),
                        reads=[b("d31", j % 2, k)] + rb, writes=[pdb])
                acco = acc[:, 0:n] if ci < 4 else acc[:, 0:256].rearrange("p (s n) -> p s n", s=NS)
                for k in range(NPE, 31):
                    src = ub[:, c0 + k:c0 + k + 512] if ci < 4 else ub_s[:, :, k:k + 64]
                    wk = col(PP_WDW, j * 31 + k)
                    if k == NPE:
                        S_.op("dve", lambda e, acco=acco, src=src, wk=wk: e.tensor_scalar(
                            out=acco, in0=src, scalar1=wk, scalar2=None, op0=ALU.mult),
                            reads=rb + [b("pp")], writes=[accb])
                    else:
                        S_.op("dve", lambda e, acco=acco, src=src, wk=wk: e.scalar_tensor_tensor(
                            out=acco, in0=src, scalar=wk, in1=acco, op0=ALU.mult, op1=ALU.add),
                            reads=rb + [b("pp"), accb], writes=[accb])
                bdw = col(PP_BDW, j)
                if NPE < 31:
                    S_.op("dve", lambda e, pd=pd, n=n, acc=acc, bdw=bdw: e.scalar_tensor_tensor(
                        out=acc[:, 0:n], in0=pd[:, 0:n], scalar=bdw, in1=acc[:, 0:n], op0=ALU.add, op1=ALU.add),
                        reads=[pdb, b("pp"), accb], writes=[accb])
                else:
                    S_.op("act", lambda e, pd=pd, n=n, acc=acc, bdw=bdw: e.activation(
                        out=acc[:, 0:n], in_=pd[:, 0:n], func=AF.Identity, bias=bdw),
                        reads=[pdb, b("pp")], writes=[accb])
                S_.op("act", lambda e, acc=acc, n=n, c0=c0, j=j: e.activation(
                    out=dz[:, j, c0:c0 + n], in_=acc[:, 0:n], func=AF.Identity),
                    reads=[accb], writes=[b("dz", j, ci)])
                dq, dqb = DQ[ci % 2], b("dq", ci % 2)
                S_.op("act", lambda e, acc=acc, n=n, dq=dq: e.activation(
                    out=dq[:, 0:n], in_=acc[:, 0:n], func=AF.Square),
                    reads=[accb], writes=[dqb])
                if j == 0:
                    S_.op("dve", lambda e, acc=acc, n=n, c0=c0: e.tensor_copy(out=S1[:, c0:c0 + n], in_=acc[:, 0:n]),
                          reads=[accb], writes=[b("S1", ci)])
                    S_.op("dve", lambda e, n=n, c0=c0, dq=dq: e.tensor_copy(out=S2[:, c0:c0 + n], in_=dq[:, 0:n]),
                          reads=[dqb], writes=[b("S2", ci)])
                else:
                    S_.op("dve", lambda e, acc=acc, n=n, c0=c0: e.tensor_tensor(
                        out=S1[:, c0:c0 + n], in0=S1[:, c0:c0 + n], in1=acc[:, 0:n], op=ALU.add),
                        reads=[accb, b("S1", ci)], writes=[b("S1", ci)])
                    S_.op("dve", lambda e, n=n, c0=c0, dq=dq: e.tensor_tensor(
                        out=S2[:, c0:c0 + n], in0=S2[:, c0:c0 + n], in1=dq[:, 0:n], op=ALU.add),
                        reads=[dqb, b("S2", ci)], writes=[b("S2", ci)])
                if j == KT - 1:
                    ln_stats(ci)
                    if ci >= 1:
                        ln_norm(ci - 1)
            if j == KT - 1:
                ln_norm(len(CH) - 1)
            if j + 1 < KT:
                ws, wsb = ws_n, wsb_n
        ws, wsb = wsbuf()
        wload(ws, wsb, 0, w_in_v[:, :, 0, :])
        wload(ws, wsb, 1, w_in_v[:, :, 8, :])
        S_.retire("S")
        S_.retire("Z")
        S_.retire("Zx1")
        S_.retire("Zs")
        S_.retire("GLa")
        S_.retire("GLb")

        Gb = V(ARENA, BF16, T)
        XX = [V(Z, F32, T), V(Z + 9216, F32, T)]
        R = V(Z + 18432, F32, T)
        I = V(Z + 27648, F32, T)
        A = V(Z + 36864, F32, T)
        D4 = [V(S + k * 1024, BF16, 4, 128) for k in range(2)]
        WG = V(S + 2048, BF16, KT, 2, 128)
        xlb = V(S + 6144, BF16, 2320)
        xlb_s = xlb[:, 2051:2319].rearrange("p (s n) -> p s n", s=NS)
        xcb = V(S + 10784, BF16, T)
        TAIL_ENG = "dve" if _os.environ.get("K_NOPOOL") else "pool"

        S_.op("pool", lambda e: e.memset(V(S + 2048, BF16, KT * 2 * 128), 0.0),
              writes=[b("wg", g_, q_) for g_ in range(2) for q_ in range(2)])
        S_.op("pool", lambda e: e.memset(xlb[:, 0:3], 0.0), writes=[b("xlbpad")])
        for g, wsrc in enumerate((w_rgr_d, w_rgi_d)):
            wv = wsrc.rearrange("(t q) d e -> q d t e", q=2)
            for q in range(2):
                S_.dma("pool", lambda e, g=g, q=q, wv=wv: e.dma_start(
                    out=WG[64 * q:64 * q + 64, :, g, 64 * q:64 * q + 64], in_=wv[q]),
                    reads=[], writes=[b("wg", g, q)])

        def build_d4(j):
            d4 = D4[j % 2]
            for k in range(4):
                S_.op("act", lambda e, k=k, d4=d4, j=j: e.activation(out=d4[:, k, :], in_=identf, func=AF.Identity,
                                                                    scale=col(PP_WLC, j * 4 + k)),
                      reads=[b("identf"), b("pp")], writes=[b("d4", j % 2, k)])

        def emit_exp_min(jq, ci):
            c0, n = CH[ci]
            S_.op("act", lambda e: e.activation(
                out=A[:, c0:c0 + n], in_=R[:, c0:c0 + n], func=AF.Exp, scale=cl[:, 0, jq:jq + 1]),
                reads=[b("R", ci), b("cl")], writes=[b("A", ci)])
            S_.op("act", lambda e: e.activation(
                out=R[:, c0:c0 + n], in_=R[:, c0:c0 + n], func=AF.Exp, scale=cl[:, 1, jq:jq + 1]),
                reads=[b("R", ci), b("cl")], writes=[b("R", ci)])
            S_.op("dve", lambda e: e.tensor_scalar(
                out=R[:, c0:c0 + n], in0=R[:, c0:c0 + n], scalar1=1.0, scalar2=-1.0, op0=ALU.min, op1=ALU.mult),
                reads=[b("R", ci)], writes=[b("R", ci)])

        def emit_sqrt(jq):
            for ci, (c0, n) in enumerate(CH):
                S_.op("act", lambda e, n=n, c0=c0: e.activation(
                    out=R[:, c0:c0 + n], in_=R[:, c0:c0 + n], func=AF.Sqrt, bias=one_ap),
                    reads=[b("R", ci), b("cst")], writes=[b("R", ci)])

        def g_gl(jq):
            return 1 if (jq // 2) % 2 == 0 else 3

        def emit_gl(jq):
            ws_q, wsb_q = lru_ws[jq]
            for ci, (c0, n) in enumerate(CH):
                pg, pgb = bank()
                mm8(pg, pgb, n, ws_q, wsb_q, g_gl(jq), hT, hT_bufs("hT", c0, n), c0)
                S_.op("act", lambda e, pg=pg, n=n, c0=c0: e.activation(
                    out=Gb[:, c0:c0 + n], in_=pg[:, 0:n], func=AF.Gelu_apprx_tanh, bias=col(PP_BIN, 8 + jq)),
                    reads=[pgb, b("pp")], writes=[b("G", ci)])

        build_d4(0)
        lru_ws = {0: (ws, wsb)}
        for j in range(KT + 1):
            jp = j - 1
            if j < KT:
                ws, wsb = lru_ws[j]
                if j + 1 < KT:
                    ws_n, wsb_n = wsbuf()
                    wload(ws_n, wsb_n, 0, w_in_v[:, :, j + 1, :])
                    wload(ws_n, wsb_n, g_gl(j + 1), w_in_v[:, :, 8 + j + 1, :])
                    lru_ws[j + 1] = (ws_n, wsb_n)
                    build_d4(j + 1)
                X = XX[j % 2]
                d4 = D4[j % 2]
                S_.op("dve", lambda e, j=j: e.tensor_copy(out=xlb_s[:, :, 0:3], in_=lc[:, j, :, :]),
                      reads=[b("lc")], writes=[b("xlbs")])
                for ci, (c0, n) in enumerate(CH):
                    if jp >= 0:
                        emit_exp_min(jp, ci)
                    px, pxb = bank()
                    mm8(px, pxb, n, ws, wsb, 0, hT, hT_bufs("hT", c0, n), c0)
                    bxl = col(PP_BIN, j)
                    if ci < 4:
                        S_.op("dve", lambda e, px=px, c0=c0, bxl=bxl: e.tensor_scalar(
                            out=xlb[:, 3 + c0:3 + c0 + 512], in0=px[:, 0:512], scalar1=bxl, scalar2=None, op0=ALU.add),
                            reads=[pxb, b("pp")], writes=[b("xlb", ci)])
                        if ci == 3:
                            S_.op("dve", lambda e, px=px, bxl=bxl, j=j: e.tensor_scalar(
                                out=stage[:, j, 0, 1:4], in0=px[:, 509:512], scalar1=bxl, scalar2=None, op0=ALU.add),
                                reads=[pxb, b("pp")], writes=[b("stage")])
                    else:
                        px3 = px[:, 0:256].rearrange("p (s n) -> p s n", s=NS)
                        S_.op("dve", lambda e, px3=px3, bxl=bxl: e.tensor_scalar(
                            out=xlb_s[:, :, 3:67], in0=px3, scalar1=bxl, scalar2=None, op0=ALU.add),
                            reads=[pxb, b("pp")], writes=[b("xlb", ci)])
                        S_.op("dve", lambda e, px3=px3, bxl=bxl, j=j: e.tensor_scalar(
                            out=stage[:, j, 1:5, 1:4], in0=px3[:, :, 61:64], scalar1=bxl, scalar2=None, op0=ALU.add),
                            reads=[pxb, b("pp")], writes=[b("stage")])
            if jp >= 0:
                if j == KT:
                    for ci in range(len(CH)):
                        emit_exp_min(jp, ci)
                emit_sqrt(jp)
                emit_gl(jp)
                Xp = XX[jp % 2]
                for ci, (c0, n) in enumerate(CH):
                    S_.op("dve", lambda e, n=n, c0=c0: e.tensor_tensor(
                        out=I[:, c0:c0 + n], in0=I[:, c0:c0 + n], in1=R[:, c0:c0 + n], op=ALU.mult),
                        reads=[b("I", ci), b("R", ci)], writes=[b("I", ci)])
                    S_.op(TAIL_ENG, lambda e, n=n, c0=c0, Xp=Xp: e.tensor_tensor(
                        out=Xp[:, c0:c0 + n], in0=Xp[:, c0:c0 + n], in1=I[:, c0:c0 + n], op=ALU.mult),
                        reads=[b("I", ci), b("X", jp % 2, ci)], writes=[b("X", jp % 2, ci)])
            if j < KT:
                for ci, (c0, n) in enumerate(CH):
                    pc, pcb = bank()
                    if ci < 4:
                        rb = [b("xlb", ci), b("xlbpad")] + ([b("xlb", ci - 1)] if ci > 0 else [])
                    else:
                        rb = [b("xlb", 4), b("xlbs")]
                    for k in range(4):
                        rhs = xlb[:, c0 + k:c0 + k + 512] if ci < 4 else xlb_s[:, :, k:k + 64]
                        pco_ = pc[:, 0:n] if ci < 4 else pc[:, 0:256].rearrange("p (s n) -> p s n", s=NS)
                        S_.op("pe", lambda e, pco_=pco_, d4=d4, k=k, rhs=rhs: e.matmul(
                            pco_, lhsT=d4[:, k, :], rhs=rhs, start=(k == 0), stop=(k == 3)),
                            reads=[b("d4", j % 2, k)] + rb, writes=[pcb])
                    S_.op("dve", lambda e, pc=pc, n=n, c0=c0, j=j: e.tensor_scalar(
                        out=xcb[:, c0:c0 + n], in0=pc[:, 0:n], scalar1=col(PP_BLC, j), scalar2=None, op0=ALU.add),
                        reads=[pcb, b("pp")], writes=[b("xcb", ci)])
                    S_.op("dve", lambda e, pc=pc, n=n, c0=c0, j=j, X=X: e.tensor_scalar(
                        out=X[:, c0:c0 + n], in0=pc[:, 0:n], scalar1=col(PP_BLC, j), scalar2=None, op0=ALU.add),
                        reads=[pcb, b("pp")], writes=[b("X", j % 2, ci)])
            if jp >= 0:
                Xp = XX[jp % 2]
                for ci in range(4):
                    c0 = ci * 512
                    init = 0.0 if ci == 0 else Xp[:, c0 - 1:c0]
                    S_.op("dve", lambda e, c0=c0, init=init, Xp=Xp: e.tensor_tensor_scan(
                        out=Xp[:, c0:c0 + 512], data0=A[:, c0:c0 + 512], data1=Xp[:, c0:c0 + 512], initial=init,
                        op0=ALU.mult, op1=ALU.add),
                        reads=[b("A", ci), b("X", jp % 2, ci)] + ([b("X", jp % 2, ci - 1)] if ci > 0 else []),
                        writes=[b("X", jp % 2, ci)])
                for s in range(NS):
                    c0 = TP + s * TS
                    S_.op("dve", lambda e, c0=c0, s=s, jp=jp, Xp=Xp: e.tensor_tensor_scan(
                        out=Xp[:, c0:c0 + TS], data0=A[:, c0:c0 + TS], data1=Xp[:, c0:c0 + TS], initial=h0[:, jp, s:s + 1],
                        op0=ALU.mult, op1=ALU.add),
                        reads=[b("A", 4), b("X", jp % 2, 4), b("h0")], writes=[b("X", jp % 2, 4)])
                S_.op("dve", lambda e, jp=jp, Xp=Xp: e.tensor_copy(out=stage[:, jp, 0, 0:1], in_=Xp[:, TP - 1:TP]),
                      reads=[b("X", jp % 2, 3)], writes=[b("stage")])
                X3 = Xp[:, TP:T].rearrange("p (s n) -> p s n", s=NS)
                S_.op("dve", lambda e, jp=jp, X3=X3: e.tensor_copy(out=stage[:, jp, 1:5, 0:1], in_=X3[:, :, TS - 1:TS]),
                      reads=[b("X", jp % 2, 4)], writes=[b("stage")])
                for ci, (c0, n) in enumerate(CH):
                    S_.op(TAIL_ENG, lambda e, n=n, c0=c0, jp=jp, Xp=Xp: e.tensor_tensor(
                        out=glru[:, jp, c0:c0 + n], in0=Gb[:, c0:c0 + n], in1=Xp[:, c0:c0 + n], op=ALU.mult),
                        reads=[b("G", ci), b("X", jp % 2, ci)], writes=[b("glru", jp, ci)])
            if j < KT:
                for ci, (c0, n) in enumerate(CH):
                    pr, prb = bank()
                    pi, pib = bank()
                    S_.op("pe", lambda e, pr=pr, n=n, c0=c0, j=j: e.matmul(pr[:, 0:n], lhsT=WG[:, j, 0, :], rhs=xcb[:, c0:c0 + n],
                                                                            start=True, stop=True),
                          reads=[b("wg", 0, 0), b("wg", 0, 1), b("xcb", ci)], writes=[prb])
                    S_.op("pe", lambda e, pi=pi, n=n, c0=c0, j=j: e.matmul(pi[:, 0:n], lhsT=WG[:, j, 1, :], rhs=xcb[:, c0:c0 + n],
                                                                            start=True, stop=True),
                          reads=[b("wg", 1, 0), b("wg", 1, 1), b("xcb", ci)], writes=[pib])
                    S_.op("act", lambda e, pr=pr, n=n, c0=c0, j=j: e.activation(
                        out=R[:, c0:c0 + n], in_=pr[:, 0:n], func=AF.Sigmoid, bias=col(PP_BR, j)),
                        reads=[prb, b("pp")], writes=[b("R", ci)])
                    S_.op("act", lambda e, pi=pi, n=n, c0=c0, j=j: e.activation(
                        out=I[:, c0:c0 + n], in_=pi[:, 0:n], func=AF.Sigmoid, bias=col(PP_BI, j)),
                        reads=[pib, b("pp")], writes=[b("I", ci)])
        finals.append(S_.dma("sp", lambda e: e.dma_start(out=st_d, in_=stage_flat), reads=[b("stage")]))

        def load_merge_w(e_):
            ws_, wsb_ = wsbuf()
            wload(ws_, wsb_, 0, w_in_v[:, :, 32 + e_, :])
            wload(ws_, wsb_, 1, w_in_v[:, :, 40 + e_, :])
            wload(ws_, wsb_, 2, w_lruo_v[:, :, e_, :])
            wload(ws_, wsb_, 3, w_ccmo_v[:, :, e_, :])
            return ws_, wsb_

        ws, wsb = load_merge_w(0)
        S_.retire("S")
        S_.retire("Z")
        S_.retire("Zx1")
        S_.retire("Zs")

        merged = V(Z, BF16, KT, T)
        SGL = [V(Z + 36864, F32, 512)] * 2
        SGC = [V(Z + 38912, F32, 512)] * 2
        M1 = [V(Z + 40960, F32, 512)] * 2
        M2 = [V(Z + 43008, F32, 512)] * 2
        WO = V(S, BF16, KT, D)
        S_.dma("pool", lambda e: e.dma_start(out=WO, in_=w_out_v), writes=[b("wo")])
        it = 0
        for e_ in range(KT):
            if e_ + 1 < KT:
                ws_n, wsb_n = load_merge_w(e_ + 1)
            for ci, (c0, n) in enumerate(CH):
                psl, pslb = bank()
                psc, pscb = bank()
                plo, plob = bank()
                pco, pcob = bank()
                mm8(psl, pslb, n, ws, wsb, 0, hT, hT_bufs("hT", c0, n), c0)
                mm8(psc, pscb, n, ws, wsb, 1, hT, hT_bufs("hT", c0, n), c0)
                mm8(plo, plob, n, ws, wsb, 2, glru, [b("glru", kt, ci) for kt in range(KT)], c0)
                mm8(pco, pcob, n, ws, wsb, 3, dn, [b("dn", kt, ci) for kt in range(KT)], c0)
                k2 = it % 2
                it += 1
                sgl, sglb = SGL[k2], b("sgl", 0)
                sgc, sgcb = SGC[k2], b("sgc", 0)
                m1, m1b = M1[k2], b("m1", 0)
                m2, m2b = M2[k2], b("m2", 0)
                S_.op("act", lambda e, psl=psl, n=n, sgl=sgl, e_=e_: e.activation(
                    out=sgl[:, 0:n], in_=psl[:, 0:n], func=AF.Sigmoid, bias=col(PP_BIN, 32 + e_)),
                    reads=[pslb, b("pp")], writes=[sglb])
                S_.op("act", lambda e, psc=psc, n=n, sgc=sgc, e_=e_: e.activation(
                    out=sgc[:, 0:n], in_=psc[:, 0:n], func=AF.Sigmoid, bias=col(PP_BIN, 40 + e_)),
                    reads=[pscb, b("pp")], writes=[sgcb])
                S_.op("dve", lambda e, plo=plo, n=n, sgl=sgl, m1=m1: e.tensor_tensor(
                    out=m1[:, 0:n], in0=plo[:, 0:n], in1=sgl[:, 0:n], op=ALU.mult),
                    reads=[plob, sglb], writes=[m1b])
                S_.op("dve", lambda e, pco=pco, n=n, sgc=sgc, m2=m2: e.tensor_tensor(
                    out=m2[:, 0:n], in0=pco[:, 0:n], in1=sgc[:, 0:n], op=ALU.mult),
                    reads=[pcob, sgcb], writes=[m2b])
                S_.op("dve", lambda e, n=n, m1=m1, m2=m2, e_=e_, c0=c0: e.tensor_tensor(
                    out=merged[:, e_, c0:c0 + n], in0=m1[:, 0:n], in1=m2[:, 0:n], op=ALU.add),
                    reads=[m1b, m2b], writes=[b("mg", e_, ci)])
            if e_ + 1 < KT:
                ws, wsb = ws_n, wsb_n
        S_.retire("DN")
        S_.retire("GLa")
        S_.retire("GLb")
        S_.retire("R1")

        X2D = [V(GL + 8192 + k * 4096, F32, D) for k in range(3)]
        XAD = [V(GL + 20480 + k * 4096, F32, D) for k in range(3)]
        XSD = [V(GL + 32768 + k * 2048, BF16, D) for k in range(2)]
        JK_D = V(Z + 36864, BF16, D)
        WD = V(DN, BF16, FT, D)
        h2T = V(R1, BF16, KT, T)
        S_.dma("sp", lambda e: e.dma_start(out=GBflat, in_=gffn_d), writes=[b("GB")])

        def load_ffn_w(f):
            ws_, wsb_ = wsbuf()
            wload(ws_, wsb_, 0, w_gate_v[:, :, f, :])
            wload(ws_, wsb_, 1, w_up_v[:, :, f, :])
            return ws_, wsb_

        ffn_w0 = load_ffn_w(0)
        WDQ = list(range(FT))

        def load_xD(i):
            xa = XAD[i % 3]
            S_.dma("sp", lambda e: e.dma_start(out=xa, in_=x_d[i * 128:(i + 1) * 128, :]), writes=[b("xaD", i % 3)])

        for i in range(3):
            load_xD(i)
        trD = {}
        for i in range(NT + 1):
            if i < NT:
                xa, xab = XAD[i % 3], b("xaD", i % 3)
                x2, x2b = X2D[i % 3], b("x2D", i % 3)
                ci = min(i // 4, 4)
                pos = []
                for hh in range(2):
                    po, pob = bank()
                    pos.append((po, pob))
                    for kt in range(KT):
                        S_.op("pe", lambda e, po=po, kt=kt, i=i, hh=hh: e.matmul(
                            po[:, 0:512], lhsT=merged[:, kt, i * 128:(i + 1) * 128], rhs=WO[:, kt, hh * 512:(hh + 1) * 512],
                            start=(kt == 0), stop=(kt == KT - 1)),
                            reads=[b("wo")] + [b("mg", kt, ci) for kt in range(KT)], writes=[pob])
            if i >= 1:
                i2 = i - 1
                x2p, x2pb = X2D[i2 % 3], b("x2D", i2 % 3)
                xs, xsb = XSD[i2 % 2], b("xsD", i2 % 2)
                norm_scale(i2, 1, x2p, [x2pb], xs, xsb)
                S_.dma("pool", lambda e, i2=i2, x2p=x2p: e.dma_start(out=x2s_d[i2 * 128:(i2 + 1) * 128, :], in_=x2p),
                       reads=[x2pb], writes=[b("x2s", i2)])
                trD[i2] = norm_transpose(i2, xs, xsb)
            if i < NT:
                for hh in range(2):
                    po, pob = pos[hh]
                    S_.op("dve", lambda e, po=po, hh=hh, xa=xa, x2=x2: e.tensor_tensor(
                        out=x2[:, hh * 512:(hh + 1) * 512], in0=po[:, 0:512], in1=xa[:, hh * 512:(hh + 1) * 512], op=ALU.add),
                        reads=[pob, xab], writes=[x2b])
                if i + 3 < NT:
                    load_xD(i + 3)
                norm_stage1([i], 1, lambda i_: (X2D[i_ % 3], [b("x2D", i_ % 3)]), JK_D, "jkD")
            if i >= 1:
                pv3, pbb = trD.pop(i - 1)
                norm_evac(i - 1, pv3, pbb, h2T, "h2T")
        S_.retire("Z")
        S_.retire("Zx1")
        S_.retire("Zs")
        S_.retire("S")
        S_.retire("GLb")

        FF = V(DN + 45056, BF16, FT, 1280)
        X2T = [V(DN + 101376 + k * 4096, F32, D) for k in range(2)]
        YT = [V(DN + 109568 + k * 4096, F32, D) for k in range(2)]
        assert DN + 109568 + 8192 <= WS0
        SGF = [V(S + k * 2048, F32, 512) for k in range(2)]
        JK_F = V(S + 4096, BF16, D)
        SD = V(S + 8192, F32, 32)
        it = 0
        for hi_, (chs, tiles, t0, tn) in enumerate(HALVES):
            ws, wsb = ffn_w0
            for f in range(FT):
                if f + 1 < FT:
                    ws_n, wsb_n = load_ffn_w(f + 1)
                elif hi_ == 0:
                    ffn_w0 = load_ffn_w(0)
                if hi_ == 0:
                    S_.dma("pool", lambda e, f=f: e.dma_start(out=WD[:, f:f + 1, :], in_=w_down_v[:, f:f + 1, :]),
                           writes=[b("wd", f)])
                for ci in chs:
                    c0, n = CH[ci]
                    pg, pgb = bank()
                    pu, pub = bank()
                    mm8(pg, pgb, n, ws, wsb, 0, h2T, hT_bufs("h2T", c0, n), c0)
                    mm8(pu, pub, n, ws, wsb, 1, h2T, hT_bufs("h2T", c0, n), c0)
                    sgf, sgfb = SGF[it % 2], b("sgf", it % 2)
                    it += 1
                    S_.op("act", lambda e, pg=pg, n=n, sgf=sgf: e.activation(out=sgf[:, 0:n], in_=pg[:, 0:n], func=AF.Silu),
                          reads=[pgb], writes=[sgfb])
                    S_.op("dve", lambda e, pu=pu, n=n, sgf=sgf, f=f, c0=c0, t0=t0: e.tensor_tensor(
                        out=FF[:, f, c0 - t0:c0 - t0 + n], in0=pu[:, 0:n], in1=sgf[:, 0:n], op=ALU.mult),
                        reads=[pub, sgfb], writes=[b("ff", f, ci - chs[0])])
                if f + 1 < FT:
                    ws, wsb = ws_n, wsb_n
            if hi_ == 0:
                S_.dma("sp", lambda e: e.dma_start(out=GBflat, in_=gfin_d), writes=[b("GB")])
            for i in tiles:
                ci = min(i // 4, 4)
                x2t, x2tb = X2T[i % 2], b("x2t", i % 2)
                yt, ytb = YT[i % 2], b("yt", i % 2)
                S_.dma("sp", lambda e, x2t=x2t, i=i: e.dma_start(out=x2t, in_=x2s_d[i * 128:(i + 1) * 128, :]),
                       reads=[b("x2s", i)], writes=[x2tb])
                lo = i * 128 - t0
                for hh in range(2):
                    po, pob = bank()
                    for f in range(FT):
                        S_.op("pe", lambda e, po=po, f=f, lo=lo, hh=hh: e.matmul(
                            po[:, 0:512], lhsT=FF[:, f, lo:lo + 128], rhs=WD[:, f, hh * 512:(hh + 1) * 512],
                            start=(f == 0), stop=(f == FT - 1)),
                            reads=[b("wd", WDQ[f]), b("ff", f, ci - chs[0])], writes=[pob])
                    S_.op("dve", lambda e, po=po, hh=hh, x2t=x2t: e.tensor_tensor(
                        out=x2t[:, hh * 512:(hh + 1) * 512], in0=po[:, 0:512], in1=x2t[:, hh * 512:(hh + 1) * 512], op=ALU.add),
                        reads=[pob, x2tb], writes=[x2tb])
                S_.op("act", lambda e, x2t=x2t, i=i: e.activation(out=JK_F, in_=x2t, func=AF.Square, accum_out=ss[:, 2, i:i + 1]),
                      reads=[x2tb, b("ss")], writes=[b("jkF"), b("ss3", i)])
                S_.op("act", lambda e, i=i: e.activation(out=SD[:, i:i + 1], in_=ss[:, 2, i:i + 1], func=AF.Sqrt,
                                                        bias=eps_ap, scale=1.0 / D),
                      reads=[b("ss3", i), b("cst")], writes=[b("sd", i)])
                S_.op("dve", lambda e, i=i: e.reciprocal(out=SD[:, i:i + 1], in_=SD[:, i:i + 1]),
                      reads=[b("sd", i)], writes=[b("sd", i)])
                S_.op("dve", lambda e, x2t=x2t, yt=yt, i=i: e.scalar_tensor_tensor(
                    out=yt, in0=x2t, scalar=SD[:, i:i + 1], in1=GBflat, op0=ALU.mult, op1=ALU.mult),
                    reads=[x2tb, b("sd", i), b("GB")], writes=[ytb])
                finals.append(S_.dma("pool", lambda e, yt=yt, i=i: e.dma_start(out=y_d[i * 128:(i + 1) * 128, :], in_=yt),
                                     reads=[ytb]))
        S_.emit(final_wait_ops=finals)
    return nc


_NC_CACHE = {}


def _colT(v):
    return np.ascontiguousarray(np.asarray(v, np.float32).reshape(-1, 128).T)


def kernel(x_prompt, x_sample, state_lru_h, cache_lru_conv, cache_ccm_conv,
           g_mix, w_in, b_in, w_lru_conv, b_lru_conv, w_rg_r, b_rg_r, w_rg_i, b_rg_i,
           lru_lambda, w_lru_o, w_ccm_dw, b_ccm_dw, g_ccm_ln, b_ccm_ln, w_ccm_o, w_out,
           g_ffn, w_ffn_gate, w_ffn_up, w_ffn_down, g_final):
    f = lambda a: np.ascontiguousarray(np.asarray(a, np.float32))
    n_cores = 8
    pp = np.zeros((128, NPP), np.float32)
    pp[:, PP_BIN:PP_BIN + 48] = _colT(b_in[0])
    pp[:, PP_WLC:PP_WLC + 32] = f(w_lru_conv[0]).reshape(4, 8, 128).transpose(2, 1, 0).reshape(128, 32)
    pp[:, PP_BLC:PP_BLC + 8] = _colT(b_lru_conv[0])
    pp[:, PP_BR:PP_BR + 8] = _colT(b_rg_r[0])
    pp[:, PP_BI:PP_BI + 8] = _colT(b_rg_i[0])
    pp[:, PP_LAM:PP_LAM + 8] = _colT(lru_lambda[0])
    pp[:, PP_WDW:PP_WDW + 248] = f(w_ccm_dw[0]).reshape(31, 8, 128).transpose(2, 1, 0).reshape(128, 248)
    pp[:, PP_BDW:PP_BDW + 8] = _colT(b_ccm_dw[0])
    pp[:, PP_GLN:PP_GLN + 8] = _colT(g_ccm_ln[0])
    pp[:, PP_BLN:PP_BLN + 8] = _colT(b_ccm_ln[0])

    def gB(g):
        return np.ascontiguousarray(np.repeat(_colT(g)[:, :, None], 128, axis=2).reshape(128, D))

    gmixB = gB(g_mix[0])
    gffnB = gB(g_ffn[0])
    gfinB = np.ascontiguousarray(np.broadcast_to(f(g_final)[None, :], (128, D)))
    shared = {
        "pp": pp, "gmixB": gmixB, "gffnB": gffnB, "gfinB": gfinB,
        "w_in": f(w_in[0]), "w_rg_r": f(w_rg_r[0]), "w_rg_i": f(w_rg_i[0]),
        "w_lru_o": f(w_lru_o[0]), "w_ccm_o": f(w_ccm_o[0]), "w_out": f(w_out[0]),
        "w_ffn_gate": f(w_ffn_gate[0]), "w_ffn_up": f(w_ffn_up[0]), "w_ffn_down": f(w_ffn_down[0]),
    }
    xp = f(x_prompt)
    xs = f(x_sample)
    in_maps = []
    for c in range(n_cores):
        sl = slice(NS * c, NS * c + NS)
        m = dict(shared)
        m["x"] = np.ascontiguousarray(np.concatenate([xp[c], xs[sl].reshape(NS * TS, D)], axis=0))
        m["h0"] = np.ascontiguousarray(f(state_lru_h[0, sl]).reshape(NS, 8, 128).transpose(2, 1, 0).reshape(128, KT * NS))
        m["lc"] = np.ascontiguousarray(f(cache_lru_conv[0, sl]).reshape(NS, 3, 8, 128).transpose(3, 2, 0, 1).reshape(128, KT * NS * 3))
        m["cc"] = np.ascontiguousarray(f(cache_ccm_conv[0, sl]).reshape(NS, 30, 8, 128).transpose(3, 2, 0, 1).reshape(128, KT * NS * 30))
        in_maps.append(m)
    if "nc" not in _NC_CACHE:
        _NC_CACHE["nc"] = build_nc()
    nc = _NC_CACHE["nc"]
    import os as _os2
    if _os2.environ.get("K_TRACE"):
        res = run_bass_kernel_spmd(nc, in_maps, core_ids=list(range(n_cores)), trace=True)
        print("EXEC_TIME_NS", res.exec_time_ns)
    else:
        res = run_bass_kernel_spmd(nc, in_maps, core_ids=list(range(n_cores)))
    y_prompt = np.zeros((8, TP, D), np.float32)
    y_sample = np.zeros((32, TS, D), np.float32)
    p_h = np.zeros((1, 8, D), np.float32)
    p_lb = np.zeros((1, 8, 3, D), np.float32)
    p_cb = np.zeros((1, 8, 30, D), np.float32)
    s_h = np.zeros((1, 32, D), np.float32)
    s_lb = np.zeros((1, 32, 3, D), np.float32)
    s_cb = np.zeros((1, 32, 30, D), np.float32)
    for c in range(n_cores):
        r = res.results[c]
        y = np.asarray(r["y"], np.float32)
        y_prompt[c] = y[:TP]
        y_sample[NS * c:NS * c + NS] = y[TP:].reshape(NS, TS, D)
        st5 = np.asarray(r["st"], np.float32).reshape(128, KT, 5, 34).transpose(2, 3, 1, 0).reshape(5, 34, D)
        p_h[0, c] = st5[0, 0]
        p_lb[0, c] = st5[0, 1:4]
        p_cb[0, c] = st5[0, 4:34]
        for s in range(NS):
            s_h[0, NS * c + s] = st5[1 + s, 0]
            s_lb[0, NS * c + s] = st5[1 + s, 1:4]
            s_cb[0, NS * c + s] = st5[1 + s, 4:34]
    return (y_prompt, y_sample, p_h, p_lb, p_cb, s_h, s_lb, s_cb)
```
